# Optimizing a Trainium2 kernel written in Bass

```python
import math
import jax, jax.numpy as jnp
from jax import lax
import numpy as np

D_MODEL = 1024
BATCH = 32
SEQ = 256
DEPTH = 4
DEC_BATCH = 2
DEC_SEQ = 2048
PAST_LEN = 512

GRID_W = 64
N_EV = (DEPTH + 1) // 2
N_OD = DEPTH // 2
D_FF = 4 * D_MODEL
ALPHA = (2.0 * DEPTH) ** 0.25
BETA = (8.0 * DEPTH) ** -0.25
LN_EPS = 1e-5
RMS_EPS = 1e-6
Q_BLOCK = 128
ROPE_THETA = 10000.0

HY_DIM = D_MODEL // 2
HY_ORDER = 2
HY_SHORT = 3
HY_BANDS = 16
HY_EMB = 2 * HY_BANDS + 1
HY_FILT_HID = 64
HY_DECAY_MIN = 3.0
HY_DECAY_MAX = 15.0

MLA_HEADS = 8
MLA_NOPE = 64
MLA_ROPE = 32
MLA_V = 64
MLA_Q_RANK = 256
MLA_KV_RANK = 128

CV_DIM = D_MODEL // 2
CV_WIDTH = 31

GQA_HEADS = 8
GQA_KV_HEADS = 2
GQA_HD = 64

EV_SPLITS = (3 * HY_DIM, 3 * HY_DIM + MLA_Q_RANK, 3 * HY_DIM + MLA_Q_RANK + MLA_KV_RANK)
EV_IN = EV_SPLITS[2] + MLA_ROPE
EV_OUT = HY_DIM + MLA_HEADS * MLA_V
OD_SPLITS = (2 * CV_DIM, 2 * CV_DIM + GQA_HEADS * GQA_HD, 2 * CV_DIM + (GQA_HEADS + GQA_KV_HEADS) * GQA_HD)
OD_IN = OD_SPLITS[2] + GQA_KV_HEADS * GQA_HD
OD_OUT = CV_DIM + GQA_HEADS * GQA_HD

kernel_name = 'hybrid_hyena_mla_conformer_gqa_dit_step'


def layer_norm(x, g, b):
    xf = x.astype(jnp.float32)
    mu = jnp.mean(xf, axis=-1, keepdims=True)
    var = jnp.mean(jnp.square(xf - mu), axis=-1, keepdims=True)
    return ((xf - mu) * lax.rsqrt(var + LN_EPS) * g + b).astype(x.dtype)


def rms_norm(x, g):
    xf = x.astype(jnp.float32)
    return (xf * lax.rsqrt(jnp.mean(xf * xf, axis=-1, keepdims=True) + RMS_EPS) * g).astype(x.dtype)


def depthwise_conv(x, w, b):
    k = w.shape[0]
    y = lax.conv_general_dilated(x, w[:, None, :].astype(x.dtype), window_strides=(1,),
                                 padding=[(k // 2, k // 2)],
                                 dimension_numbers=('NWC', 'WIO', 'NWC'),
                                 feature_group_count=x.shape[-1])
    return y + b


def grid_positions(n_tok):
    rows = n_tok // GRID_W
    row = jnp.repeat(jnp.arange(rows, dtype=jnp.float32), GRID_W)
    col = jnp.tile(jnp.arange(GRID_W, dtype=jnp.float32), rows)
    return row, col


def rope_axis(x, pos):
    r = x.shape[-1]
    inv = ROPE_THETA ** (-jnp.arange(0, r, 2, dtype=jnp.float32) / r)
    ang = pos[:, None] * inv[None, :]
    ang = jnp.concatenate([ang, ang], axis=-1)[None, :, None, :]
    xf = x.astype(jnp.float32)
    x1, x2 = jnp.split(xf, 2, axis=-1)
    rot = jnp.concatenate([-x2, x1], axis=-1)
    return (xf * jnp.cos(ang) + rot * jnp.sin(ang)).astype(x.dtype)


def rope_2d(x):
    row, col = grid_positions(x.shape[1])
    half = x.shape[-1] // 2
    return jnp.concatenate([rope_axis(x[..., :half], row), rope_axis(x[..., half:], col)], axis=-1)


def block_attention(q, k, v):
    b, lq, h, dk = q.shape
    kvh, dv = k.shape[2], v.shape[-1]
    g = h // kvh
    nblk = lq // Q_BLOCK
    scale = dk ** -0.5
    qb = q.reshape(b, nblk, Q_BLOCK, kvh, g, dk).transpose(1, 0, 2, 3, 4, 5)

    def one_block(qblk):
        s = jnp.einsum('bqhgd,bkhd->bhgqk', qblk, k, preferred_element_type=jnp.float32) * scale
        p = jax.nn.softmax(s, axis=-1)
        return jnp.einsum('bhgqk,bkhd->bqhgd', p.astype(v.dtype), v)

    o = lax.map(one_block, qb)
    return o.transpose(1, 0, 2, 3, 4, 5).reshape(b, lq, h * dv)


def hyena_filter_spectra(n_tok, w1, b1, w2, b2, w3, freq, decay):
    t = jnp.arange(n_tok, dtype=jnp.float32) / n_tok
    bands = jnp.arange(1, HY_BANDS + 1, dtype=jnp.float32)
    ang = 2.0 * math.pi * t[:, None] * bands[None, :]
    z = jnp.concatenate([t[:, None], jnp.cos(ang), jnp.sin(ang)], axis=-1)
    hid = jnp.sin(freq * (z @ w1 + b1))
    hid = jnp.sin(freq * (hid @ w2 + b2))
    h = ((hid @ w3) * jnp.exp(-t[:, None] * jnp.abs(decay))).astype(jnp.float32)
    h = h.reshape(n_tok, 2, HY_ORDER, HY_DIM)
    fwd, bwd = h[:, 0], h[:, 1]
    two_sided = jnp.concatenate([fwd, jnp.zeros_like(fwd[:1]), bwd[1:][::-1]], axis=0)
    return jnp.fft.rfft(two_sided, axis=0)


def fft_long_conv(z, spec):
    n = z.shape[1]
    zf = jnp.fft.rfft(z.astype(jnp.float32), n=2 * n, axis=1)
    return jnp.fft.irfft(zf * spec[None], n=2 * n, axis=1)[:, :n].astype(z.dtype)


def hyena(u, ev, i):
    n_tok = u.shape[1]
    u = depthwise_conv(u, ev['conv_w'][i], ev['conv_b'][i])
    x1, x2, v = jnp.split(u, 3, axis=-1)
    spec = hyena_filter_spectra(n_tok, ev['f_w1'][i], ev['f_b1'][i], ev['f_w2'][i], ev['f_b2'][i],
                                ev['f_w3'][i], ev['freq'][i], ev['decay'][i])
    z = v
    for n, gate in enumerate((x1, x2)):
        z = gate * (fft_long_conv(z, spec[:, n]) + ev['skip'][i, n] * z)
    return z


def mla_expand(ckv, k_rope, w_ukv):
    b, l, _ = ckv.shape
    kv = (ckv @ w_ukv).reshape(b, l, MLA_HEADS, MLA_NOPE + MLA_V)
    k_nope, v = kv[..., :MLA_NOPE], kv[..., MLA_NOPE:]
    k_r = jnp.broadcast_to(k_rope[:, :, None, :], (b, l, MLA_HEADS, MLA_ROPE))
    return jnp.concatenate([k_nope, k_r], axis=-1), v


def even_mixer(h, ev, i, ctx):
    u = h @ ev['w_in'][i]
    u_hy, cq, ckv, kr = jnp.split(u, EV_SPLITS, axis=-1)
    y_hy = hyena(u_hy, ev, i)
    b, l, _ = h.shape
    q = (rms_norm(cq, ev['q_norm_g'][i]) @ ev['w_uq'][i]).reshape(b, l, MLA_HEADS, MLA_NOPE + MLA_ROPE)
    ckv = rms_norm(ckv, ev['kv_norm_g'][i])
    if ctx is None:
        k, v = mla_expand(ckv, kr, ev['w_ukv'][i])
        state = (ckv, kr)
    else:
        q = jnp.concatenate([q[..., :MLA_NOPE], rope_2d(q[..., MLA_NOPE:])], axis=-1)
        kr_rot = rope_2d(kr[:, :, None, :])[:, :, 0, :]
        k_lat, v_lat = mla_expand(ckv, kr_rot, ev['w_ukv'][i])
        k_ctx, v_ctx = mla_expand(ctx[0], ctx[1], ev['w_ukv'][i])
        k = jnp.concatenate([k_ctx, k_lat], axis=1)
        v = jnp.concatenate([v_ctx, v_lat], axis=1)
        state = None
    y_mla = block_attention(q, k, v)
    y = jnp.concatenate([y_hy, y_mla], axis=-1) @ ev['w_out'][i]
    return y, state


def odd_mixer(h, od, i, ctx):
    u = h @ od['w_in'][i]
    u_cv, q, k, v = jnp.split(u, OD_SPLITS, axis=-1)
    a, gt = jnp.split(u_cv, 2, axis=-1)
    y_cv = depthwise_conv(a * jax.nn.sigmoid(gt), od['dw_w'][i], od['dw_b'][i])
    y_cv = jax.nn.silu(layer_norm(y_cv, od['cv_ln_g'][i], od['cv_ln_b'][i]))
    b, l, _ = h.shape
    q = rms_norm(q.reshape(b, l, GQA_HEADS, GQA_HD), od['q_norm_g'][i])
    k = rms_norm(k.reshape(b, l, GQA_KV_HEADS, GQA_HD), od['k_norm_g'][i])
    v = v.reshape(b, l, GQA_KV_HEADS, GQA_HD)
    if ctx is None:
        state = (k, v)
    else:
        q = rope_2d(q)
        k = jnp.concatenate([ctx[0], rope_2d(k)], axis=1)
        v = jnp.concatenate([ctx[1], v], axis=1)
        state = None
    y_at = block_attention(q, k, v)
    y = jnp.concatenate([y_cv, y_at], axis=-1) @ od['w_out'][i]
    return y, state


def trunk_layer(x, l, cond, ev, od, sh, ctx):
    mod = (jax.nn.silu(cond) @ sh['ada_w'][l] + sh['ada_b'][l])[:, None, :]
    sh1, sc1, g1, sh2, sc2, g2 = jnp.split(mod, 6, axis=-1)
    h = x * (1.0 + sc1) + sh1
    if l % 2 == 0:
        y, state = even_mixer(h, ev, l // 2, ctx)
    else:
        y, state = odd_mixer(h, od, l // 2, ctx)
    x = layer_norm(ALPHA * x + g1 * y, sh['ln_g'][l, 0], sh['ln_b'][l, 0])
    h = x * (1.0 + sc2) + sh2
    f = jnp.square(jax.nn.relu(h @ sh['w1'][l] + sh['b1'][l])) @ sh['w2'][l] + sh['b2'][l]
    x = layer_norm(ALPHA * x + g2 * f, sh['ln_g'][l, 1], sh['ln_b'][l, 1])
    return x, state


def setup_inputs(seed: int = 0) -> dict:
    key = jax.random.key(seed)
    keys = iter(jax.random.split(key, 64))

    def nrm(shape, scale):
        return jax.random.normal(next(keys), shape, jnp.float32) * scale

    def gain(shape):
        return 1.0 + nrm(shape, 0.1)

    d = D_MODEL
    return {
        'x_prompt': nrm((BATCH, SEQ, d), 1.0),
        'x_sample': nrm((DEC_BATCH, DEC_SEQ, d), 1.0),
        'cache_mla_ckv': nrm((DEC_BATCH, N_EV, PAST_LEN, MLA_KV_RANK), 1.0),
        'cache_mla_krope': nrm((DEC_BATCH, N_EV, PAST_LEN, MLA_ROPE), 1.0),
        'cache_gqa_k': nrm((DEC_BATCH, N_OD, PAST_LEN, GQA_KV_HEADS, GQA_HD), 1.0),
        'cache_gqa_v': nrm((DEC_BATCH, N_OD, PAST_LEN, GQA_KV_HEADS, GQA_HD), 1.0),
        'c': nrm((DEC_BATCH, d), 1.0),
        'c_ctx': nrm((d,), 1.0),
        'ev_w_in': nrm((N_EV, d, EV_IN), d ** -0.5),
        'hy_conv_w': nrm((N_EV, HY_SHORT, 3 * HY_DIM), HY_SHORT ** -0.5),
        'hy_conv_b': nrm((N_EV, 3 * HY_DIM), 0.02),
        'hy_filt_w1': nrm((N_EV, HY_EMB, HY_FILT_HID), HY_EMB ** -0.5),
        'hy_filt_b1': nrm((N_EV, HY_FILT_HID), 0.02),
        'hy_filt_w2': nrm((N_EV, HY_FILT_HID, HY_FILT_HID), HY_FILT_HID ** -0.5),
        'hy_filt_b2': nrm((N_EV, HY_FILT_HID), 0.02),
        'hy_filt_w3': nrm((N_EV, HY_FILT_HID, 2 * HY_ORDER * HY_DIM), 0.1 * HY_FILT_HID ** -0.5),
        'hy_sin_freq': gain((N_EV, HY_FILT_HID)),
        'hy_decay': jax.random.uniform(next(keys), (N_EV, 2 * HY_ORDER * HY_DIM), jnp.float32,
                                       HY_DECAY_MIN, HY_DECAY_MAX),
        'hy_skip': nrm((N_EV, HY_ORDER, HY_DIM), 1.0),
        'mla_q_norm_g': gain((N_EV, MLA_Q_RANK)),
        'mla_w_uq': nrm((N_EV, MLA_Q_RANK, MLA_HEADS * (MLA_NOPE + MLA_ROPE)), MLA_Q_RANK ** -0.5),
        'mla_kv_norm_g': gain((N_EV, MLA_KV_RANK)),
        'mla_w_ukv': nrm((N_EV, MLA_KV_RANK, MLA_HEADS * (MLA_NOPE + MLA_V)), MLA_KV_RANK ** -0.5),
        'ev_w_out': nrm((N_EV, EV_OUT, d), BETA * EV_OUT ** -0.5),
        'od_w_in': nrm((N_OD, d, OD_IN), d ** -0.5),
        'cv_dw_w': nrm((N_OD, CV_WIDTH, CV_DIM), CV_WIDTH ** -0.5),
        'cv_dw_b': nrm((N_OD, CV_DIM), 0.02),
        'cv_ln_g': gain((N_OD, CV_DIM)),
        'cv_ln_b': nrm((N_OD, CV_DIM), 0.02),
        'gqa_q_norm_g': gain((N_OD, GQA_HD)),
        'gqa_k_norm_g': gain((N_OD, GQA_HD)),
        'od_w_out': nrm((N_OD, OD_OUT, d), BETA * OD_OUT ** -0.5),
        'ada_w': nrm((DEPTH, d, 6 * d), 0.5 * d ** -0.5),
        'ada_b': nrm((DEPTH, 6 * d), 0.02),
        'ln_g': gain((DEPTH, 2, d)),
        'ln_b': nrm((DEPTH, 2, d), 0.02),
        'mlp_w1': nrm((DEPTH, d, D_FF), d ** -0.5),
        'mlp_b1': nrm((DEPTH, D_FF), 0.02),
        'mlp_w2': nrm((DEPTH, D_FF, d), BETA * D_FF ** -0.5),
        'mlp_b2': nrm((DEPTH, d), 0.02),
    }


def reference(x_prompt, x_sample, cache_mla_ckv, cache_mla_krope, cache_gqa_k, cache_gqa_v, c, c_ctx,
              ev_w_in, hy_conv_w, hy_conv_b, hy_filt_w1, hy_filt_b1, hy_filt_w2, hy_filt_b2, hy_filt_w3,
              hy_sin_freq, hy_decay, hy_skip, mla_q_norm_g, mla_w_uq, mla_kv_norm_g, mla_w_ukv, ev_w_out,
              od_w_in, cv_dw_w, cv_dw_b, cv_ln_g, cv_ln_b, gqa_q_norm_g, gqa_k_norm_g, od_w_out,
              ada_w, ada_b, ln_g, ln_b, mlp_w1, mlp_b1, mlp_w2, mlp_b2):
    ev = {'w_in': ev_w_in, 'conv_w': hy_conv_w, 'conv_b': hy_conv_b, 'f_w1': hy_filt_w1, 'f_b1': hy_filt_b1,
          'f_w2': hy_filt_w2, 'f_b2': hy_filt_b2, 'f_w3': hy_filt_w3, 'freq': hy_sin_freq, 'decay': hy_decay,
          'skip': hy_skip, 'q_norm_g': mla_q_norm_g, 'w_uq': mla_w_uq, 'kv_norm_g': mla_kv_norm_g,
          'w_ukv': mla_w_ukv, 'w_out': ev_w_out}
    od = {'w_in': od_w_in, 'dw_w': cv_dw_w, 'dw_b': cv_dw_b, 'cv_ln_g': cv_ln_g, 'cv_ln_b': cv_ln_b,
          'q_norm_g': gqa_q_norm_g, 'k_norm_g': gqa_k_norm_g, 'w_out': od_w_out}
    sh = {'ada_w': ada_w, 'ada_b': ada_b, 'ln_g': ln_g, 'ln_b': ln_b,
          'w1': mlp_w1, 'b1': mlp_b1, 'w2': mlp_w2, 'b2': mlp_b2}

    xp = x_prompt
    ckv_list, krope_list, k_list, v_list = [], [], [], []
    for l in range(DEPTH):
        xp, st = trunk_layer(xp, l, c_ctx[None, :], ev, od, sh, None)
        if l % 2 == 0:
            ckv_list.append(st[0])
            krope_list.append(st[1])
        else:
            k_list.append(st[0])
            v_list.append(st[1])
    y_prompt = xp
    new_mla_ckv = jnp.stack(ckv_list, axis=1)
    new_mla_krope = jnp.stack(krope_list, axis=1)
    new_gqa_k = jnp.stack(k_list, axis=1)
    new_gqa_v = jnp.stack(v_list, axis=1)

    xs = x_sample
    for l in range(DEPTH):
        i = l // 2
        if l % 2 == 0:
            ctx = (cache_mla_ckv[:, i], cache_mla_krope[:, i])
        else:
            ctx = (cache_gqa_k[:, i], cache_gqa_v[:, i])
        xs, _ = trunk_layer(xs, l, c, ev, od, sh, ctx)
    y_sample = xs

    return (y_prompt, y_sample, new_mla_ckv, new_mla_krope, new_gqa_k, new_gqa_v)
```

```python
import math, bisect
from contextlib import ExitStack
from concourse.bass_utils import run_bass_kernel_spmd
import bisect
import numpy as np
import concourse.bass as bass
import concourse.mybir as mybir

F32 = mybir.dt.float32
BF16 = mybir.dt.bfloat16
AF = mybir.ActivationFunctionType
ALU = mybir.AluOpType


class Op:
    __slots__ = ("eng", "idx", "fn", "deps", "kind", "signal", "sem", "val")

    def __init__(self, eng, idx, fn, kind):
        self.eng = eng
        self.idx = idx
        self.fn = fn
        self.kind = kind
        self.deps = []
        self.signal = False
        self.sem = None
        self.val = 0


class Slot:
    def __init__(self, t, nfree, name, whole=False):
        self.t = t
        self.nfree = nfree
        self.name = name
        self.whole = whole
        self.los = [0]
        self.segs = [[0, nfree, None, {}]]

    def __getitem__(self, idx):
        return View(self, self.t[idx])

    def ap(self):
        return View(self, self.t.ap() if hasattr(self.t, "ap") else self.t[:])

    def touch(self, lo, hi, op, write, deps):
        assert 0 <= lo < hi <= self.nfree, (self.name, lo, hi, self.nfree)
        segs, los = self.segs, self.los
        i = bisect.bisect_right(los, lo) - 1
        if i < 0:
            i = 0
        out = []
        j = i
        n = len(segs)
        while j < n and segs[j][0] < hi:
            slo, shi, w, rd = segs[j]
            if shi <= lo:
                out.append(segs[j])
                j += 1
                continue
            if slo < lo:
                out.append([slo, lo, w, dict(rd)])
            a = max(slo, lo)
            b = min(shi, hi)
            if w is not None and w is not op:
                deps.add(w)
            if write:
                for r in rd.values():
                    if r is not op:
                        deps.add(r)
                if out and out[-1][2] is op and out[-1][1] == a and not out[-1][3]:
                    out[-1][1] = b
                else:
                    out.append([a, b, op, {}])
            else:
                rd2 = dict(rd)
                key = op.eng if op.kind == "c" else ("d", op.idx)
                rd2[key] = op
                out.append([a, b, w, rd2])
            if shi > hi:
                out.append([hi, shi, w, dict(rd)])
            j += 1
        segs[i:j] = out
        self.los[i:j] = [s[0] for s in out]


class Reg:
    def __init__(self, t, base, slot):
        self.t = t
        self.base = base
        self.slot = slot

    def __getitem__(self, idx):
        return View(self, self.t[idx])


class View:
    __slots__ = ("slot", "ap", "base")

    def __init__(self, slot, ap, base=0):
        if isinstance(slot, Reg):
            base = slot.base
            slot = slot.slot
        self.slot = slot
        self.ap = ap
        self.base = base

    def ranges(self):
        s = self.slot
        if s.whole:
            return [(0, s.nfree)]
        ap = self.ap
        es = 2 if ap.dtype == BF16 else 4
        dims = list(ap.ap)
        pstep = dims[0][0]
        off = ((ap.offset % pstep) * es if pstep > 0 else 0) + self.base
        fd = [(st * es, c) for st, c in dims[1:] if c > 1]
        if not fd:
            return [(off, off + es)]
        fd.sort(key=lambda x: -x[0])
        span = sum(st * (c - 1) for st, c in fd) + es
        if len(fd) >= 2:
            s0, c0 = fd[0]
            inner = sum(st * (c - 1) for st, c in fd[1:]) + es
            if inner < s0 and c0 <= 64:
                return [(off + i * s0, off + i * s0 + inner) for i in range(c0)]
        return [(off, off + span)]


ENG_NAMES = ["pe", "act", "dve", "pool", "sp"]


class Prog:
    def __init__(self, nc, es):
        self.nc = nc
        self.es = es
        self.ops = []
        self.streams = {e: [] for e in ENG_NAMES}
        self.nslots = 0

    SB_BASE = 16384
    SB_TOP = 229344

    def sbuf_at(self, name, shape, dt, offset):
        if not hasattr(self, "SB"):
            self.SB = Slot(None, 1 << 20, "SB")
            self.nreg = 0
        self.nreg += 1
        nbytes = int(np.prod(shape[1:])) * (2 if dt == BF16 else 4)
        assert offset >= self.SB_BASE and offset + nbytes <= self.SB_TOP, (name, offset, nbytes)
        t = self.nc.alloc_sbuf_tensor_at(f"{name}_{self.nreg}", list(shape), dt, offset=offset)
        return Reg(t, offset, self.SB)

    def sbuf(self, name, shape, dt):
        if not hasattr(self, "bump"):
            self.bump = self.SB_BASE
        nbytes = int(np.prod(shape[1:])) * (2 if dt == BF16 else 4)
        nbytes = (nbytes + 63) // 64 * 64
        off = self.bump
        self.bump += nbytes
        return self.sbuf_at(name, shape, dt, off)

    def psum(self, name, shape, dt=F32):
        t = self.es.enter_context(self.nc.psum_tensor(name, list(shape), dt))
        nbytes = int(np.prod(shape[1:])) * 4
        return Slot(t, nbytes, name)

    def dram(self, name, shape, dt, kind):
        t = self.nc.dram_tensor(name, list(shape), dt, kind=kind)
        return Slot(t, 1, name, whole=True)

    def op(self, eng, fn, reads, writes, kind="c"):
        o = Op(eng, len(self.ops), fn, kind)
        deps = set()
        for v in reads:
            for lo, hi in v.ranges():
                v.slot.touch(lo, hi, o, False, deps)
        for v in writes:
            for lo, hi in v.ranges():
                v.slot.touch(lo, hi, o, True, deps)
        dl = []
        for d in deps:
            if d.kind == "c" and d.eng == eng and eng == "pe":
                continue
            d.signal = True
            dl.append(d)
        dl.sort(key=lambda d: d.idx)
        o.deps = dl
        if kind == "d":
            o.signal = True
        self.ops.append(o)
        self.streams[eng].append(o)
        return o

    def mm(self, out, lhsT, rhs, start=True, stop=True, **kw):
        return self.op("pe", lambda e: e.matmul(out.ap, lhsT.ap, rhs.ap, start=start, stop=stop, **kw),
                       [lhsT, rhs], [out])

    def transpose(self, out, in_, ident):
        return self.op("pe", lambda e: e.transpose(out.ap, in_.ap, ident.ap), [in_, ident], [out])

    def act(self, out, in_, func, bias=None, scale=None, accum_out=None):
        reads = [in_]
        kw = {}
        if bias is not None:
            if isinstance(bias, View):
                reads.append(bias)
                kw["bias"] = bias.ap
            else:
                kw["bias"] = bias
        if scale is not None:
            if isinstance(scale, View):
                reads.append(scale)
                kw["scale"] = scale.ap
            else:
                kw["scale"] = scale
        writes = [out]
        if accum_out is not None:
            writes.append(accum_out)
            kw["accum_out"] = accum_out.ap
        return self.op("act", lambda e: e.activation(out.ap, in_.ap, func, **kw), reads, writes)

    def tt(self, eng, out, in0, in1, op):
        return self.op(eng, lambda e: e.tensor_tensor(out.ap, in0.ap, in1.ap, op), [in0, in1], [out])

    def ts(self, eng, out, in0, s1, op0, s2=None, op1=None):
        reads = [in0]
        a1 = s1
        if isinstance(s1, View):
            reads.append(s1)
            a1 = s1.ap
        a2 = s2
        if isinstance(s2, View):
            reads.append(s2)
            a2 = s2.ap
        if op1 is None:
            return self.op(eng, lambda e: e.tensor_scalar(out.ap, in0.ap, a1, None, op0), reads, [out])
        return self.op(eng, lambda e: e.tensor_scalar(out.ap, in0.ap, a1, a2, op0, op1), reads, [out])

    def stt(self, eng, out, in0, scalar, in1, op0, op1):
        reads = [in0, in1]
        a = scalar
        if isinstance(scalar, View):
            reads.append(scalar)
            a = scalar.ap
        return self.op(eng, lambda e: e.scalar_tensor_tensor(out.ap, in0.ap, a, in1.ap, op0, op1), reads, [out])

    def copy(self, eng, out, in_):
        if eng == "act":
            return self.act(out, in_, AF.Copy)
        return self.op(eng, lambda e: e.tensor_copy(out.ap, in_.ap), [in_], [out])

    def memset(self, eng, out, val):
        return self.op(eng, lambda e: e.memset(out.ap, val), [], [out])

    def recip(self, out, in_):
        return self.op("dve", lambda e: e.reciprocal(out.ap, in_.ap), [in_], [out])

    def dma(self, q, out, in_, **kw):
        return self.op(q, lambda e: e.dma_start(out=out.ap, in_=in_.ap, **kw), [in_], [out], kind="d")

    def emit(self):
        nc = self.nc
        es = self.es
        ROT = 30000
        NPOOL = 24
        sem_list = {}

        def new_sem(name):
            return es.enter_context(nc.semaphore(name))

        for e in ENG_NAMES:
            cnt = 0
            dcnt = 0
            cur = None
            pool = []
            for o in self.streams[e]:
                if o.kind == "c":
                    if not o.signal:
                        continue
                    if cur is None or cnt >= ROT:
                        cur = new_sem(f"s_{e}_{len(sem_list)}")
                        sem_list[id(cur)] = cur
                        cnt = 0
                    cnt += 1
                    o.sem = cur
                    o.val = cnt
                else:
                    k = dcnt % NPOOL
                    if k >= len(pool):
                        pool.append(new_sem(f"d_{e}_{k}"))
                    o.sem = pool[k]
                    o.val = 16 * (dcnt // NPOOL + 1)
                    dcnt += 1
        all_dma = [o for o in self.ops if o.kind == "d"]
        last_dma = {}
        for o in all_dma:
            last_dma[id(o.sem)] = o
        block = es.enter_context(nc.Block())

        def run_stream(e, h, final=False):
            seen = {}
            nwait = 0
            for o in self.streams[e]:
                needs = []
                for d in o.deps:
                    needs.append((d.sem, d.val))
                if o.kind == "d" and o.val > 16:
                    needs.append((o.sem, o.val - 16))
                for s, v in needs:
                    if seen.get(id(s), 0) >= v:
                        continue
                    h.wait_ge(s, v)
                    seen[id(s)] = v
                    nwait += 1
                ins = o.fn(h)
                if o.signal:
                    ins.then_inc(o.sem, 16 if o.kind == "d" else 1)
            if final:
                for o in last_dma.values():
                    if seen.get(id(o.sem), 0) < o.val:
                        h.wait_ge(o.sem, o.val)
            return nwait

        @block.tensor
        def _(h):
            run_stream("pe", h)

        @block.scalar
        def _(h):
            run_stream("act", h)

        @block.vector
        def _(h):
            run_stream("dve", h)

        @block.gpsimd
        def _(h):
            run_stream("pool", h)

        @block.sync
        def _(h):
            run_stream("sp", h, final=True)
NLAYERS = 4

T = 2048
DM = 1024
NT = 16
NB = 4
ALPHA = 8.0 ** 0.25
LN_EPS = 1e-5
RMS_EPS = 1e-6
NKEY = 2560
BIGC = 76800


class VecPack:
    def __init__(self):
        self.cols = []
        self.off = {}
        self.n = 0

    def add(self, name, a):
        a = np.asarray(a, np.float32)
        if a.shape[0] != 128:
            b = np.zeros((128,) + a.shape[1:], np.float32)
            b[: a.shape[0]] = a
            a = b
        a = a.reshape(128, -1)
        self.off[name] = (self.n, a.shape[1])
        self.cols.append(a)
        self.n += a.shape[1]

    def array(self):
        return np.ascontiguousarray(np.concatenate(self.cols, axis=1))


def chunked(v):
    v = np.asarray(v, np.float32)
    return np.ascontiguousarray(v.reshape(-1, 128).T)


def rope_perm(r):
    h = r // 2
    idx = np.concatenate([np.arange(h, r), np.arange(0, h)])
    sgn = np.concatenate([-np.ones(h), np.ones(h)]).astype(np.float32)
    return idx, sgn


def rope_tables(r_axis, sample):
    t = np.arange(T)
    row = (t // 64).astype(np.float64)
    col = (t % 64).astype(np.float64)
    inv = 10000.0 ** (-np.arange(0, r_axis, 2, dtype=np.float64) / r_axis)
    cs, sn = [], []
    for pos in (row, col):
        ang = pos[None, :] * inv[:, None]
        ang = np.concatenate([ang, ang], axis=0)
        _, sgn = rope_perm(r_axis)
        cs.append(np.cos(ang))
        sn.append(np.sin(ang) * sgn[:, None])
    c = np.concatenate(cs, 0)
    s = np.concatenate(sn, 0)
    if not sample:
        c = np.ones_like(c)
        s = np.zeros_like(s)
    return c.astype(np.float32), s.astype(np.float32)


def perm2d(r_axis):
    i1, _ = rope_perm(r_axis)
    return np.concatenate([i1, i1 + r_axis])


def make_consts(sample):
    import ml_dtypes
    L = 2048 if sample else 256
    t = np.arange(T)
    m = t % L
    seg = t // L
    c = {}
    fidx = np.arange(T)
    f = fidx % L
    fseg = fidx // L
    ang = np.pi * np.outer(m, f).astype(np.float64) / L
    same = (seg[:, None] == fseg[None, :])
    Cf = np.where(same, np.cos(ang), 0.0)
    Sf = np.where(same, np.sin(ang), 0.0)
    bf = ml_dtypes.bfloat16
    c["cf"] = np.ascontiguousarray(Cf.astype(np.float32).astype(bf))
    c["sf"] = np.ascontiguousarray(Sf.astype(np.float32).astype(bf))
    def tile_f(M):
        return np.ascontiguousarray(M.reshape(16, 128, 16, 128).transpose(2, 1, 0, 3).reshape(16, 128, 2048).astype(np.float32).astype(bf))
    c["cft"] = tile_f(Cf)
    c["sft"] = tile_f(Sf)
    FN = np.zeros((T, 8))
    FN[t, seg] = (-1.0) ** m
    c["fnt"] = np.ascontiguousarray(FN.reshape(16, 128, 8).transpose(1, 0, 2).astype(np.float32).astype(bf))
    c["gn"] = np.ascontiguousarray(FN.T.astype(np.float32).astype(bf))
    wf = np.where(f == 0, 1.0 / (2 * L), 1.0 / L)
    tpos = m / float(L)
    bands = np.arange(1, 17)
    a2 = 2 * np.pi * tpos[:, None] * bands[None, :]
    z = np.concatenate([tpos[:, None], np.cos(a2), np.sin(a2)], axis=1)
    c["zT"] = np.ascontiguousarray(z.T.astype(np.float32))
    mA = np.zeros((8, T), np.float32)
    mB = np.zeros((8, NKEY), np.float32)
    sg8 = t // 256
    mA[sg8, t] = 1.0
    if not sample:
        BIG = 30000.0
        mB[:, :512] = -BIG
        for s in range(8):
            mB[s, 512:] = np.where(sg8 == s, 0.0, -BIG)
    c["maskA"] = mA.astype(bf)
    c["maskB"] = mB.astype(bf)
    vec = {}
    vec["wf"] = chunked(wf)
    vec["negt"] = chunked(-tpos)
    vec["m0"] = chunked((m != 0).astype(np.float32))
    vec["wN"] = np.full((128, 1), 1.0 / (2 * L), np.float32)
    vec["flag"] = np.full((128, 1), 1.0 if sample else 0.0, np.float32)
    cq, sq = rope_tables(16, sample)
    cg, sg = rope_tables(32, sample)
    rc_e = np.ones((128, T), np.float32)
    rs_e = np.zeros((128, T), np.float32)
    rc_e[0:32] = cq
    rs_e[0:32] = sq
    rc_e[64:96] = cq
    rs_e[64:96] = sq
    rc_o = np.ones((128, T), np.float32)
    rs_o = np.zeros((128, T), np.float32)
    rc_o[0:64] = cg
    rs_o[0:64] = sg
    rc_o[64:128] = cg
    rs_o[64:128] = sg
    c["rope_c"] = np.stack([rc_e, rc_o]).astype(bf)
    c["rope_s"] = np.stack([rs_e, rs_o]).astype(bf)
    return c, vec

STOP = (3, 2)
DBGF = ()
EST = 9
OST = 9


def build(voff, NV, dbg=False):
    nc = bass.Bass("TRN2", target_bir_lowering=False)
    es = ExitStack()
    P = Prog(nc, es)
    D = {}

    def din(name, shape, dt=F32):
        D[name] = P.dram(name, shape, dt, "ExternalInput")
        return D[name]

    def dview(name, fn=None):
        s = D[name]
        a = s.t.ap()
        if fn is not None:
            a = fn(a)
        return View(s, a)

    def dbg(name, v, shape, dt):
        if "dbg" not in DBGF:
            return
        dr = P.dram("dbg_" + name, list(shape), dt, "ExternalOutput")
        P.dma("sp", View(dr, dr.t.ap()), v)

    din("x_in", [T, DM])
    din("vecs", [128, NV])
    din("ident", [128, 128])
    din("cft", [16, 128, 2048], BF16)
    din("sft", [16, 128, 2048], BF16)
    din("cf", [T, T], BF16)
    din("sf", [T, T], BF16)
    din("fnt", [128, 16, 8], BF16)
    din("gn", [8, T], BF16)
    din("zT", [33, T])
    din("maskA", [8, T], BF16)
    din("maskB", [8, NKEY], BF16)
    din("rope_c", [2, 128, T], BF16)
    din("rope_s", [2, 128, T], BF16)
    din("c_ckv", [2, 512, 128])
    din("c_kr", [2, 512, 32])
    din("c_gk", [2, 512, 128])
    din("c_gv", [2, 512, 128])
    din("ada_w", [4, DM, 6144])
    din("ev_w_in", [2, DM, 1984])
    din("ev_w_out", [2, DM, DM])
    din("w_uq", [2, 256, 8 * 192])
    din("w_ukv", [2, 128, 1024])
    din("f_w1", [2, 33, 64])
    din("f_w2", [2, 64, 64])
    din("f_w3", [2, 64, 2048])
    din("decay_bc", [2, 128, 2048])
    din("od_w_in", [2, DM, 2432])
    din("od_w_out", [2, DM, DM])
    din("mlp_w1", [4, DM, 4096])
    din("mlp_w2", [4, 4096, DM])
    y_out = P.dram("y_out", [T, DM], F32, "ExternalOutput")
    o_ckv = P.dram("o_ckv", [2, T, 128], F32, "ExternalOutput")
    o_kr = P.dram("o_kr", [2, T, 32], F32, "ExternalOutput")
    o_gk = P.dram("o_gk", [2, T, 128], F32, "ExternalOutput")
    o_gv = P.dram("o_gv", [2, T, 128], F32, "ExternalOutput")
    xscr = P.dram("xscr", [128, 8 * T], F32, "Internal")

    BIG = P.sbuf("BIG", [128, BIGC], BF16)
    VEC = P.sbuf("VEC", [128, NV], F32)
    S32 = P.sbuf("S32", [128, 3584], F32)
    WR = P.sbuf("WR", [128, 3, 4096], BF16)
    ROPEC = P.sbuf("ROPEC", [128, T], BF16)
    ROPES = P.sbuf("ROPES", [128, T], BF16)
    CST = P.sbuf("CST", [128, 512], F32)
    CSTB = P.sbuf("CSTB", [128, 512], BF16)
    MODT = P.sbuf("MODT", [128, 128], F32)
    SMALL = P.sbuf("SMALL", [128, 1200], BF16)
    PS = [P.psum(f"ps{i}", [128, 512]) for i in range(8)]
    st = {"pb": 0, "wr": 0}

    def nb():
        b = PS[st["pb"] % st.get("nbm", 8)]
        st["pb"] += 1
        return b

    def bg(off, n, rows=128, r0=0):
        return View(BIG, BIG.t[r0:r0 + rows, off:off + n])

    def bgf(off, n, rows=128, r0=0):
        assert off + 2 * n <= BIGC
        rg = P.sbuf_at("bgf", [128, n], F32, BIG.base + off * 2)
        return View(rg, rg.t[r0:r0 + rows, :])

    def s32(off, n, rows=128, r0=0):
        return View(S32, S32.t[r0:r0 + rows, off:off + n])

    def vec(name, c0=0, n=1, rows=128, r0=0):
        o, w = voff[name]
        return View(VEC, VEC.t[r0:r0 + rows, o + c0:o + c0 + n])

    def R(v, pat, **kw):
        return View(v.slot, v.ap.rearrange(pat, **kw), v.base)

    def sub(v, idx):
        return View(v.slot, v.ap[idx], v.base)

    def wtile(src_view):
        k = st["wr"] % 3
        st["wr"] += 1
        return k

    XT = bgf(0, 8 * T)
    XT3 = R(XT, "p (c t) -> p c t", c=8)

    P.dma("sp", View(VEC, VEC.t[:, :]), dview("vecs"))
    IDF = View(CST, CST.t[:, 0:128])
    P.dma("sp", IDF, dview("ident"))
    IDB = View(CSTB, CSTB.t[:, 0:128])
    P.copy("act", IDB, IDF)
    ONESB = View(CSTB, CSTB.t[:, 128:256])
    P.memset("dve", ONESB, 1.0 / 1024)
    ONES64 = View(CST, CST.t[0:64, 128:192])
    P.memset("dve", ONES64, 1.0)
    ONES128 = View(CST, CST.t[:, 256:384])
    P.memset("dve", ONES128, 1.0)
    E65 = View(CST, CST.t[0:65, 192:256])
    P.memset("dve", View(CST, CST.t[:, 384:512]), 0.0)
    P.memset("dve", View(CST, CST.t[0:64, 384:448]), 1.0)
    P.memset("dve", View(CST, CST.t[64:128, 448:512]), 1.0)
    P.memset("dve", View(CST, CST.t[0:64, 192:256]), 0.0)
    P.memset("dve", View(CST, CST.t[64:65, 192:256]), 1.0)
    FNT = View(SMALL, SMALL.t[:, 0:128])
    if "nofnt" not in DBGF:
        P.dma("sp", FNT, dview("fnt", lambda a: a.rearrange("p k s -> p (k s)")))
    FNT3 = R(FNT, "p (k s) -> p k s", k=16)
    GN = View(SMALL, SMALL.t[0:8, 128:128 + 512])
    SBF = View(SMALL, SMALL.t[:, 640:648])
    if "nosilu" not in DBGF:
        P.act(SBF, vec("cond", 0, 8), AF.Silu)
    YN = View(SMALL, SMALL.t[0:8, 656:656 + 512])
    CTMP = s32(0, 8 * 258)
    CT3 = R(CTMP, "p (s j) -> p s j", s=8)
    if "noctmp" not in DBGF:
        P.memset("pool", CTMP, 0.0)

    for tt in range(0 if 'noload' in DBGF else NT):
        stg = s32(2100, 1024) if tt % 2 else s32(0, 1024)
        P.dma("sp", stg, dview("x_in", lambda a: a[tt * 128:(tt + 1) * 128, :]))
        for half in range(2):
            pb = nb()
            for j in range(4):
                c = half * 4 + j
                P.transpose(pb[:, j * 128:(j + 1) * 128], sub(stg, (slice(None), slice(c * 128, (c + 1) * 128))), IDF)
            P.copy("dve" if half else "act",
                   sub(XT3, (slice(None), slice(half * 4, half * 4 + 4), slice(tt * 128, (tt + 1) * 128))),
                   R(pb[:, :], "p (c t) -> p c t", c=4))

    def load_w(dname, li, r0, nrow_chunks, c0, ncols, rows=128):
        k = st["wr"] % 3
        st["wr"] += 1
        dst = View(WR, WR.t[0:rows, k, 0:nrow_chunks * ncols].rearrange("p (k n) -> p k n", k=nrow_chunks))
        src = dview(dname, lambda a: a[li, r0:r0 + nrow_chunks * rows, c0:c0 + ncols].rearrange("(k p) n -> p k n", p=rows))
        P.dma("pool", dst, src)
        return dst

    MOD = View(MODT, MODT.t[:, 0:48])
    SCP = View(MODT, MODT.t[:, 48:64])
    FB = View(MODT, MODT.t[0:64, 64:68])

    MODN = View(MODT, MODT.t[:, 68:116])

    def adaln_tiles(l):
        pb = PS[7]
        for g in range(12):
            wt = load_w("ada_w", l, 0, 8, g * 512, 512)
            yield
            adaln_mm(pb, wt, g)
            yield
        P.tt("dve", MODN, pb[:, 0:48], vec(f"ada_b{l}", 0, 48), ALU.add)
        yield

    def adaln_mm(pb, wt, g):
        for j in range(4):
            jc = g * 4 + j
            for kc in range(8):
                P.mm(pb[:, jc:jc + 1], sub(wt, (slice(None), kc, slice(j * 128, (j + 1) * 128))),
                     sub(SBF, (slice(None), slice(kc, kc + 1))), start=(kc == 0), stop=(kc == 7))

    def adaln_apply():
        P.copy("dve", MOD, MODN)
        P.ts("dve", sub(SCP, (slice(None), slice(0, 8))), sub(MOD, (slice(None), slice(8, 16))), 1.0, ALU.add)
        P.ts("dve", sub(SCP, (slice(None), slice(8, 16))), sub(MOD, (slice(None), slice(32, 40))), 1.0, ALU.add)

    def modcol(k, c):
        return sub(MOD, (slice(None), slice(k * 8 + c, k * 8 + c + 1)))

    def layer_norm(l, which):
        for tb in range(NB):
            par = tb % 2
            rb = R(bg(32768 + par * 8192, 4096), "p (c t) -> p c t", c=8)
            rb2 = R(bg(36864 + par * 8192, 4096), "p (c t) -> p c t", c=8)
            mean_sb = s32(par * 1536, 512)
            m2 = s32(par * 1536 + 512, 512)
            rstd = s32(par * 1536 + 1024, 512)
            ts_ = slice(tb * 512, (tb + 1) * 512)
            for c in range(8):
                P.act(sub(rb, (slice(None), c, slice(None))), sub(XT3, (slice(None), c, ts_)), AF.Copy)
                P.act(sub(rb2, (slice(None), c, slice(None))), sub(XT3, (slice(None), c, ts_)), AF.Square)
            pm = nb()
            pq = nb()
            for c in range(8):
                P.mm(pm[:, :], ONESB, sub(rb, (slice(None), c, slice(None))), start=(c == 0), stop=(c == 7))
            for c in range(8):
                P.mm(pq[:, :], ONESB, sub(rb2, (slice(None), c, slice(None))), start=(c == 0), stop=(c == 7))
            P.copy("act", mean_sb, pm[:, :])
            P.tt("pool", m2, mean_sb, mean_sb, ALU.mult)
            P.ts("pool", m2, m2, -LN_EPS, ALU.add)
            P.tt("dve", rstd, pq[:, :], m2, ALU.subtract)
            P.act(rstd, rstd, AF.Sqrt)
            P.recip(rstd, rstd)
            for c in range(8):
                xv = sub(XT3, (slice(None), c, ts_))
                P.tt("dve", xv, xv, mean_sb, ALU.subtract)
                P.tt("dve", xv, xv, rstd, ALU.mult)
                P.act(xv, xv, AF.Identity, bias=vec(f"ln_b{l}_{which}", c, 1), scale=vec(f"ln_g{l}_{which}", c, 1))

    def modulate_full(dst3, kmod, bounce=False):
        for c in range(8):
            P.act(sub(dst3, (slice(None), c, slice(None))), sub(XT3, (slice(None), c, slice(None))), AF.Identity,
                  bias=modcol(0 if kmod == 0 else 3, c), scale=sub(SCP, (slice(None), slice(kmod * 8 + c, kmod * 8 + c + 1))))
            if bounce:
                P.dma("sp", View(xscr, xscr.t.ap()[:, c * T:(c + 1) * T]), sub(XT3, (slice(None), c, slice(None))))

    def bounce_out():
        for c in range(8):
            P.dma("sp", View(xscr, xscr.t.ap()[:, c * T:(c + 1) * T]), sub(XT3, (slice(None), c, slice(None))))

    def bounce_in():
        for c in range(8):
            P.dma("sp", sub(XT3, (slice(None), c, slice(None))), View(xscr, xscr.t.ap()[:, c * T:(c + 1) * T]))

    def residual(c, tb, pb, gk):
        xv = sub(XT3, (slice(None), c, slice(tb * 512, (tb + 1) * 512)))
        P.act(xv, xv, AF.Copy, scale=ALPHA)
        P.stt("dve", xv, pb, modcol(gk, c), xv, ALU.mult, ALU.add)

    def mlp(l, bg_it=None):
        st["nbm"] = 7

        def tick():
            if bg_it is not None:
                next(bg_it, None)
        HB = R(bg(32768, 8192), "p (c t) -> p c t", c=8)
        HID = R(bg(40960, 32768), "p (c t) -> p c t", c=32)
        tmp = s32(0, 512)
        for hb in range(2):
            t0 = hb * 1024
            for c in range(8):
                P.act(sub(HB, (slice(None), c, slice(None))), sub(XT3, (slice(None), c, slice(t0, t0 + 1024))), AF.Identity,
                      bias=modcol(3, c), scale=sub(SCP, (slice(None), slice(8 + c, 9 + c))))
            for g in range(8):
                wt = load_w("mlp_w1", l, 0, 8, g * 512, 512)
                for j in range(4):
                    hc = g * 4 + j
                    for t2 in range(2):
                        pb = nb()
                        for kc in range(8):
                            P.mm(pb[:, :], sub(wt, (slice(None), kc, slice(j * 128, (j + 1) * 128))),
                                 sub(HB, (slice(None), kc, slice(t2 * 512, (t2 + 1) * 512))), start=(kc == 0), stop=(kc == 7))
                        tm = s32(((hc * 2 + t2) % 2) * 512, 512)
                        P.act(tm, pb[:, :], AF.Relu, bias=vec(f"b1_{l}", hc, 1))
                        P.tt("dve", sub(HID, (slice(None), hc, slice(t2 * 512, (t2 + 1) * 512))), tm, tm, ALU.mult)
                if j == 3:
                    tick()
            for c in range(8):
                wt = load_w("mlp_w2", l, 0, 32, c * 128, 128)
                for t2 in range(2):
                    pb = nb()
                    for kc in range(32):
                        P.mm(pb[:, :], sub(wt, (slice(None), kc, slice(None))),
                             sub(HID, (slice(None), kc, slice(t2 * 512, (t2 + 1) * 512))), start=(kc == 0), stop=(kc == 31))
                    tm = s32(1024 + t2 * 512, 512)
                    P.act(tm, pb[:, :], AF.Identity, bias=vec(f"b2_{l}", c, 1))
                    residual(c, hb * 2 + t2, tm, 5)
                tick()
        layer_norm(l, 1)
        st["nbm"] = 8

    def attention(KT, QT, krows, Vfn, scale, yout_fn, osb_off=0, bg_it=None, bg_every=4):
        PT = [bg(73728 + i * 512, 512) for i in range(4)]
        pending = []

        def finish(qb, po, par):
            o0 = osb_off + par * 512
            osb = s32(o0, 512, rows=65)
            P.copy("act", osb, po[0:65, :])
            P.recip(s32(o0, 512, rows=1, r0=64), s32(o0, 512, rows=1, r0=64))
            pbc = PS[6 + par]
            P.mm(pbc[0:64, :], E65, osb)
            P.tt("dve", yout_fn(qb), s32(o0, 512, rows=64), pbc[0:64, :], ALU.mult)

        for qb in range(NB):
            qs = slice(qb * 512, (qb + 1) * 512)
            st["att"] = st.get("att", 0) + 1
            par = st["att"] % 2
            po = PS[4 + par]
            sc_b = {}
            LOOK = 3
            nk = NKEY // 128

            def issue_s(kt):
                pb = PS[kt % 4]
                P.mm(pb[:, :], sub(KT, (slice(0, krows), slice(kt * 128, (kt + 1) * 128))), sub(QT, (slice(0, krows), qs)))
                sc_b[kt] = pb
            for kt in range(min(LOOK, nk)):
                issue_s(kt)
            for kt in range(nk):
                if kt + LOOK < nk:
                    issue_s(kt + LOOK)
                pt = PT[kt % 4]
                P.act(pt, sc_b.pop(kt)[:, :], AF.Exp, scale=scale)
                P.mm(po[0:65, :], Vfn(kt), pt, start=(kt == 0), stop=(kt == nk - 1))
                if kt == 9 and pending:
                    pending.pop(0)()
                if bg_it is not None and kt % bg_every == 0:
                    next(bg_it, None)
            while pending:
                pending.pop(0)()
            pending.append(lambda qb=qb, po=po, par=par: finish(qb, po, par))
        while pending:
            pending.pop(0)()

    def outproj(dname, i, l, Z3, YATT):
        bounce_in()
        for c in range(8):
            wa = load_w(dname, i, 0, 4, c * 128, 128)
            wb = load_w(dname, i, 512, 8, c * 128, 128, rows=64)
            for tb in range(NB):
                ts_ = slice(tb * 512, (tb + 1) * 512)
                pb = nb()
                for kc in range(4):
                    P.mm(pb[:, :], sub(wa, (slice(None), kc, slice(None))), sub(Z3, (slice(None), kc, ts_)), start=(kc == 0), stop=False)
                for h in range(8):
                    P.mm(pb[:, :], sub(wb, (slice(None), h, slice(None))), sub(YATT(h), (slice(None), ts_)), start=False, stop=(h == 7))
                residual(c, tb, pb[:, :], 2)
        layer_norm(l, 0)

    O_HT = 32768
    O_Z2 = 49152
    O_KR = 57344
    O_FILT = 59904
    O_X1 = 0
    O_VT = 8192
    O_VTOK = 16384
    O_CQ = 24576
    O_CKV = 28672
    O_HID2 = 31232
    HT3 = R(bg(O_HT, 16384), "p (c t) -> p c t", c=8)

    def conv3_chunk(i, j, pbs, dest3, c):
        for tb in range(NB):
            P.copy("act", sub(CT3, (slice(None), slice(2 * tb, 2 * tb + 2), slice(1, 257))), R(pbs[tb][:, :], "p (s j) -> p s j", s=2))
        P.ts("dve", sub(CT3, (slice(None), slice(1, 8), slice(0, 1))), sub(CT3, (slice(None), slice(0, 7), slice(256, 257))), vec("flag"), ALU.mult)
        P.ts("dve", sub(CT3, (slice(None), slice(0, 7), slice(257, 258))), sub(CT3, (slice(None), slice(1, 8), slice(1, 2))), vec("flag"), ALU.mult)
        for tb in range(NB):
            pv = R(pbs[tb][:, :], "p (s j) -> p s j", s=2)
            P.act(pbs[tb][:, :], pbs[tb][:, :], AF.Identity, bias=vec(f"hcb{i}", j, 1), scale=vec(f"hcw{i}", j * 3 + 1, 1))
            P.stt("dve", pv, sub(CT3, (slice(None), slice(2 * tb, 2 * tb + 2), slice(0, 256))), vec(f"hcw{i}", j * 3 + 0, 1), pv, ALU.mult, ALU.add)
            dv = R(sub(dest3, (slice(None), c, slice(tb * 512, (tb + 1) * 512))), "p (s j) -> p s j", s=2)
            P.stt("dve", dv, sub(CT3, (slice(None), slice(2 * tb, 2 * tb + 2), slice(2, 258))), vec(f"hcw{i}", j * 3 + 2, 1), pv, ALU.mult, ALU.add)

    def to_tokmajor(srcT3, dst3):
        for kt in range(NT):
            pb = nb()
            pv = View(pb, pb.t[:, 0:256].bitcast(BF16))
            for c in range(4):
                P.transpose(sub(pv, (slice(None), slice(c * 128, (c + 1) * 128))), sub(srcT3, (slice(None), c, slice(kt * 128, (kt + 1) * 128))), IDB)
            P.copy("act" if kt % 2 else "dve", sub(dst3, (slice(None), kt, slice(None))), pv)

    def rms_rstd(dst, psum_sumsq, n, rows=128):
        P.ts("dve", dst, psum_sumsq, 1.0 / n, ALU.mult, RMS_EPS, ALU.add)
        P.act(dst, dst, AF.Sqrt)
        P.recip(dst, dst)

    def even_mixer(i, l):
        P.dma("sp", View(ROPEC, ROPEC.t[:, :]), dview("rope_c", lambda a: a[0]))
        P.dma("sp", View(ROPES, ROPES.t[:, :]), dview("rope_s", lambda a: a[0]))
        modulate_full(HT3, 0, bounce=True)
        X1 = R(bg(O_X1, 8192), "p (c t) -> p c t", c=4)
        X2 = R(bg(O_Z2, 8192), "p (c t) -> p c t", c=4)
        VT = R(bg(O_VT, 8192), "p (c t) -> p c t", c=4)
        VTOK = R(bg(O_VTOK, 8192), "p (k c) -> p k c", k=16)
        CQ = R(bg(O_CQ, 4096), "p (c t) -> p c t", c=2)
        CKV = bg(O_CKV, NKEY)
        KR = bg(O_KR, NKEY, rows=32)
        dests = [X1, X2, VT]
        for g in range(3):
            wt = load_w("ev_w_in", i, 0, 8, g * 512, 512)
            for jj in range(4):
                j = g * 4 + jj
                pbs = [nb() for _ in range(NB)]
                for tb in range(NB):
                    for kc in range(8):
                        P.mm(pbs[tb][:, :], sub(wt, (slice(None), kc, slice(jj * 128, (jj + 1) * 128))),
                             sub(HT3, (slice(None), kc, slice(tb * 512, (tb + 1) * 512))), start=(kc == 0), stop=(kc == 7))
                conv3_chunk(i, j, pbs, dests[g], jj)
        if EST <= 1:
            bounce_in()
            return
        wt = load_w("ev_w_in", i, 0, 8, 1536, 448)
        sq = s32(2100, 512)
        for tb in range(NB):
            ts_ = slice(tb * 512, (tb + 1) * 512)
            pss = nb()
            pcs = []
            for c in range(2):
                pb = nb()
                pcs.append(pb)
                for kc in range(8):
                    P.mm(pb[:, :], sub(wt, (slice(None), kc, slice(c * 128, (c + 1) * 128))), sub(HT3, (slice(None), kc, ts_)), start=(kc == 0), stop=(kc == 7))
                P.act(sq, pb[:, :], AF.Square)
                P.mm(pss[:, :], ONES128, sq, start=(c == 0), stop=(c == 1))
            rst = s32(2612, 512)
            rms_rstd(rst, pss[:, :], 256)
            for c in range(2):
                P.stt("dve", sub(CQ, (slice(None), c, ts_)), pcs[c][:, :], vec(f"qg{i}", c, 1), rst, ALU.mult, ALU.mult)
            pb = nb()
            for kc in range(8):
                P.mm(pb[:, :], sub(wt, (slice(None), kc, slice(256, 384))), sub(HT3, (slice(None), kc, ts_)), start=(kc == 0), stop=(kc == 7))
            P.act(sq, pb[:, :], AF.Square)
            pss2 = nb()
            P.mm(pss2[:, :], ONES128, sq)
            rms_rstd(rst, pss2[:, :], 128)
            ckf = s32(0, 512)
            P.stt("dve", ckf, pb[:, :], vec(f"kvg{i}", 0, 1), rst, ALU.mult, ALU.mult)
            P.copy("act", sub(CKV, (slice(None), slice(512 + tb * 512, 1024 + tb * 512))), ckf)
            if "nockvout" in DBGF:
                continue
            pt = nb()
            for q in range(4):
                P.transpose(pt[:, q * 128:(q + 1) * 128], sub(ckf, (slice(None), slice(q * 128, (q + 1) * 128))), IDF)
            ost = s32(512, 512)
            P.copy("act", ost, pt[:, :])
            P.dma("sp", View(o_ckv, o_ckv.t.ap()[i, tb * 512:(tb + 1) * 512, :].rearrange("(q p) r -> p q r", p=128)),
                  R(ost, "p (q r) -> p q r", q=4))
            if "nokr" in DBGF:
                continue
            pk = nb()
            pkp = nb()
            for kc in range(8):
                P.mm(pk[0:32, :], sub(wt, (slice(None), kc, slice(384, 416))), sub(HT3, (slice(None), kc, ts_)), start=(kc == 0), stop=(kc == 7))
            for kc in range(8):
                P.mm(pkp[0:32, :], sub(wt, (slice(None), kc, slice(416, 448))), sub(HT3, (slice(None), kc, ts_)), start=(kc == 0), stop=(kc == 7))
            k1 = s32(1024, 512, rows=32)
            k2 = s32(1536, 512, rows=32)
            P.tt("dve", k1, pk[0:32, :], View(ROPEC, ROPEC.t[0:32, ts_]), ALU.mult)
            P.tt("dve", k2, pkp[0:32, :], View(ROPES, ROPES.t[0:32, ts_]), ALU.mult)
            P.tt("pool", k1, k1, k2, ALU.add)
            P.copy("act", sub(KR, (slice(None), slice(512 + tb * 512, 1024 + tb * 512))), k1)
            if "nokrout" in DBGF:
                continue
            pt2 = nb()
            for q in range(4):
                P.mm(pt2[:, q * 32:(q + 1) * 32], sub(k1, (slice(None), slice(q * 128, (q + 1) * 128))), View(CST, CST.t[0:32, 0:32]))
            ost2 = s32(2100, 128)
            P.copy("act", ost2, pt2[:, 0:128])
            P.dma("sp", View(o_kr, o_kr.t.ap()[i, tb * 512:(tb + 1) * 512, :].rearrange("(q p) r -> p q r", p=128)),
                  R(ost2, "p (q r) -> p q r", q=4))
        if EST <= 2:
            bounce_in()
            return
        for q in range(4):
            stg = s32(0, 128)
            P.dma("sp", stg, dview("c_ckv", lambda a: a[i, q * 128:(q + 1) * 128, :]))
            stg2 = s32(128, 32)
            P.dma("sp", stg2, dview("c_kr", lambda a: a[i, q * 128:(q + 1) * 128, :]))
            pb = nb()
            P.transpose(pb[:, 0:128], stg, IDF)
            P.copy("act", sub(CKV, (slice(None), slice(q * 128, (q + 1) * 128))), pb[:, 0:128])
            if "noctxkr" not in DBGF:
                pb2 = nb()
                P.mm(pb2[0:32, 0:128], stg2, IDF)
                P.copy("act", sub(KR, (slice(None), slice(q * 128, (q + 1) * 128))), pb2[0:32, 0:128])
        if EST <= 3:
            bounce_in()
            return
        if i == 0:
            dbg("x1", bg(O_X1, 8192), [128, 8192], BF16)
            dbg("x2", bg(O_Z2, 8192), [128, 8192], BF16)
            dbg("v", bg(O_VT, 8192), [128, 8192], BF16)
            dbg("cq", bg(O_CQ, 4096), [128, 4096], BF16)
        hyena(i, X1, X2, VT, VTOK)
        if i == 0:
            dbg("y1", bg(O_VT, 8192), [128, 8192], BF16)
            dbg("z2", bg(O_Z2, 8192), [128, 8192], BF16)
        if EST <= 4:
            bounce_in()
            return
        mla(i, CQ, CKV, KR)
        if i == 0:
            dbg("yatt", bg(O_HT, 16384, rows=64), [64, 16384], BF16)
            dbg("vh", bg(12288 + 1300, 1300), [128, 1300], BF16)
            dbg("kt", bg(NKEY, NKEY, rows=104), [104, NKEY], BF16)
            dbg("qt", bg(8192 + T, T, rows=104), [104, T], BF16)
        if EST <= 5:
            bounce_in()
            return
        YATT = lambda h: bg(O_HT + h * T, T, rows=64)
        outproj("ev_w_out", i, l, X2, YATT)

    def hyena(i, X1, X2, VT, VTOK):
        FILT = R(bg(O_FILT, 16384), "p (k d c) -> p k d c", k=16, d=2)
        YR = R(bg(O_HT, 8192), "p (f c) -> p f c", f=16)
        YI = R(bg(O_HT + 8192, 8192), "p (f c) -> p f c", f=16)

        h1 = bgf(O_HT, 2048, rows=64)
        h2 = bgf(O_HT + 4096, 2048, rows=64)
        tmpf = bgf(O_HT + 8192, 2048, rows=64)
        zT = bgf(O_HT + 12288, 2048, rows=33)
        P.dma("sp", zT, dview("zT"))
        w1 = s32(2100, 64, rows=33)
        P.dma("sp", w1, dview("f_w1", lambda a: a[i]))
        w2 = s32(2164, 64, rows=64)
        P.dma("sp", w2, dview("f_w2", lambda a: a[i]))
        P.tt("dve", sub(FB, (slice(None), slice(0, 1))), vec(f"fb1_{i}", 0, 1, rows=64), vec(f"ffreq{i}", 0, 1, rows=64), ALU.mult)
        P.tt("dve", sub(FB, (slice(None), slice(1, 2))), vec(f"fb2_{i}", 0, 1, rows=64), vec(f"ffreq{i}", 0, 1, rows=64), ALU.mult)

        def sin_layer(dst, wv, src, krows, bcol):
            for tb in range(NB):
                ts_ = slice(tb * 512, (tb + 1) * 512)
                pb = nb()
                P.mm(pb[0:64, :], wv, sub(src, (slice(0, krows), ts_)))
                a = sub(dst, (slice(None), ts_))
                tq = sub(tmpf, (slice(None), ts_))
                P.act(a, pb[0:64, :], AF.Identity, bias=sub(FB, (slice(None), slice(bcol, bcol + 1))), scale=vec(f"ffreq{i}", 0, 1, rows=64))
                MG = 12582912.0
                P.ts("dve", tq, a, 1.0 / (2 * math.pi), ALU.mult, MG, ALU.add)
                P.ts("dve", tq, tq, MG, ALU.subtract, -2 * math.pi, ALU.mult)
                P.tt("dve", a, a, tq, ALU.add)
                P.ts("dve", a, a, 3.141592, ALU.min, -3.141592, ALU.max)
                P.act(a, a, AF.Sin)
        sin_layer(h1, w1, zT, 33, 0)
        sin_layer(h2, w2, h1, 64, 1)
        HB2 = View(SMALL, SMALL.t[0:64, 0:1])
        H2B = View(BIG, BIG.t[64:128, O_KR:O_KR + T])
        P.copy("dve", H2B, h2)
        to_tokmajor(VT, VTOK)
        for n in range(2):
            k = st["wr"] % 3
            st["wr"] += 1
            W3 = View(WR, WR.t[64:128, k, 0:1024])
            P.dma("pool", sub(W3, (slice(None), slice(0, 512))), dview("f_w3", lambda a: a[i, :, n * 512:(n + 1) * 512]))
            P.dma("pool", sub(W3, (slice(None), slice(512, 1024))), dview("f_w3", lambda a: a[i, :, 1024 + n * 512:1024 + (n + 1) * 512]))
            AD = s32(2100, 1024)
            P.dma("sp", sub(AD, (slice(None), slice(0, 512))), dview("decay_bc", lambda a: a[i, :, n * 512:(n + 1) * 512]))
            P.dma("sp", sub(AD, (slice(None), slice(512, 1024))), dview("decay_bc", lambda a: a[i, :, 1024 + n * 512:1024 + (n + 1) * 512]))
            P.act(AD, AD, AF.Abs)
            for kt in range(NT):
                pf = nb()
                pbk = nb()
                P.mm(pf[:, :], sub(H2B, (slice(None), slice(kt * 128, (kt + 1) * 128))), sub(W3, (slice(None), slice(0, 512))))
                P.mm(pbk[:, :], sub(H2B, (slice(None), slice(kt * 128, (kt + 1) * 128))), sub(W3, (slice(None), slice(512, 1024))))
                o_ = (kt % 2) * 1024
                w1 = s32(o_, 512)
                w2 = s32(o_ + 512, 512)
                P.act(w1, sub(AD, (slice(None), slice(0, 512))), AF.Exp, scale=vec("negt", kt, 1))
                P.act(w2, sub(AD, (slice(None), slice(512, 1024))), AF.Exp, scale=vec("negt", kt, 1))
                P.tt("dve", w1, pf[:, :], w1, ALU.mult)
                P.stt("dve", w2, pbk[:, :], vec("m0", kt, 1), w2, ALU.mult, ALU.mult)
                P.tt("pool", sub(FILT, (slice(None), kt, 0, slice(None))), w1, w2, ALU.add)
                P.tt("pool", sub(FILT, (slice(None), kt, 1, slice(None))), w2, w1, ALU.subtract)
            SRC = VTOK
            if n == 0 and i == 0:
                dbg("filt", bg(O_FILT, 16384), [128, 16384], BF16)
                dbg("h2b", View(BIG, BIG.t[64:128, O_KR:O_KR + T]), [64, T], BF16)
                dbg("vtok", bg(O_VTOK, 8192), [128, 8192], BF16)
            pn1 = nb()
            pn2 = nb()
            for kt in range(NT):
                P.mm(pn1[0:8, :], sub(FNT3, (slice(None), kt, slice(None))), sub(SRC, (slice(None), kt, slice(None))), start=(kt == 0), stop=(kt == NT - 1))
            for kt in range(NT):
                P.mm(pn2[0:8, :], sub(FNT3, (slice(None), kt, slice(None))), sub(FILT, (slice(None), kt, 0, slice(None))), start=(kt == 0), stop=(kt == NT - 1))
            tn = s32(3072, 512, rows=8)
            P.act(tn, pn2[0:8, :], AF.Identity, scale=vec("wN", 0, 1, rows=8))
            P.tt("dve", YN, pn1[0:8, :], tn, ALU.mult)
            for fc in range(16):
                k = st["wr"] % 3
                st["wr"] += 1
                CT = View(WR, WR.t[:, k, 0:2048].rearrange("p (k j) -> p k j", k=16))
                ST = View(WR, WR.t[:, k, 2048:4096].rearrange("p (k j) -> p k j", k=16))
                P.dma("sp", CT, dview("cft", lambda a: a[fc].rearrange("p (k j) -> p k j", k=16)))
                P.dma("sp", ST, dview("sft", lambda a: a[fc].rearrange("p (k j) -> p k j", k=16)))
                pa, pp, pbb, pq = nb(), nb(), nb(), nb()
                for kt in range(NT):
                    fl = (kt == 0)
                    ll = (kt == NT - 1)
                    P.mm(pa[:, :], sub(CT, (slice(None), kt, slice(None))), sub(SRC, (slice(None), kt, slice(None))), start=fl, stop=ll)
                    P.mm(pp[:, :], sub(CT, (slice(None), kt, slice(None))), sub(FILT, (slice(None), kt, 0, slice(None))), start=fl, stop=ll)
                    P.mm(pbb[:, :], sub(ST, (slice(None), kt, slice(None))), sub(SRC, (slice(None), kt, slice(None))), start=fl, stop=ll)
                    P.mm(pq[:, :], sub(ST, (slice(None), kt, slice(None))), sub(FILT, (slice(None), kt, 1, slice(None))), start=fl, stop=ll)
                p1 = s32(0, 512)
                q1 = s32(512, 512)
                P.act(p1, pp[:, :], AF.Identity, scale=vec("wf", fc, 1))
                P.act(q1, pq[:, :], AF.Identity, scale=vec("wf", fc, 1))
                t1, t2, t3, t4 = s32(1024, 512), s32(1536, 512), s32(2100, 512), s32(2612, 512)
                P.tt("dve", t1, pa[:, :], p1, ALU.mult)
                P.tt("dve", t2, pbb[:, :], q1, ALU.mult)
                P.tt("pool", sub(YR, (slice(None), fc, slice(None))), t1, t2, ALU.add)
                P.tt("dve", t3, pbb[:, :], p1, ALU.mult)
                P.tt("dve", t4, pa[:, :], q1, ALU.mult)
                P.tt("pool", sub(YI, (slice(None), fc, slice(None))), t3, t4, ALU.subtract)
            if n == 0 and i == 0:
                dbg("yr", bg(O_HT, 8192), [128, 8192], BF16)
                dbg("yi", bg(O_HT + 8192, 8192), [128, 8192], BF16)
            for tb in range(NB):
                ts_ = slice(tb * 512, (tb + 1) * 512)
                P.dma("sp", GN, dview("gn", lambda a: a[:, ts_]))
                acc = [nb() for _ in range(4)]
                for fq in range(4):
                    k = st["wr"] % 3
                    st["wr"] += 1
                    GC4 = View(WR, WR.t[:, k, 0:2048].rearrange("p (k t) -> p k t", k=4))
                    GS4 = View(WR, WR.t[:, k, 2048:4096].rearrange("p (k t) -> p k t", k=4))
                    P.dma("sp", GC4, dview("cf", lambda a: a[fq * 512:(fq + 1) * 512, ts_].rearrange("(k p) t -> p k t", p=128)))
                    P.dma("sp", GS4, dview("sf", lambda a: a[fq * 512:(fq + 1) * 512, ts_].rearrange("(k p) t -> p k t", p=128)))
                    for f4 in range(4):
                        fc = fq * 4 + f4
                        for c in range(4):
                            cs = slice(c * 128, (c + 1) * 128)
                            P.mm(acc[c][:, :], sub(YR, (slice(None), fc, cs)), sub(GC4, (slice(None), f4, slice(None))), start=(fc == 0), stop=False)
                            P.mm(acc[c][:, :], sub(YI, (slice(None), fc, cs)), sub(GS4, (slice(None), f4, slice(None))), start=False, stop=False)
                for c in range(4):
                    cs = slice(c * 128, (c + 1) * 128)
                    P.mm(acc[c][:, :], sub(YN, (slice(None), cs)), GN, start=False, stop=True)
                    tm = s32(3072, 512)
                    if n == 0:
                        vv = sub(VT, (slice(None), c, ts_))
                        P.stt("dve", tm, vv, vec(f"skip{i}_0", c, 1), acc[c][:, :], ALU.mult, ALU.add)
                        P.tt("pool", vv, tm, sub(X1, (slice(None), c, ts_)), ALU.mult)
                    else:
                        vv = sub(VT, (slice(None), c, ts_))
                        P.stt("dve", tm, vv, vec(f"skip{i}_1", c, 1), acc[c][:, :], ALU.mult, ALU.add)
                        xv = sub(X2, (slice(None), c, ts_))
                        P.tt("pool", xv, tm, xv, ALU.mult)
            if n == 0:
                to_tokmajor(VT, VTOK)

    def mla(i, CQ, CKV, KR):
        KTb = [bg(0 + b * NKEY, NKEY) for b in range(2)]
        QTb = [bg(8192 + b * T, T) for b in range(2)]
        VHb = [R(bg(12288 + b * 1300, 1300), "p (k e) -> p k e", k=20) for b in range(2)]
        for b in range(2):
            P.dma("sp", View(BIG, BIG.t[96:104, b * NKEY:(b + 1) * NKEY]), dview("maskB"))
            P.dma("sp", View(BIG, BIG.t[96:104, 8192 + b * T:8192 + (b + 1) * T]), dview("maskA"))
            P.memset("dve", bg(12288 + b * 1300, 1300), 1.0)
        k = st["wr"] % 3
        st["wr"] += 1
        WKV = View(WR, WR.t[:, k, 0:1024])
        P.dma("pool", WKV, dview("w_ukv", lambda a: a[i]))
        WUQ = View(WR, WR.t[:, k, 1024:4096].rearrange("p (k n) -> p k n", k=2))
        P.dma("pool", WUQ, dview("w_uq", lambda a: a[i].rearrange("(k p) n -> p k n", p=128)))
        sc = 96.0 ** -0.5
        for h in range(8):
            b = h % 2
            KT, QT, VH = KTb[b], QTb[b], VHb[b]
            for kb in range(5):
                pb = nb()
                P.mm(pb[0:64, :], sub(WKV, (slice(None), slice(h * 128, h * 128 + 64))), sub(CKV, (slice(None), slice(kb * 512, (kb + 1) * 512))))
                P.copy("act", sub(KT, (slice(0, 64), slice(kb * 512, (kb + 1) * 512))), pb[0:64, :])
            P.copy("dve", sub(KT, (slice(64, 96), slice(None))), KR)
            for k0, kn in ((0, 8), (8, 8), (16, 4)):
                pb = nb()
                for jj in range(kn):
                    kt = k0 + jj
                    P.mm(pb[:, jj * 64:(jj + 1) * 64], sub(CKV, (slice(None), slice(kt * 128, (kt + 1) * 128))), sub(WKV, (slice(None), slice(h * 128 + 64, h * 128 + 128))))
                P.copy("act", sub(VH, (slice(None), slice(k0, k0 + kn), slice(0, 64))), R(pb[:, 0:kn * 64], "p (k e) -> p k e", k=kn))
            for tb in range(NB):
                ts_ = slice(tb * 512, (tb + 1) * 512)
                pq = nb()
                pp = nb()
                for kc in range(2):
                    P.mm(pq[0:96, :], sub(WUQ, (slice(None), kc, slice(h * 192, h * 192 + 96))), sub(CQ, (slice(None), kc, ts_)), start=(kc == 0), stop=(kc == 1))
                for kc in range(2):
                    P.mm(pp[0:96, :], sub(WUQ, (slice(None), kc, slice(h * 192 + 96, h * 192 + 192))), sub(CQ, (slice(None), kc, ts_)), start=(kc == 0), stop=(kc == 1))
                t1 = s32(1024 + (tb % 2) * 1024, 512, rows=32, r0=64)
                t2 = s32(1536 + (tb % 2) * 1024, 512, rows=32, r0=64)
                P.copy("dve", sub(QT, (slice(0, 64), ts_)), pq[0:64, :])
                P.tt("dve", t1, pq[64:96, :], View(ROPEC, ROPEC.t[64:96, ts_]), ALU.mult)
                P.tt("dve", t2, pp[64:96, :], View(ROPES, ROPES.t[64:96, ts_]), ALU.mult)
                P.tt("pool", sub(QT, (slice(64, 96), ts_)), t1, t2, ALU.add)
            attention(KT, QT, 104, lambda kt, VH=VH: sub(VH, (slice(None), kt, slice(None))), sc,
                      lambda qb, h=h: bg(O_HT + h * T + qb * 512, 512, rows=64))

    def odd_mixer(i, l):
        P.dma("sp", View(ROPEC, ROPEC.t[:, :]), dview("rope_c", lambda a: a[1]))
        P.dma("sp", View(ROPES, ROPES.t[:, :]), dview("rope_s", lambda a: a[1]))
        modulate_full(HT3, 0, bounce=True)
        YCVF = R(bgf(0, 4 * T), "p (c t) -> p c t", c=4)
        YCV = R(bg(O_Z2, 8192), "p (c t) -> p c t", c=4)
        KTs = [bg(16384 + b * NKEY, NKEY) for b in range(2)]
        VV = R(bg(21504, 2600), "p (k g e) -> p k g e", k=20, g=2)
        QTs = [bg(57344 + h * T, T) for h in range(8)]
        for b in range(2):
            P.dma("sp", View(BIG, BIG.t[64:72, 16384 + b * NKEY:16384 + (b + 1) * NKEY]), dview("maskB"))
        for h in range(8):
            P.dma("sp", View(BIG, BIG.t[64:72, 57344 + h * T:57344 + (h + 1) * T]), dview("maskA"))
        P.memset("dve", bg(21504, 2600), 1.0)
        G3 = R(s32(0, 8 * 286), "p (s j) -> p s j", s=8)
        P.memset("pool", s32(0, 8 * 286), 0.0)
        GBF = R(bg(24576, 8192), "p (c t) -> p c t", c=4)
        for c in range(4):
            wa = load_w("od_w_in", i, 0, 8, c * 128, 128)
            wg = load_w("od_w_in", i, 0, 8, 512 + c * 128, 128)
            for tb in range(NB):
                ts_ = slice(tb * 512, (tb + 1) * 512)
                pa = nb()
                pg = nb()
                for kc in range(8):
                    P.mm(pa[:, :], sub(wa, (slice(None), kc, slice(None))), sub(HT3, (slice(None), kc, ts_)), start=(kc == 0), stop=(kc == 7))
                for kc in range(8):
                    P.mm(pg[:, :], sub(wg, (slice(None), kc, slice(None))), sub(HT3, (slice(None), kc, ts_)), start=(kc == 0), stop=(kc == 7))
                sg = s32(2400 + (tb % 2) * 512, 512)
                P.act(sg, pg[:, :], AF.Sigmoid)
                P.tt("dve", sub(GBF, (slice(None), c, ts_)), pa[:, :], sg, ALU.mult)

        def conv_chunk(c):
            P.memset("pool", sub(G3, (slice(None), slice(0, 1), slice(0, 15))), 0.0)
            P.memset("pool", sub(G3, (slice(None), slice(7, 8), slice(271, 286))), 0.0)
            P.copy("pool", sub(G3, (slice(None), slice(None), slice(15, 271))), R(sub(GBF, (slice(None), c, slice(None))), "p (s j) -> p s j", s=8))
            P.ts("dve", sub(G3, (slice(None), slice(1, 8), slice(0, 15))), sub(G3, (slice(None), slice(0, 7), slice(256, 271))), vec("flag"), ALU.mult)
            P.ts("dve", sub(G3, (slice(None), slice(0, 7), slice(271, 286))), sub(G3, (slice(None), slice(1, 8), slice(15, 30))), vec("flag"), ALU.mult)
            av = R(sub(YCVF, (slice(None), c, slice(None))), "p (s j) -> p s j", s=8)
            P.act(av, sub(G3, (slice(None), slice(None), slice(0, 256))), AF.Identity,
                  bias=vec(f"dwb{i}", c, 1), scale=vec(f"dww{i}", c * 31, 1))
            yield
            for k in range(1, 31):
                P.stt("dve", av, sub(G3, (slice(None), slice(None), slice(k, k + 256))), vec(f"dww{i}", c * 31 + k, 1), av, ALU.mult, ALU.add)
                yield
        if OST <= 1:
            bounce_in()
            return
        BONES = View(CST, CST.t[:, 384:512])

        def normed_pair(dst0, dst1, wcol, gname, gpname, tb, wt_main, wt_part):
            ts_ = slice(tb * 512, (tb + 1) * 512)
            pq = nb()
            pp = nb()
            for kc in range(8):
                P.mm(pq[:, :], sub(wt_main, (slice(None), kc, wcol)), sub(HT3, (slice(None), kc, ts_)), start=(kc == 0), stop=(kc == 7))
            for kc in range(8):
                P.mm(pp[:, :], sub(wt_part, (slice(None), kc, wcol)), sub(HT3, (slice(None), kc, ts_)), start=(kc == 0), stop=(kc == 7))
            st["nh"] = st.get("nh", 0) + 1
            o_ = (st["nh"] % 2) * 1536
            sq = s32(o_ + 512, 512)
            P.act(sq, pq[:, :], AF.Square)
            pss = nb()
            P.mm(pss[:, :], BONES, sq)
            rst = s32(o_ + 1024, 512)
            rms_rstd(rst, pss[:, :], 64)
            t1 = s32(o_, 512)
            t2 = s32(o_ + 512, 512)
            P.stt("dve", t1, pq[:, :], vec(gname, 0, 1), rst, ALU.mult, ALU.mult)
            P.stt("dve", t2, pp[:, :], vec(gpname, 0, 1), rst, ALU.mult, ALU.mult)
            P.tt("pool", t1, t1, View(ROPEC, ROPEC.t[:, ts_]), ALU.mult)
            P.tt("pool", t2, t2, View(ROPES, ROPES.t[:, ts_]), ALU.mult)
            P.tt("dve", t1, t1, t2, ALU.add)
            P.copy("act", dst0, sub(t1, (slice(0, 64), slice(None))))
            P.copy("dve", dst1, sub(t1, (slice(64, 128), slice(None))))
            return t1
        for g in range(0 if "noq" in DBGF else 2):
            wq = load_w("od_w_in", i, 0, 8, 1024 + g * 256, 256)
            wqp = load_w("od_w_in", i, 0, 8, 1280 + 512 + g * 256, 256)
            for hp in range(2):
                h = g * 4 + hp * 2
                for tb in range(NB):
                    tsl = slice(tb * 512, (tb + 1) * 512)
                    normed_pair(sub(QTs[h], (slice(0, 64), tsl)), sub(QTs[h + 1], (slice(0, 64), tsl)), slice(hp * 128, hp * 128 + 128),
                                f"gq2{i}", f"gqp2{i}", tb, wq, wqp)
        wk = load_w("od_w_in", i, 0, 8, 1536, 256)
        wkp = load_w("od_w_in", i, 0, 8, 2304, 128)
        for tb in range(0 if "nok" in DBGF else NB):
            ksl = slice(512 + tb * 512, 1024 + tb * 512)
            t1 = normed_pair(sub(KTs[0], (slice(0, 64), ksl)), sub(KTs[1], (slice(0, 64), ksl)), slice(0, 128),
                             f"gk2{i}", f"gkp2{i}", tb, wk, wkp)
            pt = nb()
            for q in range(4):
                P.transpose(pt[:, q * 128:(q + 1) * 128], sub(t1, (slice(None), slice(q * 128, (q + 1) * 128))), IDF)
            ost = s32(3072, 512)
            P.copy("act", ost, pt[:, :])
            P.dma("sp", View(o_gk, o_gk.t.ap()[i, tb * 512:(tb + 1) * 512, :].rearrange("(q p) r -> p q r", p=128)),
                  R(ost, "p (q r) -> p q r", q=4))
        for kt in range(0 if "nov" in DBGF else NT):
            pb = nb()
            for kc in range(8):
                P.mm(pb[:, 0:128], sub(HT3, (slice(None), kc, slice(kt * 128, (kt + 1) * 128))), sub(wk, (slice(None), kc, slice(128, 256))), start=(kc == 0), stop=(kc == 7))
            ost = s32(3328 + (kt % 2) * 128, 128)
            P.copy("act", ost, pb[:, 0:128])
            P.copy("act", sub(VV, (slice(None), 4 + kt, slice(None), slice(0, 64))), R(ost, "p (g e) -> p g e", g=2))
            P.dma("sp", View(o_gv, o_gv.t.ap()[i, kt * 128:(kt + 1) * 128, :]), ost)
        for q in range(0 if "noctx" in DBGF else 4):
            stg = s32(0, 128)
            P.dma("sp", stg, dview("c_gk", lambda a: a[i, q * 128:(q + 1) * 128, :]))
            stg2 = s32(128, 128)
            P.dma("sp", stg2, dview("c_gv", lambda a: a[i, q * 128:(q + 1) * 128, :]))
            for kv in range(0 if "noctxk" in DBGF else 2):
                pb = nb()
                P.mm(pb[0:64, 0:128], sub(stg, (slice(None), slice(kv * 64, kv * 64 + 64))), IDF)
                P.copy("act", sub(KTs[kv], (slice(0, 64), slice(q * 128, (q + 1) * 128))), pb[0:64, 0:128])
            if "noctxv" not in DBGF:
                P.copy("act", sub(VV, (slice(None), q, slice(None), slice(0, 64))), R(stg2, "p (g e) -> p g e", g=2))
        if OST <= 2:
            bounce_in()
            return
        cv_it = iter(())
        for h in range(8):
            kv = h // 4
            if h % 2 == 0:
                for _ in cv_it:
                    pass
                cv_it = conv_chunk(h // 2)
            attention(KTs[kv], QTs[h], 72, lambda kt, kv=kv: sub(VV, (slice(None), kt, kv, slice(None))), 0.125,
                      lambda qb, h=h: bg(O_HT + h * T + qb * 512, 512, rows=64), osb_off=2400, bg_it=cv_it, bg_every=4)
        for _ in cv_it:
            pass
        rb = R(bg(24576, 2048), "p (c t) -> p c t", c=4)
        rb2 = R(bg(24576 + 2048, 2048), "p (c t) -> p c t", c=4)
        ONES512 = View(CSTB, CSTB.t[:, 256:384])
        mean_sb = s32(0, 512)
        m2 = s32(512, 512)
        rstd = s32(1024, 512)
        for tb in range(NB):
            ts_ = slice(tb * 512, (tb + 1) * 512)
            for c in range(4):
                P.act(sub(rb, (slice(None), c, slice(None))), sub(YCVF, (slice(None), c, ts_)), AF.Copy)
                P.act(sub(rb2, (slice(None), c, slice(None))), sub(YCVF, (slice(None), c, ts_)), AF.Square)
            pm = nb()
            pq = nb()
            for c in range(4):
                P.mm(pm[:, :], ONES512, sub(rb, (slice(None), c, slice(None))), start=(c == 0), stop=(c == 3))
            for c in range(4):
                P.mm(pq[:, :], ONES512, sub(rb2, (slice(None), c, slice(None))), start=(c == 0), stop=(c == 3))
            P.copy("act", mean_sb, pm[:, :])
            P.tt("pool", m2, mean_sb, mean_sb, ALU.mult)
            P.ts("pool", m2, m2, -LN_EPS, ALU.add)
            P.tt("dve", rstd, pq[:, :], m2, ALU.subtract)
            P.act(rstd, rstd, AF.Sqrt)
            P.recip(rstd, rstd)
            for c in range(4):
                xv = sub(YCVF, (slice(None), c, ts_))
                P.tt("dve", xv, xv, mean_sb, ALU.subtract)
                P.tt("dve", xv, xv, rstd, ALU.mult)
                P.act(sub(YCV, (slice(None), c, ts_)), xv, AF.Silu, bias=vec(f"cvb{i}", c, 1), scale=vec(f"cvg{i}", c, 1))
        if OST <= 3:
            bounce_in()
            return
        if OST <= 4:
            bounce_in()
            return
        YATT = lambda h: bg(O_HT + h * T, T, rows=64)
        outproj("od_w_out", i, l, YCV, YATT)

    P.memset("dve", View(CSTB, CSTB.t[:, 256:384]), 1.0 / 512)
    ada_it = adaln_tiles(0)
    for _ in ada_it:
        pass
    for l in range(4):
        if (l, 1) <= tuple(STOP):
            adaln_apply()
            if "nomixer" in DBGF:
                pass
            elif l % 2 == 0:
                P.memset("pool", CTMP, 0.0)
                even_mixer(l // 2, l)
            else:
                odd_mixer(l // 2, l)
        if (l, 2) <= tuple(STOP):
            ada_it = adaln_tiles(l + 1) if (l < 3 and "noada" not in DBGF) else iter(())
            mlp(l, ada_it)
            for _ in ada_it:
                pass
    for tt in range(0 if 'noout' in DBGF else NT):
        stg = s32(((tt % 2) * 1024), 1024)
        for half in range(2):
            pb = nb()
            for j in range(4):
                c = half * 4 + j
                P.transpose(pb[:, j * 128:(j + 1) * 128], sub(XT3, (slice(None), c, slice(tt * 128, (tt + 1) * 128))), IDF)
            P.copy("dve" if half else "act", sub(stg, (slice(None), slice(half * 512, half * 512 + 512))), pb[:, :])
        P.dma("sp", View(y_out, y_out.t.ap()[tt * 128:(tt + 1) * 128, :]), stg)
    P.emit()
    es.close()
    return nc

_CACHE = {}


def _prep_shared(inp):
    g = lambda k: np.asarray(inp[k], np.float32)
    W = {}
    ev = g("ev_w_in")
    p32 = perm2d(16)
    krp = ev[:, :, 1920:1952][:, :, p32]
    W["ev_w_in"] = np.ascontiguousarray(np.concatenate([ev, krp], axis=2))
    uq = g("mla_w_uq").reshape(2, 256, 8, 96)
    part = np.zeros((2, 256, 8, 96), np.float32)
    part[:, :, :, 64:96] = uq[:, :, :, 64:96][:, :, :, p32]
    W["w_uq"] = np.ascontiguousarray(np.concatenate([uq, part], axis=3).reshape(2, 256, 8 * 192))
    W["w_ukv"] = g("mla_w_ukv")
    W["ev_w_out"] = g("ev_w_out")
    od = g("od_w_in")
    p64 = perm2d(32)
    q = od[:, :, 1024:1536].reshape(2, 1024, 8, 64)
    k = od[:, :, 1536:1664].reshape(2, 1024, 2, 64)
    qp = q[:, :, :, p64].reshape(2, 1024, 512)
    kp = k[:, :, :, p64].reshape(2, 1024, 128)
    W["od_w_in"] = np.ascontiguousarray(np.concatenate([od, qp, kp], axis=2))
    W["od_w_out"] = g("od_w_out")
    W["ada_w"] = g("ada_w")
    W["mlp_w1"] = g("mlp_w1")
    W["mlp_w2"] = g("mlp_w2")
    W["f_w1"] = g("hy_filt_w1")
    W["f_w2"] = g("hy_filt_w2")
    W["f_w3"] = g("hy_filt_w3")
    W["decay_bc"] = np.ascontiguousarray(np.broadcast_to(g("hy_decay")[:, None, :], (2, 128, 2048)))
    W["ident"] = np.eye(128, dtype=np.float32)
    return W


def _vecs(inp, cond, cvec):
    g = lambda k: np.asarray(inp[k], np.float32)
    vp = VecPack()
    vp.add("cond", chunked(cond))
    for k_, v_ in cvec.items():
        vp.add(k_, v_)
    for l in range(4):
        vp.add(f"ada_b{l}", chunked(g("ada_b")[l]))
        for w in range(2):
            vp.add(f"ln_g{l}_{w}", chunked(g("ln_g")[l, w]))
            vp.add(f"ln_b{l}_{w}", chunked(g("ln_b")[l, w]))
        vp.add(f"b1_{l}", chunked(g("mlp_b1")[l]))
        vp.add(f"b2_{l}", chunked(g("mlp_b2")[l]))
    p64 = perm2d(32)
    for i in range(2):
        cw = g("hy_conv_w")[i]
        hcw = cw.T.reshape(12, 128, 3).transpose(1, 0, 2).reshape(128, 36)
        vp.add(f"hcw{i}", hcw)
        vp.add(f"hcb{i}", chunked(g("hy_conv_b")[i]))
        vp.add(f"skip{i}_0", chunked(g("hy_skip")[i, 0]))
        vp.add(f"skip{i}_1", chunked(g("hy_skip")[i, 1]))
        vp.add(f"qg{i}", chunked(g("mla_q_norm_g")[i]))
        vp.add(f"kvg{i}", chunked(g("mla_kv_norm_g")[i]))
        vp.add(f"fb1_{i}", g("hy_filt_b1")[i][:, None])
        vp.add(f"fb2_{i}", g("hy_filt_b2")[i][:, None])
        vp.add(f"ffreq{i}", g("hy_sin_freq")[i][:, None])
        dw = g("cv_dw_w")[i]
        vp.add(f"dww{i}", dw.T.reshape(4, 128, 31).transpose(1, 0, 2).reshape(128, 124))
        vp.add(f"dwb{i}", chunked(g("cv_dw_b")[i]))
        vp.add(f"cvg{i}", chunked(g("cv_ln_g")[i]))
        vp.add(f"cvb{i}", chunked(g("cv_ln_b")[i]))
        gq = g("gqa_q_norm_g")[i]
        gk = g("gqa_k_norm_g")[i]
        vp.add(f"gq2{i}", np.concatenate([gq, gq])[:, None])
        vp.add(f"gqp2{i}", np.concatenate([gq[p64], gq[p64]])[:, None])
        vp.add(f"gk2{i}", np.concatenate([gk, gk])[:, None])
        vp.add(f"gkp2{i}", np.concatenate([gk[p64], gk[p64]])[:, None])
        vp.add(f"gq{i}", gq[:, None])
        vp.add(f"gqp{i}", gq[p64][:, None])
        vp.add(f"gk{i}", gk[:, None])
        vp.add(f"gkp{i}", gk[p64][:, None])
    return vp


def kernel(**inp):
    W = _prep_shared(inp)
    consts = {True: make_consts(True), False: make_consts(False)}
    xp = np.asarray(inp["x_prompt"], np.float32)
    xs = np.asarray(inp["x_sample"], np.float32)
    in_maps = []
    voff = None
    NV = None
    for core in range(8):
        sample = core in (4, 5)
        c, cvec = consts[sample]
        m = dict(W)
        for k_ in ("cft", "sft", "cf", "sf", "fnt", "gn", "zT", "maskA", "maskB", "rope_c", "rope_s"):
            m[k_] = c[k_]
        if sample:
            b = core - 4
            m["x_in"] = np.ascontiguousarray(xs[b])
            cond = np.asarray(inp["c"], np.float32)[b]
            m["c_ckv"] = np.ascontiguousarray(np.asarray(inp["cache_mla_ckv"], np.float32)[b])
            m["c_kr"] = np.ascontiguousarray(np.asarray(inp["cache_mla_krope"], np.float32)[b])
            m["c_gk"] = np.ascontiguousarray(np.asarray(inp["cache_gqa_k"], np.float32)[b].reshape(2, 512, 128))
            m["c_gv"] = np.ascontiguousarray(np.asarray(inp["cache_gqa_v"], np.float32)[b].reshape(2, 512, 128))
        else:
            pc = core if core < 4 else core - 6
            m["x_in"] = np.ascontiguousarray(xp[pc * 8:(pc + 1) * 8].reshape(T, DM))
            cond = np.asarray(inp["c_ctx"], np.float32)
            m["c_ckv"] = np.zeros((2, 512, 128), np.float32)
            m["c_kr"] = np.zeros((2, 512, 32), np.float32)
            m["c_gk"] = np.zeros((2, 512, 128), np.float32)
            m["c_gv"] = np.zeros((2, 512, 128), np.float32)
        vp = _vecs(inp, cond, cvec)
        m["vecs"] = vp.array()
        voff, NV = vp.off, vp.n
        in_maps.append(m)
    key = NV
    if key not in _CACHE:
        _CACHE[key] = build(voff, NV)
    nc = _CACHE[key]
    res = run_bass_kernel_spmd(nc, in_maps, core_ids=list(range(8)))
    r = res.results
    y_prompt = np.concatenate([r[c]["y_out"].reshape(8, 256, DM) for c in range(4)], axis=0)
    y_sample = np.stack([r[4]["y_out"], r[5]["y_out"]], axis=0)

    def st(name, last):
        outs = []
        for c in range(4):
            a = r[c][name]
            a = a.reshape(2, 8, 256, -1).transpose(1, 0, 2, 3)
            outs.append(a)
        a = np.concatenate(outs, axis=0)
        return np.ascontiguousarray(a.reshape((32, 2, 256) + last)).astype(np.float32)
    return (y_prompt.astype(np.float32), y_sample.astype(np.float32), st("o_ckv", (128,)), st("o_kr", (32,)),
            st("o_gk", (2, 64)), st("o_gv", (2, 64)))
```

```python
import math, bisect
from contextlib import ExitStack
from concourse.bass_utils import run_bass_kernel_spmd
import bisect
import numpy as np
import concourse.bass as bass
import concourse.mybir as mybir

F32 = mybir.dt.float32
BF16 = mybir.dt.bfloat16
AF = mybir.ActivationFunctionType
ALU = mybir.AluOpType


class Op:
    __slots__ = ("eng", "idx", "fn", "deps", "kind", "signal", "sem", "val")

    def __init__(self, eng, idx, fn, kind):
        self.eng = eng
        self.idx = idx
        self.fn = fn
        self.kind = kind
        self.deps = []
        self.signal = False
        self.sem = None
        self.val = 0


class Slot:
    def __init__(self, t, nfree, name, whole=False):
        self.t = t
        self.nfree = nfree
        self.name = name
        self.whole = whole
        self.los = [0]
        self.segs = [[0, nfree, None, {}]]

    def __getitem__(self, idx):
        return View(self, self.t[idx])

    def ap(self):
        return View(self, self.t.ap() if hasattr(self.t, "ap") else self.t[:])

    def touch(self, lo, hi, op, write, deps):
        assert 0 <= lo < hi <= self.nfree, (self.name, lo, hi, self.nfree)
        segs, los = self.segs, self.los
        i = bisect.bisect_right(los, lo) - 1
        if i < 0:
            i = 0
        out = []
        j = i
        n = len(segs)
        while j < n and segs[j][0] < hi:
            slo, shi, w, rd = segs[j]
            if shi <= lo:
                out.append(segs[j])
                j += 1
                continue
            if slo < lo:
                out.append([slo, lo, w, dict(rd)])
            a = max(slo, lo)
            b = min(shi, hi)
            if w is not None and w is not op:
                deps.add(w)
            if write:
                for r in rd.values():
                    if r is not op:
                        deps.add(r)
                if out and out[-1][2] is op and out[-1][1] == a and not out[-1][3]:
                    out[-1][1] = b
                else:
                    out.append([a, b, op, {}])
            else:
                rd2 = dict(rd)
                key = op.eng if op.kind == "c" else ("d", op.idx)
                rd2[key] = op
                out.append([a, b, w, rd2])
            if shi > hi:
                out.append([hi, shi, w, dict(rd)])
            j += 1
        segs[i:j] = out
        self.los[i:j] = [s[0] for s in out]


class Reg:
    def __init__(self, t, base, slot):
        self.t = t
        self.base = base
        self.slot = slot

    def __getitem__(self, idx):
        return View(self, self.t[idx])


class View:
    __slots__ = ("slot", "ap", "base")

    def __init__(self, slot, ap, base=0):
        if isinstance(slot, Reg):
            base = slot.base
            slot = slot.slot
        self.slot = slot
        self.ap = ap
        self.base = base

    def ranges(self):
        s = self.slot
        if s.whole:
            return [(0, s.nfree)]
        ap = self.ap
        es = 2 if ap.dtype == BF16 else 4
        dims = list(ap.ap)
        pstep = dims[0][0]
        off = ((ap.offset % pstep) * es if pstep > 0 else 0) + self.base
        fd = [(st * es, c) for st, c in dims[1:] if c > 1]
        if not fd:
            return [(off, off + es)]
        fd.sort(key=lambda x: -x[0])
        span = sum(st * (c - 1) for st, c in fd) + es
        if len(fd) >= 2:
            s0, c0 = fd[0]
            inner = sum(st * (c - 1) for st, c in fd[1:]) + es
            if inner < s0 and c0 <= 64:
                return [(off + i * s0, off + i * s0 + inner) for i in range(c0)]
        return [(off, off + span)]


ENG_NAMES = ["pe", "act", "dve", "pool", "sp"]


class Prog:
    def __init__(self, nc, es):
        self.nc = nc
        self.es = es
        self.ops = []
        self.streams = {e: [] for e in ENG_NAMES}
        self.nslots = 0

    SB_BASE = 16384
    SB_TOP = 229344

    def sbuf_at(self, name, shape, dt, offset):
        if not hasattr(self, "SB"):
            self.SB = Slot(None, 1 << 20, "SB")
            self.nreg = 0
        self.nreg += 1
        nbytes = int(np.prod(shape[1:])) * (2 if dt == BF16 else 4)
        assert offset >= self.SB_BASE and offset + nbytes <= self.SB_TOP, (name, offset, nbytes)
        t = self.nc.alloc_sbuf_tensor_at(f"{name}_{self.nreg}", list(shape), dt, offset=offset)
        return Reg(t, offset, self.SB)

    def sbuf(self, name, shape, dt):
        if not hasattr(self, "bump"):
            self.bump = self.SB_BASE
        nbytes = int(np.prod(shape[1:])) * (2 if dt == BF16 else 4)
        nbytes = (nbytes + 63) // 64 * 64
        off = self.bump
        self.bump += nbytes
        return self.sbuf_at(name, shape, dt, off)

    def psum(self, name, shape, dt=F32):
        t = self.es.enter_context(self.nc.psum_tensor(name, list(shape), dt))
        nbytes = int(np.prod(shape[1:])) * 4
        return Slot(t, nbytes, name)

    def dram(self, name, shape, dt, kind):
        t = self.nc.dram_tensor(name, list(shape), dt, kind=kind)
        return Slot(t, 1, name, whole=True)

    def op(self, eng, fn, reads, writes, kind="c"):
        o = Op(eng, len(self.ops), fn, kind)
        deps = set()
        for v in reads:
            for lo, hi in v.ranges():
                v.slot.touch(lo, hi, o, False, deps)
        for v in writes:
            for lo, hi in v.ranges():
                v.slot.touch(lo, hi, o, True, deps)
        dl = []
        for d in deps:
            if d.kind == "c" and d.eng == eng and eng == "pe":
                continue
            d.signal = True
            dl.append(d)
        dl.sort(key=lambda d: d.idx)
        o.deps = dl
        if kind == "d":
            o.signal = True
        self.ops.append(o)
        self.streams[eng].append(o)
        return o

    def mm(self, out, lhsT, rhs, start=True, stop=True, **kw):
        return self.op("pe", lambda e: e.matmul(out.ap, lhsT.ap, rhs.ap, start=start, stop=stop, **kw),
                       [lhsT, rhs], [out])

    def transpose(self, out, in_, ident):
        return self.op("pe", lambda e: e.transpose(out.ap, in_.ap, ident.ap), [in_, ident], [out])

    def act(self, out, in_, func, bias=None, scale=None, accum_out=None):
        reads = [in_]
        kw = {}
        if bias is not None:
            if isinstance(bias, View):
                reads.append(bias)
                kw["bias"] = bias.ap
            else:
                kw["bias"] = bias
        if scale is not None:
            if isinstance(scale, View):
                reads.append(scale)
                kw["scale"] = scale.ap
            else:
                kw["scale"] = scale
        writes = [out]
        if accum_out is not None:
            writes.append(accum_out)
            kw["accum_out"] = accum_out.ap
        return self.op("act", lambda e: e.activation(out.ap, in_.ap, func, **kw), reads, writes)

    def tt(self, eng, out, in0, in1, op):
        return self.op(eng, lambda e: e.tensor_tensor(out.ap, in0.ap, in1.ap, op), [in0, in1], [out])

    def ts(self, eng, out, in0, s1, op0, s2=None, op1=None):
        reads = [in0]
        a1 = s1
        if isinstance(s1, View):
            reads.append(s1)
            a1 = s1.ap
        a2 = s2
        if isinstance(s2, View):
            reads.append(s2)
            a2 = s2.ap
        if op1 is None:
            return self.op(eng, lambda e: e.tensor_scalar(out.ap, in0.ap, a1, None, op0), reads, [out])
        return self.op(eng, lambda e: e.tensor_scalar(out.ap, in0.ap, a1, a2, op0, op1), reads, [out])

    def stt(self, eng, out, in0, scalar, in1, op0, op1):
        reads = [in0, in1]
        a = scalar
        if isinstance(scalar, View):
            reads.append(scalar)
            a = scalar.ap
        return self.op(eng, lambda e: e.scalar_tensor_tensor(out.ap, in0.ap, a, in1.ap, op0, op1), reads, [out])

    def copy(self, eng, out, in_):
        if eng == "act":
            return self.act(out, in_, AF.Copy)
        return self.op(eng, lambda e: e.tensor_copy(out.ap, in_.ap), [in_], [out])

    def memset(self, eng, out, val):
        return self.op(eng, lambda e: e.memset(out.ap, val), [], [out])

    def recip(self, out, in_):
        return self.op("dve", lambda e: e.reciprocal(out.ap, in_.ap), [in_], [out])

    def dma(self, q, out, in_, **kw):
        return self.op(q, lambda e: e.dma_start(out=out.ap, in_=in_.ap, **kw), [in_], [out], kind="d")

    def emit(self):
        nc = self.nc
        es = self.es
        ROT = 30000
        NPOOL = 24
        sem_list = {}

        def new_sem(name):
            return es.enter_context(nc.semaphore(name))

        for e in ENG_NAMES:
            cnt = 0
            dcnt = 0
            cur = None
            pool = []
            for o in self.streams[e]:
                if o.kind == "c":
                    if not o.signal:
                        continue
                    if cur is None or cnt >= ROT:
                        cur = new_sem(f"s_{e}_{len(sem_list)}")
                        sem_list[id(cur)] = cur
                        cnt = 0
                    cnt += 1
                    o.sem = cur
                    o.val = cnt
                else:
                    k = dcnt % NPOOL
                    if k >= len(pool):
                        pool.append(new_sem(f"d_{e}_{k}"))
                    o.sem = pool[k]
                    o.val = 16 * (dcnt // NPOOL + 1)
                    dcnt += 1
        all_dma = [o for o in self.ops if o.kind == "d"]
        last_dma = {}
        for o in all_dma:
            last_dma[id(o.sem)] = o
        block = es.enter_context(nc.Block())

        def run_stream(e, h, final=False):
            seen = {}
            nwait = 0
            for o in self.streams[e]:
                needs = []
                for d in o.deps:
                    needs.append((d.sem, d.val))
                if o.kind == "d" and o.val > 16:
                    needs.append((o.sem, o.val - 16))
                for s, v in needs:
                    if seen.get(id(s), 0) >= v:
                        continue
                    h.wait_ge(s, v)
                    seen[id(s)] = v
                    nwait += 1
                ins = o.fn(h)
                if o.signal:
                    ins.then_inc(o.sem, 16 if o.kind == "d" else 1)
            if final:
                for o in last_dma.values():
                    if seen.get(id(o.sem), 0) < o.val:
                        h.wait_ge(o.sem, o.val)
            return nwait

        @block.tensor
        def _(h):
            run_stream("pe", h)

        @block.scalar
        def _(h):
            run_stream("act", h)

        @block.vector
        def _(h):
            run_stream("dve", h)

        @block.gpsimd
        def _(h):
            run_stream("pool", h)

        @block.sync
        def _(h):
            run_stream("sp", h, final=True)
NLAYERS = 4

T = 2048
DM = 1024
NT = 16
NB = 4
ALPHA = 8.0 ** 0.25
LN_EPS = 1e-5
RMS_EPS = 1e-6
NKEY = 2560
BIGC = 76800


class VecPack:
    def __init__(self):
        self.cols = []
        self.off = {}
        self.n = 0

    def add(self, name, a):
        a = np.asarray(a, np.float32)
        if a.shape[0] != 128:
            b = np.zeros((128,) + a.shape[1:], np.float32)
            b[: a.shape[0]] = a
            a = b
        a = a.reshape(128, -1)
        self.off[name] = (self.n, a.shape[1])
        self.cols.append(a)
        self.n += a.shape[1]

    def array(self):
        return np.ascontiguousarray(np.concatenate(self.cols, axis=1))


def chunked(v):
    v = np.asarray(v, np.float32)
    return np.ascontiguousarray(v.reshape(-1, 128).T)


def rope_perm(r):
    h = r // 2
    idx = np.concatenate([np.arange(h, r), np.arange(0, h)])
    sgn = np.concatenate([-np.ones(h), np.ones(h)]).astype(np.float32)
    return idx, sgn


def rope_tables(r_axis, sample):
    t = np.arange(T)
    row = (t // 64).astype(np.float64)
    col = (t % 64).astype(np.float64)
    inv = 10000.0 ** (-np.arange(0, r_axis, 2, dtype=np.float64) / r_axis)
    cs, sn = [], []
    for pos in (row, col):
        ang = pos[None, :] * inv[:, None]
        ang = np.concatenate([ang, ang], axis=0)
        _, sgn = rope_perm(r_axis)
        cs.append(np.cos(ang))
        sn.append(np.sin(ang) * sgn[:, None])
    c = np.concatenate(cs, 0)
    s = np.concatenate(sn, 0)
    if not sample:
        c = np.ones_like(c)
        s = np.zeros_like(s)
    return c.astype(np.float32), s.astype(np.float32)


def perm2d(r_axis):
    i1, _ = rope_perm(r_axis)
    return np.concatenate([i1, i1 + r_axis])


def make_consts(sample):
    import ml_dtypes
    L = 2048 if sample else 256
    t = np.arange(T)
    m = t % L
    seg = t // L
    c = {}
    fidx = np.arange(T)
    f = fidx % L
    fseg = fidx // L
    ang = np.pi * np.outer(m, f).astype(np.float64) / L
    same = (seg[:, None] == fseg[None, :])
    Cf = np.where(same, np.cos(ang), 0.0)
    Sf = np.where(same, np.sin(ang), 0.0)
    bf = ml_dtypes.bfloat16
    c["cf"] = np.ascontiguousarray(Cf.astype(np.float32).astype(bf))
    c["sf"] = np.ascontiguousarray(Sf.astype(np.float32).astype(bf))
    def tile_f(M):
        return np.ascontiguousarray(M.reshape(16, 128, 16, 128).transpose(2, 1, 0, 3).reshape(16, 128, 2048).astype(np.float32).astype(bf))
    c["cft"] = tile_f(Cf)
    c["sft"] = tile_f(Sf)
    FN = np.zeros((T, 8))
    FN[t, seg] = (-1.0) ** m
    c["fnt"] = np.ascontiguousarray(FN.reshape(16, 128, 8).transpose(1, 0, 2).astype(np.float32).astype(bf))
    c["gn"] = np.ascontiguousarray(FN.T.astype(np.float32).astype(bf))
    wf = np.where(f == 0, 1.0 / (2 * L), 1.0 / L)
    tpos = m / float(L)
    bands = np.arange(1, 17)
    a2 = 2 * np.pi * tpos[:, None] * bands[None, :]
    z = np.concatenate([tpos[:, None], np.cos(a2), np.sin(a2)], axis=1)
    c["zT"] = np.ascontiguousarray(z.T.astype(np.float32))
    mA = np.zeros((8, T), np.float32)
    mB = np.zeros((8, NKEY), np.float32)
    sg8 = t // 256
    mA[sg8, t] = 1.0
    if not sample:
        BIG = 30000.0
        mB[:, :512] = -BIG
        for s in range(8):
            mB[s, 512:] = np.where(sg8 == s, 0.0, -BIG)
    c["maskA"] = mA.astype(bf)
    c["maskB"] = mB.astype(bf)
    vec = {}
    vec["wf"] = chunked(wf)
    vec["negt"] = chunked(-tpos)
    vec["m0"] = chunked((m != 0).astype(np.float32))
    vec["wN"] = np.full((128, 1), 1.0 / (2 * L), np.float32)
    vec["flag"] = np.full((128, 1), 1.0 if sample else 0.0, np.float32)
    cq, sq = rope_tables(16, sample)
    cg, sg = rope_tables(32, sample)
    rc_e = np.ones((128, T), np.float32)
    rs_e = np.zeros((128, T), np.float32)
    rc_e[0:32] = cq
    rs_e[0:32] = sq
    rc_e[64:96] = cq
    rs_e[64:96] = sq
    rc_o = np.ones((128, T), np.float32)
    rs_o = np.zeros((128, T), np.float32)
    rc_o[0:64] = cg
    rs_o[0:64] = sg
    rc_o[64:128] = cg
    rs_o[64:128] = sg
    c["rope_c"] = np.stack([rc_e, rc_o]).astype(bf)
    c["rope_s"] = np.stack([rs_e, rs_o]).astype(bf)
    return c, vec

STOP = (3, 2)
DBGF = ()
EST = 9
OST = 9


def build(voff, NV, dbg=False):
    nc = bass.Bass("TRN2", target_bir_lowering=False)
    es = ExitStack()
    P = Prog(nc, es)
    D = {}

    def din(name, shape, dt=F32):
        D[name] = P.dram(name, shape, dt, "ExternalInput")
        return D[name]

    def dview(name, fn=None):
        s = D[name]
        a = s.t.ap()
        if fn is not None:
            a = fn(a)
        return View(s, a)

    def dbg(name, v, shape, dt):
        if "dbg" not in DBGF:
            return
        dr = P.dram("dbg_" + name, list(shape), dt, "ExternalOutput")
        P.dma("sp", View(dr, dr.t.ap()), v)

    din("x_in", [T, DM])
    din("vecs", [128, NV])
    din("ident", [128, 128])
    din("cft", [16, 128, 2048], BF16)
    din("sft", [16, 128, 2048], BF16)
    din("cf", [T, T], BF16)
    din("sf", [T, T], BF16)
    din("fnt", [128, 16, 8], BF16)
    din("gn", [8, T], BF16)
    din("zT", [33, T])
    din("maskA", [8, T], BF16)
    din("maskB", [8, NKEY], BF16)
    din("rope_c", [2, 128, T], BF16)
    din("rope_s", [2, 128, T], BF16)
    din("c_ckv", [2, 512, 128])
    din("c_kr", [2, 512, 32])
    din("c_gk", [2, 512, 128])
    din("c_gv", [2, 512, 128])
    din("ada_w", [4, DM, 6144])
    din("ev_w_in", [2, DM, 1984])
    din("ev_w_out", [2, DM, DM])
    din("w_uq", [2, 256, 8 * 192])
    din("w_ukv", [2, 128, 1024])
    din("f_w1", [2, 33, 64])
    din("f_w2", [2, 64, 64])
    din("f_w3", [2, 64, 2048])
    din("decay_bc", [2, 128, 2048])
    din("od_w_in", [2, DM, 2432])
    din("od_w_out", [2, DM, DM])
    din("mlp_w1", [4, DM, 4096])
    din("mlp_w2", [4, 4096, DM])
    y_out = P.dram("y_out", [T, DM], F32, "ExternalOutput")
    o_ckv = P.dram("o_ckv", [2, T, 128], F32, "ExternalOutput")
    o_kr = P.dram("o_kr", [2, T, 32], F32, "ExternalOutput")
    o_gk = P.dram("o_gk", [2, T, 128], F32, "ExternalOutput")
    o_gv = P.dram("o_gv", [2, T, 128], F32, "ExternalOutput")
    xscr = P.dram("xscr", [128, 8 * T], F32, "Internal")

    BIG = P.sbuf("BIG", [128, BIGC], BF16)
    VEC = P.sbuf("VEC", [128, NV], F32)
    S32 = P.sbuf("S32", [128, 3584], F32)
    WR = P.sbuf("WR", [128, 3, 4096], BF16)
    ROPEC = P.sbuf("ROPEC", [128, T], BF16)
    ROPES = P.sbuf("ROPES", [128, T], BF16)
    CST = P.sbuf("CST", [128, 512], F32)
    CSTB = P.sbuf("CSTB", [128, 512], BF16)
    MODT = P.sbuf("MODT", [128, 128], F32)
    SMALL = P.sbuf("SMALL", [128, 1200], BF16)
    PS = [P.psum(f"ps{i}", [128, 512]) for i in range(8)]
    st = {"pb": 0, "wr": 0}

    def nb():
        b = PS[st["pb"] % st.get("nbm", 8)]
        st["pb"] += 1
        return b

    def bg(off, n, rows=128, r0=0):
        return View(BIG, BIG.t[r0:r0 + rows, off:off + n])

    def bgf(off, n, rows=128, r0=0):
        assert off + 2 * n <= BIGC
        rg = P.sbuf_at("bgf", [128, n], F32, BIG.base + off * 2)
        return View(rg, rg.t[r0:r0 + rows, :])

    def s32(off, n, rows=128, r0=0):
        return View(S32, S32.t[r0:r0 + rows, off:off + n])

    def vec(name, c0=0, n=1, rows=128, r0=0):
        o, w = voff[name]
        return View(VEC, VEC.t[r0:r0 + rows, o + c0:o + c0 + n])

    def R(v, pat, **kw):
        return View(v.slot, v.ap.rearrange(pat, **kw), v.base)

    def sub(v, idx):
        return View(v.slot, v.ap[idx], v.base)

    def wtile(src_view):
        k = st["wr"] % 3
        st["wr"] += 1
        return k

    XT = bgf(0, 8 * T)
    XT3 = R(XT, "p (c t) -> p c t", c=8)

    P.dma("sp", View(VEC, VEC.t[:, :]), dview("vecs"))
    IDF = View(CST, CST.t[:, 0:128])
    P.dma("sp", IDF, dview("ident"))
    IDB = View(CSTB, CSTB.t[:, 0:128])
    P.copy("act", IDB, IDF)
    ONESB = View(CSTB, CSTB.t[:, 128:256])
    P.memset("dve", ONESB, 1.0 / 1024)
    ONES64 = View(CST, CST.t[0:64, 128:192])
    P.memset("dve", ONES64, 1.0)
    ONES128 = View(CST, CST.t[:, 256:384])
    P.memset("dve", ONES128, 1.0)
    E65 = View(CST, CST.t[0:65, 192:256])
    P.memset("dve", View(CST, CST.t[:, 384:512]), 0.0)
    P.memset("dve", View(CST, CST.t[0:64, 384:448]), 1.0)
    P.memset("dve", View(CST, CST.t[64:128, 448:512]), 1.0)
    P.memset("dve", View(CST, CST.t[0:64, 192:256]), 0.0)
    P.memset("dve", View(CST, CST.t[64:65, 192:256]), 1.0)
    FNT = View(SMALL, SMALL.t[:, 0:128])
    if "nofnt" not in DBGF:
        P.dma("sp", FNT, dview("fnt", lambda a: a.rearrange("p k s -> p (k s)")))
    FNT3 = R(FNT, "p (k s) -> p k s", k=16)
    GN = View(SMALL, SMALL.t[0:8, 128:128 + 512])
    SBF = View(SMALL, SMALL.t[:, 640:648])
    if "nosilu" not in DBGF:
        P.act(SBF, vec("cond", 0, 8), AF.Silu)
    YN = View(SMALL, SMALL.t[0:8, 656:656 + 512])
    CTMP = s32(0, 8 * 258)
    CT3 = R(CTMP, "p (s j) -> p s j", s=8)
    if "noctmp" not in DBGF:
        P.memset("pool", CTMP, 0.0)

    for tt in range(0 if 'noload' in DBGF else NT):
        stg = s32(2100, 1024) if tt % 2 else s32(0, 1024)
        P.dma("sp", stg, dview("x_in", lambda a: a[tt * 128:(tt + 1) * 128, :]))
        for half in range(2):
            pb = nb()
            for j in range(4):
                c = half * 4 + j
                P.transpose(pb[:, j * 128:(j + 1) * 128], sub(stg, (slice(None), slice(c * 128, (c + 1) * 128))), IDF)
            P.copy("dve" if half else "act",
                   sub(XT3, (slice(None), slice(half * 4, half * 4 + 4), slice(tt * 128, (tt + 1) * 128))),
                   R(pb[:, :], "p (c t) -> p c t", c=4))

    def load_w(dname, li, r0, nrow_chunks, c0, ncols, rows=128):
        k = st["wr"] % 3
        st["wr"] += 1
        dst = View(WR, WR.t[0:rows, k, 0:nrow_chunks * ncols].rearrange("p (k n) -> p k n", k=nrow_chunks))
        src = dview(dname, lambda a: a[li, r0:r0 + nrow_chunks * rows, c0:c0 + ncols].rearrange("(k p) n -> p k n", p=rows))
        P.dma("pool", dst, src)
        return dst

    MOD = View(MODT, MODT.t[:, 0:48])
    SCP = View(MODT, MODT.t[:, 48:64])
    FB = View(MODT, MODT.t[0:64, 64:68])

    MODN = View(MODT, MODT.t[:, 68:116])

    def adaln_tiles(l):
        pb = PS[7]
        for g in range(12):
            wt = load_w("ada_w", l, 0, 8, g * 512, 512)
            yield
            adaln_mm(pb, wt, g)
            yield
        P.tt("dve", MODN, pb[:, 0:48], vec(f"ada_b{l}", 0, 48), ALU.add)
        yield

    def adaln_mm(pb, wt, g):
        for j in range(4):
            jc = g * 4 + j
            for kc in range(8):
                P.mm(pb[:, jc:jc + 1], sub(wt, (slice(None), kc, slice(j * 128, (j + 1) * 128))),
                     sub(SBF, (slice(None), slice(kc, kc + 1))), start=(kc == 0), stop=(kc == 7))

    def adaln_apply():
        P.copy("dve", MOD, MODN)
        P.ts("dve", sub(SCP, (slice(None), slice(0, 8))), sub(MOD, (slice(None), slice(8, 16))), 1.0, ALU.add)
        P.ts("dve", sub(SCP, (slice(None), slice(8, 16))), sub(MOD, (slice(None), slice(32, 40))), 1.0, ALU.add)

    def modcol(k, c):
        return sub(MOD, (slice(None), slice(k * 8 + c, k * 8 + c + 1)))

    def layer_norm(l, which):
        for tb in range(NB):
            par = tb % 2
            rb = R(bg(32768 + par * 8192, 4096), "p (c t) -> p c t", c=8)
            rb2 = R(bg(36864 + par * 8192, 4096), "p (c t) -> p c t", c=8)
            mean_sb = s32(par * 1536, 512)
            m2 = s32(par * 1536 + 512, 512)
            rstd = s32(par * 1536 + 1024, 512)
            ts_ = slice(tb * 512, (tb + 1) * 512)
            for c in range(8):
                P.act(sub(rb, (slice(None), c, slice(None))), sub(XT3, (slice(None), c, ts_)), AF.Copy)
                P.act(sub(rb2, (slice(None), c, slice(None))), sub(XT3, (slice(None), c, ts_)), AF.Square)
            pm = nb()
            pq = nb()
            for c in range(8):
                P.mm(pm[:, :], ONESB, sub(rb, (slice(None), c, slice(None))), start=(c == 0), stop=(c == 7))
            for c in range(8):
                P.mm(pq[:, :], ONESB, sub(rb2, (slice(None), c, slice(None))), start=(c == 0), stop=(c == 7))
            P.copy("act", mean_sb, pm[:, :])
            P.tt("pool", m2, mean_sb, mean_sb, ALU.mult)
            P.ts("pool", m2, m2, -LN_EPS, ALU.add)
            P.tt("dve", rstd, pq[:, :], m2, ALU.subtract)
            P.act(rstd, rstd, AF.Sqrt)
            P.recip(rstd, rstd)
            for c in range(8):
                xv = sub(XT3, (slice(None), c, ts_))
                P.tt("dve", xv, xv, mean_sb, ALU.subtract)
                P.tt("dve", xv, xv, rstd, ALU.mult)
                P.act(xv, xv, AF.Identity, bias=vec(f"ln_b{l}_{which}", c, 1), scale=vec(f"ln_g{l}_{which}", c, 1))

    def modulate_full(dst3, kmod, bounce=False):
        for c in range(8):
            P.act(sub(dst3, (slice(None), c, slice(None))), sub(XT3, (slice(None), c, slice(None))), AF.Identity,
                  bias=modcol(0 if kmod == 0 else 3, c), scale=sub(SCP, (slice(None), slice(kmod * 8 + c, kmod * 8 + c + 1))))
            if bounce:
                P.dma("sp", View(xscr, xscr.t.ap()[:, c * T:(c + 1) * T]), sub(XT3, (slice(None), c, slice(None))))

    def bounce_out():
        for c in range(8):
            P.dma("sp", View(xscr, xscr.t.ap()[:, c * T:(c + 1) * T]), sub(XT3, (slice(None), c, slice(None))))

    def bounce_in():
        for c in range(8):
            P.dma("sp", sub(XT3, (slice(None), c, slice(None))), View(xscr, xscr.t.ap()[:, c * T:(c + 1) * T]))

    def residual(c, tb, pb, gk):
        xv = sub(XT3, (slice(None), c, slice(tb * 512, (tb + 1) * 512)))
        P.act(xv, xv, AF.Copy, scale=ALPHA)
        P.stt("dve", xv, pb, modcol(gk, c), xv, ALU.mult, ALU.add)

    def mlp(l, bg_it=None):
        st["nbm"] = 7

        def tick():
            if bg_it is not None:
                next(bg_it, None)
        HB = R(bg(32768, 8192), "p (c t) -> p c t", c=8)
        HID = R(bg(40960, 32768), "p (c t) -> p c t", c=32)
        tmp = s32(0, 512)
        for hb in range(2):
            t0 = hb * 1024
            for c in range(8):
                P.act(sub(HB, (slice(None), c, slice(None))), sub(XT3, (slice(None), c, slice(t0, t0 + 1024))), AF.Identity,
                      bias=modcol(3, c), scale=sub(SCP, (slice(None), slice(8 + c, 9 + c))))
            for g in range(8):
                wt = load_w("mlp_w1", l, 0, 8, g * 512, 512)
                for j in range(4):
                    hc = g * 4 + j
                    for t2 in range(2):
                        pb = nb()
                        for kc in range(8):
                            P.mm(pb[:, :], sub(wt, (slice(None), kc, slice(j * 128, (j + 1) * 128))),
                                 sub(HB, (slice(None), kc, slice(t2 * 512, (t2 + 1) * 512))), start=(kc == 0), stop=(kc == 7))
                        tm = s32(((hc * 2 + t2) % 2) * 512, 512)
                        P.act(tm, pb[:, :], AF.Relu, bias=vec(f"b1_{l}", hc, 1))
                        P.tt("dve", sub(HID, (slice(None), hc, slice(t2 * 512, (t2 + 1) * 512))), tm, tm, ALU.mult)
                if j == 3:
                    tick()
            for c in range(8):
                wt = load_w("mlp_w2", l, 0, 32, c * 128, 128)
                for t2 in range(2):
                    pb = nb()
                    for kc in range(32):
                        P.mm(pb[:, :], sub(wt, (slice(None), kc, slice(None))),
                             sub(HID, (slice(None), kc, slice(t2 * 512, (t2 + 1) * 512))), start=(kc == 0), stop=(kc == 31))
                    tm = s32(1024 + t2 * 512, 512)
                    P.act(tm, pb[:, :], AF.Identity, bias=vec(f"b2_{l}", c, 1))
                    residual(c, hb * 2 + t2, tm, 5)
                tick()
        layer_norm(l, 1)
        st["nbm"] = 8

    def attention(KT, QT, krows, Vfn, scale, yout_fn, osb_off=0, bg_it=None, bg_every=4):
        PT = [bg(73728 + i * 512, 512) for i in range(4)]
        pending = []

        def finish(qb, po, par):
            o0 = osb_off + par * 512
            osb = s32(o0, 512, rows=65)
            P.copy("act", osb, po[0:65, :])
            P.recip(s32(o0, 512, rows=1, r0=64), s32(o0, 512, rows=1, r0=64))
            pbc = PS[6 + par]
            P.mm(pbc[0:64, :], E65, osb)
            P.tt("dve", yout_fn(qb), s32(o0, 512, rows=64), pbc[0:64, :], ALU.mult)

        for qb in range(NB):
            qs = slice(qb * 512, (qb + 1) * 512)
            st["att"] = st.get("att", 0) + 1
            par = st["att"] % 2
            po = PS[4 + par]
            sc_b = {}
            LOOK = 3
            nk = NKEY // 128

            def issue_s(kt):
                pb = PS[kt % 4]
                P.mm(pb[:, :], sub(KT, (slice(0, krows), slice(kt * 128, (kt + 1) * 128))), sub(QT, (slice(0, krows), qs)))
                sc_b[kt] = pb
            for kt in range(min(LOOK, nk)):
                issue_s(kt)
            for kt in range(nk):
                if kt + LOOK < nk:
                    issue_s(kt + LOOK)
                pt = PT[kt % 4]
                P.act(pt, sc_b.pop(kt)[:, :], AF.Exp, scale=scale)
                P.mm(po[0:65, :], Vfn(kt), pt, start=(kt == 0), stop=(kt == nk - 1))
                if kt == 9 and pending:
                    pending.pop(0)()
                if bg_it is not None and kt % bg_every == 0:
                    next(bg_it, None)
            while pending:
                pending.pop(0)()
            pending.append(lambda qb=qb, po=po, par=par: finish(qb, po, par))
        while pending:
            pending.pop(0)()

    def outproj(dname, i, l, Z3, YATT):
        bounce_in()
        for c in range(8):
            wa = load_w(dname, i, 0, 4, c * 128, 128)
            wb = load_w(dname, i, 512, 8, c * 128, 128, rows=64)
            for tb in range(NB):
                ts_ = slice(tb * 512, (tb + 1) * 512)
                pb = nb()
                for kc in range(4):
                    P.mm(pb[:, :], sub(wa, (slice(None), kc, slice(None))), sub(Z3, (slice(None), kc, ts_)), start=(kc == 0), stop=False)
                for h in range(8):
                    P.mm(pb[:, :], sub(wb, (slice(None), h, slice(None))), sub(YATT(h), (slice(None), ts_)), start=False, stop=(h == 7))
                residual(c, tb, pb[:, :], 2)
        layer_norm(l, 0)

    O_HT = 32768
    O_Z2 = 49152
    O_KR = 57344
    O_FILT = 59904
    O_X1 = 0
    O_VT = 8192
    O_VTOK = 16384
    O_CQ = 24576
    O_CKV = 28672
    O_HID2 = 31232
    HT3 = R(bg(O_HT, 16384), "p (c t) -> p c t", c=8)

    def conv3_chunk(i, j, pbs, dest3, c):
        for tb in range(NB):
            P.copy("act", sub(CT3, (slice(None), slice(2 * tb, 2 * tb + 2), slice(1, 257))), R(pbs[tb][:, :], "p (s j) -> p s j", s=2))
        P.ts("dve", sub(CT3, (slice(None), slice(1, 8), slice(0, 1))), sub(CT3, (slice(None), slice(0, 7), slice(256, 257))), vec("flag"), ALU.mult)
        P.ts("dve", sub(CT3, (slice(None), slice(0, 7), slice(257, 258))), sub(CT3, (slice(None), slice(1, 8), slice(1, 2))), vec("flag"), ALU.mult)
        for tb in range(NB):
            pv = R(pbs[tb][:, :], "p (s j) -> p s j", s=2)
            P.act(pbs[tb][:, :], pbs[tb][:, :], AF.Identity, bias=vec(f"hcb{i}", j, 1), scale=vec(f"hcw{i}", j * 3 + 1, 1))
            P.stt("dve", pv, sub(CT3, (slice(None), slice(2 * tb, 2 * tb + 2), slice(0, 256))), vec(f"hcw{i}", j * 3 + 0, 1), pv, ALU.mult, ALU.add)
            dv = R(sub(dest3, (slice(None), c, slice(tb * 512, (tb + 1) * 512))), "p (s j) -> p s j", s=2)
            P.stt("dve", dv, sub(CT3, (slice(None), slice(2 * tb, 2 * tb + 2), slice(2, 258))), vec(f"hcw{i}", j * 3 + 2, 1), pv, ALU.mult, ALU.add)

    def to_tokmajor(srcT3, dst3):
        for kt in range(NT):
            pb = nb()
            pv = View(pb, pb.t[:, 0:256].bitcast(BF16))
            for c in range(4):
                P.transpose(sub(pv, (slice(None), slice(c * 128, (c + 1) * 128))), sub(srcT3, (slice(None), c, slice(kt * 128, (kt + 1) * 128))), IDB)
            P.copy("act" if kt % 2 else "dve", sub(dst3, (slice(None), kt, slice(None))), pv)

    def rms_rstd(dst, psum_sumsq, n, rows=128):
        P.ts("dve", dst, psum_sumsq, 1.0 / n, ALU.mult, RMS_EPS, ALU.add)
        P.act(dst, dst, AF.Sqrt)
        P.recip(dst, dst)

    def even_mixer(i, l):
        P.dma("sp", View(ROPEC, ROPEC.t[:, :]), dview("rope_c", lambda a: a[0]))
        P.dma("sp", View(ROPES, ROPES.t[:, :]), dview("rope_s", lambda a: a[0]))
        modulate_full(HT3, 0, bounce=True)
        X1 = R(bg(O_X1, 8192), "p (c t) -> p c t", c=4)
        X2 = R(bg(O_Z2, 8192), "p (c t) -> p c t", c=4)
        VT = R(bg(O_VT, 8192), "p (c t) -> p c t", c=4)
        VTOK = R(bg(O_VTOK, 8192), "p (k c) -> p k c", k=16)
        CQ = R(bg(O_CQ, 4096), "p (c t) -> p c t", c=2)
        CKV = bg(O_CKV, NKEY)
        KR = bg(O_KR, NKEY, rows=32)
        dests = [X1, X2, VT]
        for g in range(3):
            wt = load_w("ev_w_in", i, 0, 8, g * 512, 512)
            for jj in range(4):
                j = g * 4 + jj
                pbs = [nb() for _ in range(NB)]
                for tb in range(NB):
                    for kc in range(8):
                        P.mm(pbs[tb][:, :], sub(wt, (slice(None), kc, slice(jj * 128, (jj + 1) * 128))),
                             sub(HT3, (slice(None), kc, slice(tb * 512, (tb + 1) * 512))), start=(kc == 0), stop=(kc == 7))
                conv3_chunk(i, j, pbs, dests[g], jj)
        if EST <= 1:
            bounce_in()
            return
        wt = load_w("ev_w_in", i, 0, 8, 1536, 448)
        sq = s32(2100, 512)
        for tb in range(NB):
            ts_ = slice(tb * 512, (tb + 1) * 512)
            pss = nb()
            pcs = []
            for c in range(2):
                pb = nb()
                pcs.append(pb)
                for kc in range(8):
                    P.mm(pb[:, :], sub(wt, (slice(None), kc, slice(c * 128, (c + 1) * 128))), sub(HT3, (slice(None), kc, ts_)), start=(kc == 0), stop=(kc == 7))
                P.act(sq, pb[:, :], AF.Square)
                P.mm(pss[:, :], ONES128, sq, start=(c == 0), stop=(c == 1))
            rst = s32(2612, 512)
            rms_rstd(rst, pss[:, :], 256)
            for c in range(2):
                P.stt("dve", sub(CQ, (slice(None), c, ts_)), pcs[c][:, :], vec(f"qg{i}", c, 1), rst, ALU.mult, ALU.mult)
            pb = nb()
            for kc in range(8):
                P.mm(pb[:, :], sub(wt, (slice(None), kc, slice(256, 384))), sub(HT3, (slice(None), kc, ts_)), start=(kc == 0), stop=(kc == 7))
            P.act(sq, pb[:, :], AF.Square)
            pss2 = nb()
            P.mm(pss2[:, :], ONES128, sq)
            rms_rstd(rst, pss2[:, :], 128)
            ckf = s32(0, 512)
            P.stt("dve", ckf, pb[:, :], vec(f"kvg{i}", 0, 1), rst, ALU.mult, ALU.mult)
            P.copy("act", sub(CKV, (slice(None), slice(512 + tb * 512, 1024 + tb * 512))), ckf)
            if "nockvout" in DBGF:
                continue
            pt = nb()
            for q in range(4):
                P.transpose(pt[:, q * 128:(q + 1) * 128], sub(ckf, (slice(None), slice(q * 128, (q + 1) * 128))), IDF)
            ost = s32(512, 512)
            P.copy("act", ost, pt[:, :])
            P.dma("sp", View(o_ckv, o_ckv.t.ap()[i, tb * 512:(tb + 1) * 512, :].rearrange("(q p) r -> p q r", p=128)),
                  R(ost, "p (q r) -> p q r", q=4))
            if "nokr" in DBGF:
                continue
            pk = nb()
            pkp = nb()
            for kc in range(8):
                P.mm(pk[0:32, :], sub(wt, (slice(None), kc, slice(384, 416))), sub(HT3, (slice(None), kc, ts_)), start=(kc == 0), stop=(kc == 7))
            for kc in range(8):
                P.mm(pkp[0:32, :], sub(wt, (slice(None), kc, slice(416, 448))), sub(HT3, (slice(None), kc, ts_)), start=(kc == 0), stop=(kc == 7))
            k1 = s32(1024, 512, rows=32)
            k2 = s32(1536, 512, rows=32)
            P.tt("dve", k1, pk[0:32, :], View(ROPEC, ROPEC.t[0:32, ts_]), ALU.mult)
            P.tt("dve", k2, pkp[0:32, :], View(ROPES, ROPES.t[0:32, ts_]), ALU.mult)
            P.tt("pool", k1, k1, k2, ALU.add)
            P.copy("act", sub(KR, (slice(None), slice(512 + tb * 512, 1024 + tb * 512))), k1)
            if "nokrout" in DBGF:
                continue
            pt2 = nb()
            for q in range(4):
                P.mm(pt2[:, q * 32:(q + 1) * 32], sub(k1, (slice(None), slice(q * 128, (q + 1) * 128))), View(CST, CST.t[0:32, 0:32]))
            ost2 = s32(2100, 128)
            P.copy("act", ost2, pt2[:, 0:128])
            P.dma("sp", View(o_kr, o_kr.t.ap()[i, tb * 512:(tb + 1) * 512, :].rearrange("(q p) r -> p q r", p=128)),
                  R(ost2, "p (q r) -> p q r", q=4))
        if EST <= 2:
            bounce_in()
            return
        for q in range(4):
            stg = s32(0, 128)
            P.dma("sp", stg, dview("c_ckv", lambda a: a[i, q * 128:(q + 1) * 128, :]))
            stg2 = s32(128, 32)
            P.dma("sp", stg2, dview("c_kr", lambda a: a[i, q * 128:(q + 1) * 128, :]))
            pb = nb()
            P.transpose(pb[:, 0:128], stg, IDF)
            P.copy("act", sub(CKV, (slice(None), slice(q * 128, (q + 1) * 128))), pb[:, 0:128])
            if "noctxkr" not in DBGF:
                pb2 = nb()
                P.mm(pb2[0:32, 0:128], stg2, IDF)
                P.copy("act", sub(KR, (slice(None), slice(q * 128, (q + 1) * 128))), pb2[0:32, 0:128])
        if EST <= 3:
            bounce_in()
            return
        if i == 0:
            dbg("x1", bg(O_X1, 8192), [128, 8192], BF16)
            dbg("x2", bg(O_Z2, 8192), [128, 8192], BF16)
            dbg("v", bg(O_VT, 8192), [128, 8192], BF16)
            dbg("cq", bg(O_CQ, 4096), [128, 4096], BF16)
        hyena(i, X1, X2, VT, VTOK)
        if i == 0:
            dbg("y1", bg(O_VT, 8192), [128, 8192], BF16)
            dbg("z2", bg(O_Z2, 8192), [128, 8192], BF16)
        if EST <= 4:
            bounce_in()
            return
        mla(i, CQ, CKV, KR)
        if i == 0:
            dbg("yatt", bg(O_HT, 16384, rows=64), [64, 16384], BF16)
            dbg("vh", bg(12288 + 1300, 1300), [128, 1300], BF16)
            dbg("kt", bg(NKEY, NKEY, rows=104), [104, NKEY], BF16)
            dbg("qt", bg(8192 + T, T, rows=104), [104, T], BF16)
        if EST <= 5:
            bounce_in()
            return
        YATT = lambda h: bg(O_HT + h * T, T, rows=64)
        outproj("ev_w_out", i, l, X2, YATT)

    def hyena(i, X1, X2, VT, VTOK):
        FILT = R(bg(O_FILT, 16384), "p (k d c) -> p k d c", k=16, d=2)
        YR = R(bg(O_HT, 8192), "p (f c) -> p f c", f=16)
        YI = R(bg(O_HT + 8192, 8192), "p (f c) -> p f c", f=16)

        h1 = bgf(O_HT, 2048, rows=64)
        h2 = bgf(O_HT + 4096, 2048, rows=64)
        tmpf = bgf(O_HT + 8192, 2048, rows=64)
        zT = bgf(O_HT + 12288, 2048, rows=33)
        P.dma("sp", zT, dview("zT"))
        w1 = s32(2100, 64, rows=33)
        P.dma("sp", w1, dview("f_w1", lambda a: a[i]))
        w2 = s32(2164, 64, rows=64)
        P.dma("sp", w2, dview("f_w2", lambda a: a[i]))
        P.tt("dve", sub(FB, (slice(None), slice(0, 1))), vec(f"fb1_{i}", 0, 1, rows=64), vec(f"ffreq{i}", 0, 1, rows=64), ALU.mult)
        P.tt("dve", sub(FB, (slice(None), slice(1, 2))), vec(f"fb2_{i}", 0, 1, rows=64), vec(f"ffreq{i}", 0, 1, rows=64), ALU.mult)

        def sin_layer(dst, wv, src, krows, bcol):
            for tb in range(NB):
                ts_ = slice(tb * 512, (tb + 1) * 512)
                pb = nb()
                P.mm(pb[0:64, :], wv, sub(src, (slice(0, krows), ts_)))
                a = sub(dst, (slice(None), ts_))
                tq = sub(tmpf, (slice(None), ts_))
                P.act(a, pb[0:64, :], AF.Identity, bias=sub(FB, (slice(None), slice(bcol, bcol + 1))), scale=vec(f"ffreq{i}", 0, 1, rows=64))
                MG = 12582912.0
                P.ts("dve", tq, a, 1.0 / (2 * math.pi), ALU.mult, MG, ALU.add)
                P.ts("dve", tq, tq, MG, ALU.subtract, -2 * math.pi, ALU.mult)
                P.tt("dve", a, a, tq, ALU.add)
                P.ts("dve", a, a, 3.141592, ALU.min, -3.141592, ALU.max)
                P.act(a, a, AF.Sin)
        sin_layer(h1, w1, zT, 33, 0)
        sin_layer(h2, w2, h1, 64, 1)
        HB2 = View(SMALL, SMALL.t[0:64, 0:1])
        H2B = View(BIG, BIG.t[64:128, O_KR:O_KR + T])
        P.copy("dve", H2B, h2)
        to_tokmajor(VT, VTOK)
        for n in range(2):
            k = st["wr"] % 3
            st["wr"] += 1
            W3 = View(WR, WR.t[64:128, k, 0:1024])
            P.dma("pool", sub(W3, (slice(None), slice(0, 512))), dview("f_w3", lambda a: a[i, :, n * 512:(n + 1) * 512]))
            P.dma("pool", sub(W3, (slice(None), slice(512, 1024))), dview("f_w3", lambda a: a[i, :, 1024 + n * 512:1024 + (n + 1) * 512]))
            AD = s32(2100, 1024)
            P.dma("sp", sub(AD, (slice(None), slice(0, 512))), dview("decay_bc", lambda a: a[i, :, n * 512:(n + 1) * 512]))
            P.dma("sp", sub(AD, (slice(None), slice(512, 1024))), dview("decay_bc", lambda a: a[i, :, 1024 + n * 512:1024 + (n + 1) * 512]))
            P.act(AD, AD, AF.Abs)
            for kt in range(NT):
                pf = nb()
                pbk = nb()
                P.mm(pf[:, :], sub(H2B, (slice(None), slice(kt * 128, (kt + 1) * 128))), sub(W3, (slice(None), slice(0, 512))))
                P.mm(pbk[:, :], sub(H2B, (slice(None), slice(kt * 128, (kt + 1) * 128))), sub(W3, (slice(None), slice(512, 1024))))
                o_ = (kt % 2) * 1024
                w1 = s32(o_, 512)
                w2 = s32(o_ + 512, 512)
                P.act(w1, sub(AD, (slice(None), slice(0, 512))), AF.Exp, scale=vec("negt", kt, 1))
                P.act(w2, sub(AD, (slice(None), slice(512, 1024))), AF.Exp, scale=vec("negt", kt, 1))
                P.tt("dve", w1, pf[:, :], w1, ALU.mult)
                P.stt("dve", w2, pbk[:, :], vec("m0", kt, 1), w2, ALU.mult, ALU.mult)
                P.tt("pool", sub(FILT, (slice(None), kt, 0, slice(None))), w1, w2, ALU.add)
                P.tt("pool", sub(FILT, (slice(None), kt, 1, slice(None))), w2, w1, ALU.subtract)
            SRC = VTOK
            if n == 0 and i == 0:
                dbg("filt", bg(O_FILT, 16384), [128, 16384], BF16)
                dbg("h2b", View(BIG, BIG.t[64:128, O_KR:O_KR + T]), [64, T], BF16)
                dbg("vtok", bg(O_VTOK, 8192), [128, 8192], BF16)
            pn1 = nb()
            pn2 = nb()
            for kt in range(NT):
                P.mm(pn1[0:8, :], sub(FNT3, (slice(None), kt, slice(None))), sub(SRC, (slice(None), kt, slice(None))), start=(kt == 0), stop=(kt == NT - 1))
            for kt in range(NT):
                P.mm(pn2[0:8, :], sub(FNT3, (slice(None), kt, slice(None))), sub(FILT, (slice(None), kt, 0, slice(None))), start=(kt == 0), stop=(kt == NT - 1))
            tn = s32(3072, 512, rows=8)
            P.act(tn, pn2[0:8, :], AF.Identity, scale=vec("wN", 0, 1, rows=8))
            P.tt("dve", YN, pn1[0:8, :], tn, ALU.mult)
            for fc in range(16):
                k = st["wr"] % 3
                st["wr"] += 1
                CT = View(WR, WR.t[:, k, 0:2048].rearrange("p (k j) -> p k j", k=16))
                ST = View(WR, WR.t[:, k, 2048:4096].rearrange("p (k j) -> p k j", k=16))
                P.dma("sp", CT, dview("cft", lambda a: a[fc].rearrange("p (k j) -> p k j", k=16)))
                P.dma("sp", ST, dview("sft", lambda a: a[fc].rearrange("p (k j) -> p k j", k=16)))
                pa, pp, pbb, pq = nb(), nb(), nb(), nb()
                for kt in range(NT):
                    fl = (kt == 0)
                    ll = (kt == NT - 1)
                    P.mm(pa[:, :], sub(CT, (slice(None), kt, slice(None))), sub(SRC, (slice(None), kt, slice(None))), start=fl, stop=ll)
                    P.mm(pp[:, :], sub(CT, (slice(None), kt, slice(None))), sub(FILT, (slice(None), kt, 0, slice(None))), start=fl, stop=ll)
                    P.mm(pbb[:, :], sub(ST, (slice(None), kt, slice(None))), sub(SRC, (slice(None), kt, slice(None))), start=fl, stop=ll)
                    P.mm(pq[:, :], sub(ST, (slice(None), kt, slice(None))), sub(FILT, (slice(None), kt, 1, slice(None))), start=fl, stop=ll)
                p1 = s32(0, 512)
                q1 = s32(512, 512)
                P.act(p1, pp[:, :], AF.Identity, scale=vec("wf", fc, 1))
                P.act(q1, pq[:, :], AF.Identity, scale=vec("wf", fc, 1))
                t1, t2, t3, t4 = s32(1024, 512), s32(1536, 512), s32(2100, 512), s32(2612, 512)
                P.tt("dve", t1, pa[:, :], p1, ALU.mult)
                P.tt("dve", t2, pbb[:, :], q1, ALU.mult)
                P.tt("pool", sub(YR, (slice(None), fc, slice(None))), t1, t2, ALU.add)
                P.tt("dve", t3, pbb[:, :], p1, ALU.mult)
                P.tt("dve", t4, pa[:, :], q1, ALU.mult)
                P.tt("pool", sub(YI, (slice(None), fc, slice(None))), t3, t4, ALU.subtract)
            if n == 0 and i == 0:
                dbg("yr", bg(O_HT, 8192), [128, 8192], BF16)
                dbg("yi", bg(O_HT + 8192, 8192), [128, 8192], BF16)
            for tb in range(NB):
                ts_ = slice(tb * 512, (tb + 1) * 512)
                P.dma("sp", GN, dview("gn", lambda a: a[:, ts_]))
                acc = [nb() for _ in range(4)]
                for fq in range(4):
                    k = st["wr"] % 3
                    st["wr"] += 1
                    GC4 = View(WR, WR.t[:, k, 0:2048].rearrange("p (k t) -> p k t", k=4))
                    GS4 = View(WR, WR.t[:, k, 2048:4096].rearrange("p (k t) -> p k t", k=4))
                    P.dma("sp", GC4, dview("cf", lambda a: a[fq * 512:(fq + 1) * 512, ts_].rearrange("(k p) t -> p k t", p=128)))
                    P.dma("sp", GS4, dview("sf", lambda a: a[fq * 512:(fq + 1) * 512, ts_].rearrange("(k p) t -> p k t", p=128)))
                    for f4 in range(4):
                        fc = fq * 4 + f4
                        for c in range(4):
                            cs = slice(c * 128, (c + 1) * 128)
                            P.mm(acc[c][:, :], sub(YR, (slice(None), fc, cs)), sub(GC4, (slice(None), f4, slice(None))), start=(fc == 0), stop=False)
                            P.mm(acc[c][:, :], sub(YI, (slice(None), fc, cs)), sub(GS4, (slice(None), f4, slice(None))), start=False, stop=False)
                for c in range(4):
                    cs = slice(c * 128, (c + 1) * 128)
                    P.mm(acc[c][:, :], sub(YN, (slice(None), cs)), GN, start=False, stop=True)
                    tm = s32(3072, 512)
                    if n == 0:
                        vv = sub(VT, (slice(None), c, ts_))
                        P.stt("dve", tm, vv, vec(f"skip{i}_0", c, 1), acc[c][:, :], ALU.mult, ALU.add)
                        P.tt("pool", vv, tm, sub(X1, (slice(None), c, ts_)), ALU.mult)
                    else:
                        vv = sub(VT, (slice(None), c, ts_))
                        P.stt("dve", tm, vv, vec(f"skip{i}_1", c, 1), acc[c][:, :], ALU.mult, ALU.add)
                        xv = sub(X2, (slice(None), c, ts_))
                        P.tt("pool", xv, tm, xv, ALU.mult)
            if n == 0:
                to_tokmajor(VT, VTOK)

    def mla(i, CQ, CKV, KR):
        KTb = [bg(0 + b * NKEY, NKEY) for b in range(2)]
        QTb = [bg(8192 + b * T, T) for b in range(2)]
        VHb = [R(bg(12288 + b * 1300, 1300), "p (k e) -> p k e", k=20) for b in range(2)]
        for b in range(2):
            P.dma("sp", View(BIG, BIG.t[96:104, b * NKEY:(b + 1) * NKEY]), dview("maskB"))
            P.dma("sp", View(BIG, BIG.t[96:104, 8192 + b * T:8192 + (b + 1) * T]), dview("maskA"))
            P.memset("dve", bg(12288 + b * 1300, 1300), 1.0)
        k = st["wr"] % 3
        st["wr"] += 1
        WKV = View(WR, WR.t[:, k, 0:1024])
        P.dma("pool", WKV, dview("w_ukv", lambda a: a[i]))
        WUQ = View(WR, WR.t[:, k, 1024:4096].rearrange("p (k n) -> p k n", k=2))
        P.dma("pool", WUQ, dview("w_uq", lambda a: a[i].rearrange("(k p) n -> p k n", p=128)))
        sc = 96.0 ** -0.5
        for h in range(8):
            b = h % 2
            KT, QT, VH = KTb[b], QTb[b], VHb[b]
            for kb in range(5):
                pb = nb()
                P.mm(pb[0:64, :], sub(WKV, (slice(None), slice(h * 128, h * 128 + 64))), sub(CKV, (slice(None), slice(kb * 512, (kb + 1) * 512))))
                P.copy("act", sub(KT, (slice(0, 64), slice(kb * 512, (kb + 1) * 512))), pb[0:64, :])
            P.copy("dve", sub(KT, (slice(64, 96), slice(None))), KR)
            for k0, kn in ((0, 8), (8, 8), (16, 4)):
                pb = nb()
                for jj in range(kn):
                    kt = k0 + jj
                    P.mm(pb[:, jj * 64:(jj + 1) * 64], sub(CKV, (slice(None), slice(kt * 128, (kt + 1) * 128))), sub(WKV, (slice(None), slice(h * 128 + 64, h * 128 + 128))))
                P.copy("act", sub(VH, (slice(None), slice(k0, k0 + kn), slice(0, 64))), R(pb[:, 0:kn * 64], "p (k e) -> p k e", k=kn))
            for tb in range(NB):
                ts_ = slice(tb * 512, (tb + 1) * 512)
                pq = nb()
                pp = nb()
                for kc in range(2):
                    P.mm(pq[0:96, :], sub(WUQ, (slice(None), kc, slice(h * 192, h * 192 + 96))), sub(CQ, (slice(None), kc, ts_)), start=(kc == 0), stop=(kc == 1))
                for kc in range(2):
                    P.mm(pp[0:96, :], sub(WUQ, (slice(None), kc, slice(h * 192 + 96, h * 192 + 192))), sub(CQ, (slice(None), kc, ts_)), start=(kc == 0), stop=(kc == 1))
                t1 = s32(1024 + (tb % 2) * 1024, 512, rows=32, r0=64)
                t2 = s32(1536 + (tb % 2) * 1024, 512, rows=32, r0=64)
                P.copy("dve", sub(QT, (slice(0, 64), ts_)), pq[0:64, :])
                P.tt("dve", t1, pq[64:96, :], View(ROPEC, ROPEC.t[64:96, ts_]), ALU.mult)
                P.tt("dve", t2, pp[64:96, :], View(ROPES, ROPES.t[64:96, ts_]), ALU.mult)
                P.tt("pool", sub(QT, (slice(64, 96), ts_)), t1, t2, ALU.add)
            attention(KT, QT, 104, lambda kt, VH=VH: sub(VH, (slice(None), kt, slice(None))), sc,
                      lambda qb, h=h: bg(O_HT + h * T + qb * 512, 512, rows=64))

    def odd_mixer(i, l):
        P.dma("sp", View(ROPEC, ROPEC.t[:, :]), dview("rope_c", lambda a: a[1]))
        P.dma("sp", View(ROPES, ROPES.t[:, :]), dview("rope_s", lambda a: a[1]))
        modulate_full(HT3, 0, bounce=True)
        YCVF = R(bgf(0, 4 * T), "p (c t) -> p c t", c=4)
        YCV = R(bg(O_Z2, 8192), "p (c t) -> p c t", c=4)
        KTs = [bg(16384 + b * NKEY, NKEY) for b in range(2)]
        VV = R(bg(21504, 2600), "p (k g e) -> p k g e", k=20, g=2)
        QTs = [bg(57344 + h * T, T) for h in range(8)]
        for b in range(2):
            P.dma("sp", View(BIG, BIG.t[64:72, 16384 + b * NKEY:16384 + (b + 1) * NKEY]), dview("maskB"))
        for h in range(8):
            P.dma("sp", View(BIG, BIG.t[64:72, 57344 + h * T:57344 + (h + 1) * T]), dview("maskA"))
        P.memset("dve", bg(21504, 2600), 1.0)
        G3 = R(s32(0, 8 * 286), "p (s j) -> p s j", s=8)
        P.memset("pool", s32(0, 8 * 286), 0.0)
        GBF = R(bg(24576, 8192), "p (c t) -> p c t", c=4)
        for c in range(4):
            wa = load_w("od_w_in", i, 0, 8, c * 128, 128)
            wg = load_w("od_w_in", i, 0, 8, 512 + c * 128, 128)
            for tb in range(NB):
                ts_ = slice(tb * 512, (tb + 1) * 512)
                pa = nb()
                pg = nb()
                for kc in range(8):
                    P.mm(pa[:, :], sub(wa, (slice(None), kc, slice(None))), sub(HT3, (slice(None), kc, ts_)), start=(kc == 0), stop=(kc == 7))
                for kc in range(8):
                    P.mm(pg[:, :], sub(wg, (slice(None), kc, slice(None))), sub(HT3, (slice(None), kc, ts_)), start=(kc == 0), stop=(kc == 7))
                sg = s32(2400 + (tb % 2) * 512, 512)
                P.act(sg, pg[:, :], AF.Sigmoid)
                P.tt("dve", sub(GBF, (slice(None), c, ts_)), pa[:, :], sg, ALU.mult)

        def conv_chunk(c):
            P.memset("pool", sub(G3, (slice(None), slice(0, 1), slice(0, 15))), 0.0)
            P.memset("pool", sub(G3, (slice(None), slice(7, 8), slice(271, 286))), 0.0)
            P.copy("pool", sub(G3, (slice(None), slice(None), slice(15, 271))), R(sub(GBF, (slice(None), c, slice(None))), "p (s j) -> p s j", s=8))
            P.ts("dve", sub(G3, (slice(None), slice(1, 8), slice(0, 15))), sub(G3, (slice(None), slice(0, 7), slice(256, 271))), vec("flag"), ALU.mult)
            P.ts("dve", sub(G3, (slice(None), slice(0, 7), slice(271, 286))), sub(G3, (slice(None), slice(1, 8), slice(15, 30))), vec("flag"), ALU.mult)
            av = R(sub(YCVF, (slice(None), c, slice(None))), "p (s j) -> p s j", s=8)
            P.act(av, sub(G3, (slice(None), slice(None), slice(0, 256))), AF.Identity,
                  bias=vec(f"dwb{i}", c, 1), scale=vec(f"dww{i}", c * 31, 1))
            yield
            for k in range(1, 31):
                P.stt("dve", av, sub(G3, (slice(None), slice(None), slice(k, k + 256))), vec(f"dww{i}", c * 31 + k, 1), av, ALU.mult, ALU.add)
                yield
        if OST <= 1:
            bounce_in()
            return
        BONES = View(CST, CST.t[:, 384:512])

        def normed_pair(dst0, dst1, wcol, gname, gpname, tb, wt_main, wt_part):
            ts_ = slice(tb * 512, (tb + 1) * 512)
            pq = nb()
            pp = nb()
            for kc in range(8):
                P.mm(pq[:, :], sub(wt_main, (slice(None), kc, wcol)), sub(HT3, (slice(None), kc, ts_)), start=(kc == 0), stop=(kc == 7))
            for kc in range(8):
                P.mm(pp[:, :], sub(wt_part, (slice(None), kc, wcol)), sub(HT3, (slice(None), kc, ts_)), start=(kc == 0), stop=(kc == 7))
            st["nh"] = st.get("nh", 0) + 1
            o_ = (st["nh"] % 2) * 1536
            sq = s32(o_ + 512, 512)
            P.act(sq, pq[:, :], AF.Square)
            pss = nb()
            P.mm(pss[:, :], BONES, sq)
            rst = s32(o_ + 1024, 512)
            rms_rstd(rst, pss[:, :], 64)
            t1 = s32(o_, 512)
            t2 = s32(o_ + 512, 512)
            P.stt("dve", t1, pq[:, :], vec(gname, 0, 1), rst, ALU.mult, ALU.mult)
            P.stt("dve", t2, pp[:, :], vec(gpname, 0, 1), rst, ALU.mult, ALU.mult)
            P.tt("pool", t1, t1, View(ROPEC, ROPEC.t[:, ts_]), ALU.mult)
            P.tt("pool", t2, t2, View(ROPES, ROPES.t[:, ts_]), ALU.mult)
            P.tt("dve", t1, t1, t2, ALU.add)
            P.copy("act", dst0, sub(t1, (slice(0, 64), slice(None))))
            P.copy("dve", dst1, sub(t1, (slice(64, 128), slice(None))))
            return t1
        for g in range(0 if "noq" in DBGF else 2):
            wq = load_w("od_w_in", i, 0, 8, 1024 + g * 256, 256)
            wqp = load_w("od_w_in", i, 0, 8, 1280 + 512 + g * 256, 256)
            for hp in range(2):
                h = g * 4 + hp * 2
                for tb in range(NB):
                    tsl = slice(tb * 512, (tb + 1) * 512)
                    normed_pair(sub(QTs[h], (slice(0, 64), tsl)), sub(QTs[h + 1], (slice(0, 64), tsl)), slice(hp * 128, hp * 128 + 128),
                                f"gq2{i}", f"gqp2{i}", tb, wq, wqp)
        wk = load_w("od_w_in", i, 0, 8, 1536, 256)
        wkp = load_w("od_w_in", i, 0, 8, 2304, 128)
        for tb in range(0 if "nok" in DBGF else NB):
            ksl = slice(512 + tb * 512, 1024 + tb * 512)
            t1 = normed_pair(sub(KTs[0], (slice(0, 64), ksl)), sub(KTs[1], (slice(0, 64), ksl)), slice(0, 128),
                             f"gk2{i}", f"gkp2{i}", tb, wk, wkp)
            pt = nb()
            for q in range(4):
                P.transpose(pt[:, q * 128:(q + 1) * 128], sub(t1, (slice(None), slice(q * 128, (q + 1) * 128))), IDF)
            ost = s32(3072, 512)
            P.copy("act", ost, pt[:, :])
            P.dma("sp", View(o_gk, o_gk.t.ap()[i, tb * 512:(tb + 1) * 512, :].rearrange("(q p) r -> p q r", p=128)),
                  R(ost, "p (q r) -> p q r", q=4))
        for k0 in range(0, 0 if "nov" in DBGF else NT, 4):
            pb = nb()
            for jj in range(4):
                kt = k0 + jj
                for kc in range(8):
                    P.mm(pb[:, jj * 128:(jj + 1) * 128], sub(HT3, (slice(None), kc, slice(kt * 128, (kt + 1) * 128))), sub(wk, (slice(None), kc, slice(128, 256))), start=(kc == 0), stop=(kc == 7))
            ost = s32(3072, 512)
            P.copy("act", ost, pb[:, :])
            P.copy("act", sub(VV, (slice(None), slice(4 + k0, 8 + k0), slice(None), slice(0, 64))), R(ost, "p (k g e) -> p k g e", k=4, g=2))
            P.dma("sp", View(o_gv, o_gv.t.ap()[i, k0 * 128:(k0 + 4) * 128, :].rearrange("(k p) r -> p k r", p=128)), R(ost, "p (k r) -> p k r", k=4))
        for q in range(0 if "noctx" in DBGF else 4):
            stg = s32(0, 128)
            P.dma("sp", stg, dview("c_gk", lambda a: a[i, q * 128:(q + 1) * 128, :]))
            stg2 = s32(128, 128)
            P.dma("sp", stg2, dview("c_gv", lambda a: a[i, q * 128:(q + 1) * 128, :]))
            for kv in range(0 if "noctxk" in DBGF else 2):
                pb = nb()
                P.mm(pb[0:64, 0:128], sub(stg, (slice(None), slice(kv * 64, kv * 64 + 64))), IDF)
                P.copy("act", sub(KTs[kv], (slice(0, 64), slice(q * 128, (q + 1) * 128))), pb[0:64, 0:128])
            if "noctxv" not in DBGF:
                P.copy("act", sub(VV, (slice(None), q, slice(None), slice(0, 64))), R(stg2, "p (g e) -> p g e", g=2))
        if OST <= 2:
            bounce_in()
            return
        cv_it = iter(())
        for h in range(8):
            kv = h // 4
            if h % 2 == 0:
                for _ in cv_it:
                    pass
                cv_it = conv_chunk(h // 2)
            attention(KTs[kv], QTs[h], 72, lambda kt, kv=kv: sub(VV, (slice(None), kt, kv, slice(None))), 0.125,
                      lambda qb, h=h: bg(O_HT + h * T + qb * 512, 512, rows=64), osb_off=2400, bg_it=cv_it, bg_every=4)
        for _ in cv_it:
            pass
        rb = R(bg(24576, 2048), "p (c t) -> p c t", c=4)
        rb2 = R(bg(24576 + 2048, 2048), "p (c t) -> p c t", c=4)
        ONES512 = View(CSTB, CSTB.t[:, 256:384])
        mean_sb = s32(0, 512)
        m2 = s32(512, 512)
        rstd = s32(1024, 512)
        for tb in range(NB):
            ts_ = slice(tb * 512, (tb + 1) * 512)
            for c in range(4):
                P.act(sub(rb, (slice(None), c, slice(None))), sub(YCVF, (slice(None), c, ts_)), AF.Copy)
                P.act(sub(rb2, (slice(None), c, slice(None))), sub(YCVF, (slice(None), c, ts_)), AF.Square)
            pm = nb()
            pq = nb()
            for c in range(4):
                P.mm(pm[:, :], ONES512, sub(rb, (slice(None), c, slice(None))), start=(c == 0), stop=(c == 3))
            for c in range(4):
                P.mm(pq[:, :], ONES512, sub(rb2, (slice(None), c, slice(None))), start=(c == 0), stop=(c == 3))
            P.copy("act", mean_sb, pm[:, :])
            P.tt("pool", m2, mean_sb, mean_sb, ALU.mult)
            P.ts("pool", m2, m2, -LN_EPS, ALU.add)
            P.tt("dve", rstd, pq[:, :], m2, ALU.subtract)
            P.act(rstd, rstd, AF.Sqrt)
            P.recip(rstd, rstd)
            for c in range(4):
                xv = sub(YCVF, (slice(None), c, ts_))
                P.tt("dve", xv, xv, mean_sb, ALU.subtract)
                P.tt("dve", xv, xv, rstd, ALU.mult)
                P.act(sub(YCV, (slice(None), c, ts_)), xv, AF.Silu, bias=vec(f"cvb{i}", c, 1), scale=vec(f"cvg{i}", c, 1))
        if OST <= 3:
            bounce_in()
            return
        if OST <= 4:
            bounce_in()
            return
        YATT = lambda h: bg(O_HT + h * T, T, rows=64)
        outproj("od_w_out", i, l, YCV, YATT)

    P.memset("dve", View(CSTB, CSTB.t[:, 256:384]), 1.0 / 512)
    ada_it = adaln_tiles(0)
    for _ in ada_it:
        pass
    for l in range(4):
        if (l, 1) <= tuple(STOP):
            adaln_apply()
            if "nomixer" in DBGF:
                pass
            elif l % 2 == 0:
                P.memset("pool", CTMP, 0.0)
                even_mixer(l // 2, l)
            else:
                odd_mixer(l // 2, l)
        if (l, 2) <= tuple(STOP):
            ada_it = adaln_tiles(l + 1) if (l < 3 and "noada" not in DBGF) else iter(())
            mlp(l, ada_it)
            for _ in ada_it:
                pass
    for tt in range(0 if 'noout' in DBGF else NT):
        stg = s32(((tt % 2) * 1024), 1024)
        for half in range(2):
            pb = nb()
            for j in range(4):
                c = half * 4 + j
                P.transpose(pb[:, j * 128:(j + 1) * 128], sub(XT3, (slice(None), c, slice(tt * 128, (tt + 1) * 128))), IDF)
            P.copy("dve" if half else "act", sub(stg, (slice(None), slice(half * 512, half * 512 + 512))), pb[:, :])
        P.dma("sp", View(y_out, y_out.t.ap()[tt * 128:(tt + 1) * 128, :]), stg)
    P.emit()
    es.close()
    return nc

_CACHE = {}


def _prep_shared(inp):
    g = lambda k: np.asarray(inp[k], np.float32)
    W = {}
    ev = g("ev_w_in")
    p32 = perm2d(16)
    krp = ev[:, :, 1920:1952][:, :, p32]
    W["ev_w_in"] = np.ascontiguousarray(np.concatenate([ev, krp], axis=2))
    uq = g("mla_w_uq").reshape(2, 256, 8, 96)
    part = np.zeros((2, 256, 8, 96), np.float32)
    part[:, :, :, 64:96] = uq[:, :, :, 64:96][:, :, :, p32]
    W["w_uq"] = np.ascontiguousarray(np.concatenate([uq, part], axis=3).reshape(2, 256, 8 * 192))
    W["w_ukv"] = g("mla_w_ukv")
    W["ev_w_out"] = g("ev_w_out")
    od = g("od_w_in")
    p64 = perm2d(32)
    q = od[:, :, 1024:1536].reshape(2, 1024, 8, 64)
    k = od[:, :, 1536:1664].reshape(2, 1024, 2, 64)
    qp = q[:, :, :, p64].reshape(2, 1024, 512)
    kp = k[:, :, :, p64].reshape(2, 1024, 128)
    W["od_w_in"] = np.ascontiguousarray(np.concatenate([od, qp, kp], axis=2))
    W["od_w_out"] = g("od_w_out")
    W["ada_w"] = g("ada_w")
    W["mlp_w1"] = g("mlp_w1")
    W["mlp_w2"] = g("mlp_w2")
    W["f_w1"] = g("hy_filt_w1")
    W["f_w2"] = g("hy_filt_w2")
    W["f_w3"] = g("hy_filt_w3")
    W["decay_bc"] = np.ascontiguousarray(np.broadcast_to(g("hy_decay")[:, None, :], (2, 128, 2048)))
    W["ident"] = np.eye(128, dtype=np.float32)
    return W


def _vecs(inp, cond, cvec):
    g = lambda k: np.asarray(inp[k], np.float32)
    vp = VecPack()
    vp.add("cond", chunked(cond))
    for k_, v_ in cvec.items():
        vp.add(k_, v_)
    for l in range(4):
        vp.add(f"ada_b{l}", chunked(g("ada_b")[l]))
        for w in range(2):
            vp.add(f"ln_g{l}_{w}", chunked(g("ln_g")[l, w]))
            vp.add(f"ln_b{l}_{w}", chunked(g("ln_b")[l, w]))
        vp.add(f"b1_{l}", chunked(g("mlp_b1")[l]))
        vp.add(f"b2_{l}", chunked(g("mlp_b2")[l]))
    p64 = perm2d(32)
    for i in range(2):
        cw = g("hy_conv_w")[i]
        hcw = cw.T.reshape(12, 128, 3).transpose(1, 0, 2).reshape(128, 36)
        vp.add(f"hcw{i}", hcw)
        vp.add(f"hcb{i}", chunked(g("hy_conv_b")[i]))
        vp.add(f"skip{i}_0", chunked(g("hy_skip")[i, 0]))
        vp.add(f"skip{i}_1", chunked(g("hy_skip")[i, 1]))
        vp.add(f"qg{i}", chunked(g("mla_q_norm_g")[i]))
        vp.add(f"kvg{i}", chunked(g("mla_kv_norm_g")[i]))
        vp.add(f"fb1_{i}", g("hy_filt_b1")[i][:, None])
        vp.add(f"fb2_{i}", g("hy_filt_b2")[i][:, None])
        vp.add(f"ffreq{i}", g("hy_sin_freq")[i][:, None])
        dw = g("cv_dw_w")[i]
        vp.add(f"dww{i}", dw.T.reshape(4, 128, 31).transpose(1, 0, 2).reshape(128, 124))
        vp.add(f"dwb{i}", chunked(g("cv_dw_b")[i]))
        vp.add(f"cvg{i}", chunked(g("cv_ln_g")[i]))
        vp.add(f"cvb{i}", chunked(g("cv_ln_b")[i]))
        gq = g("gqa_q_norm_g")[i]
        gk = g("gqa_k_norm_g")[i]
        vp.add(f"gq2{i}", np.concatenate([gq, gq])[:, None])
        vp.add(f"gqp2{i}", np.concatenate([gq[p64], gq[p64]])[:, None])
        vp.add(f"gk2{i}", np.concatenate([gk, gk])[:, None])
        vp.add(f"gkp2{i}", np.concatenate([gk[p64], gk[p64]])[:, None])
        vp.add(f"gq{i}", gq[:, None])
        vp.add(f"gqp{i}", gq[p64][:, None])
        vp.add(f"gk{i}", gk[:, None])
        vp.add(f"gkp{i}", gk[p64][:, None])
    return vp


def kernel(**inp):
    W = _prep_shared(inp)
    consts = {True: make_consts(True), False: make_consts(False)}
    xp = np.asarray(inp["x_prompt"], np.float32)
    xs = np.asarray(inp["x_sample"], np.float32)
    in_maps = []
    voff = None
    NV = None
    for core in range(8):
        sample = core in (4, 5)
        c, cvec = consts[sample]
        m = dict(W)
        for k_ in ("cft", "sft", "cf", "sf", "fnt", "gn", "zT", "maskA", "maskB", "rope_c", "rope_s"):
            m[k_] = c[k_]
        if sample:
            b = core - 4
            m["x_in"] = np.ascontiguousarray(xs[b])
            cond = np.asarray(inp["c"], np.float32)[b]
            m["c_ckv"] = np.ascontiguousarray(np.asarray(inp["cache_mla_ckv"], np.float32)[b])
            m["c_kr"] = np.ascontiguousarray(np.asarray(inp["cache_mla_krope"], np.float32)[b])
            m["c_gk"] = np.ascontiguousarray(np.asarray(inp["cache_gqa_k"], np.float32)[b].reshape(2, 512, 128))
            m["c_gv"] = np.ascontiguousarray(np.asarray(inp["cache_gqa_v"], np.float32)[b].reshape(2, 512, 128))
        else:
            pc = core if core < 4 else core - 6
            m["x_in"] = np.ascontiguousarray(xp[pc * 8:(pc + 1) * 8].reshape(T, DM))
            cond = np.asarray(inp["c_ctx"], np.float32)
            m["c_ckv"] = np.zeros((2, 512, 128), np.float32)
            m["c_kr"] = np.zeros((2, 512, 32), np.float32)
            m["c_gk"] = np.zeros((2, 512, 128), np.float32)
            m["c_gv"] = np.zeros((2, 512, 128), np.float32)
        vp = _vecs(inp, cond, cvec)
        m["vecs"] = vp.array()
        voff, NV = vp.off, vp.n
        in_maps.append(m)
    key = NV
    if key not in _CACHE:
        _CACHE[key] = build(voff, NV)
    nc = _CACHE[key]
    res = run_bass_kernel_spmd(nc, in_maps, core_ids=list(range(8)))
    r = res.results
    y_prompt = np.concatenate([r[c]["y_out"].reshape(8, 256, DM) for c in range(4)], axis=0)
    y_sample = np.stack([r[4]["y_out"], r[5]["y_out"]], axis=0)

    def st(name, last):
        outs = []
        for c in range(4):
            a = r[c][name]
            a = a.reshape(2, 8, 256, -1).transpose(1, 0, 2, 3)
            outs.append(a)
        a = np.concatenate(outs, axis=0)
        return np.ascontiguousarray(a.reshape((32, 2, 256) + last)).astype(np.float32)
    return (y_prompt.astype(np.float32), y_sample.astype(np.float32), st("o_ckv", (128,)), st("o_kr", (32,)),
            st("o_gk", (2, 64)), st("o_gv", (2, 64)))
```

```python
import math, bisect
from contextlib import ExitStack
from concourse.bass_utils import run_bass_kernel_spmd
import bisect
import numpy as np
import concourse.bass as bass
import concourse.mybir as mybir

F32 = mybir.dt.float32
BF16 = mybir.dt.bfloat16
AF = mybir.ActivationFunctionType
ALU = mybir.AluOpType


class Op:
    __slots__ = ("eng", "idx", "fn", "deps", "kind", "signal", "sem", "val")

    def __init__(self, eng, idx, fn, kind):
        self.eng = eng
        self.idx = idx
        self.fn = fn
        self.kind = kind
        self.deps = []
        self.signal = False
        self.sem = None
        self.val = 0


class Slot:
    def __init__(self, t, nfree, name, whole=False):
        self.t = t
        self.nfree = nfree
        self.name = name
        self.whole = whole
        self.los = [0]
        self.segs = [[0, nfree, None, {}]]

    def __getitem__(self, idx):
        return View(self, self.t[idx])

    def ap(self):
        return View(self, self.t.ap() if hasattr(self.t, "ap") else self.t[:])

    def touch(self, lo, hi, op, write, deps):
        assert 0 <= lo < hi <= self.nfree, (self.name, lo, hi, self.nfree)
        segs, los = self.segs, self.los
        i = bisect.bisect_right(los, lo) - 1
        if i < 0:
            i = 0
        out = []
        j = i
        n = len(segs)
        while j < n and segs[j][0] < hi:
            slo, shi, w, rd = segs[j]
            if shi <= lo:
                out.append(segs[j])
                j += 1
                continue
            if slo < lo:
                out.append([slo, lo, w, dict(rd)])
            a = max(slo, lo)
            b = min(shi, hi)
            if w is not None and w is not op:
                deps.add(w)
            if write:
                for r in rd.values():
                    if r is not op:
                        deps.add(r)
                if out and out[-1][2] is op and out[-1][1] == a and not out[-1][3]:
                    out[-1][1] = b
                else:
                    out.append([a, b, op, {}])
            else:
                rd2 = dict(rd)
                key = op.eng if op.kind == "c" else ("d", op.idx)
                rd2[key] = op
                out.append([a, b, w, rd2])
            if shi > hi:
                out.append([hi, shi, w, dict(rd)])
            j += 1
        segs[i:j] = out
        self.los[i:j] = [s[0] for s in out]


class Reg:
    def __init__(self, t, base, slot):
        self.t = t
        self.base = base
        self.slot = slot

    def __getitem__(self, idx):
        return View(self, self.t[idx])


class View:
    __slots__ = ("slot", "ap", "base")

    def __init__(self, slot, ap, base=0):
        if isinstance(slot, Reg):
            base = slot.base
            slot = slot.slot
        self.slot = slot
        self.ap = ap
        self.base = base

    def ranges(self):
        s = self.slot
        if s.whole:
            return [(0, s.nfree)]
        ap = self.ap
        es = 2 if ap.dtype == BF16 else 4
        dims = list(ap.ap)
        pstep = dims[0][0]
        off = ((ap.offset % pstep) * es if pstep > 0 else 0) + self.base
        fd = [(st * es, c) for st, c in dims[1:] if c > 1]
        if not fd:
            return [(off, off + es)]
        fd.sort(key=lambda x: -x[0])
        span = sum(st * (c - 1) for st, c in fd) + es
        if len(fd) >= 2:
            s0, c0 = fd[0]
            inner = sum(st * (c - 1) for st, c in fd[1:]) + es
            if inner < s0 and c0 <= 64:
                return [(off + i * s0, off + i * s0 + inner) for i in range(c0)]
        return [(off, off + span)]


ENG_NAMES = ["pe", "act", "dve", "pool", "sp"]


class Prog:
    def __init__(self, nc, es):
        self.nc = nc
        self.es = es
        self.ops = []
        self.streams = {e: [] for e in ENG_NAMES}
        self.nslots = 0

    SB_BASE = 16384
    SB_TOP = 229344

    def sbuf_at(self, name, shape, dt, offset):
        if not hasattr(self, "SB"):
            self.SB = Slot(None, 1 << 20, "SB")
            self.nreg = 0
        self.nreg += 1
        nbytes = int(np.prod(shape[1:])) * (2 if dt == BF16 else 4)
        assert offset >= self.SB_BASE and offset + nbytes <= self.SB_TOP, (name, offset, nbytes)
        t = self.nc.alloc_sbuf_tensor_at(f"{name}_{self.nreg}", list(shape), dt, offset=offset)
        return Reg(t, offset, self.SB)

    def sbuf(self, name, shape, dt):
        if not hasattr(self, "bump"):
            self.bump = self.SB_BASE
        nbytes = int(np.prod(shape[1:])) * (2 if dt == BF16 else 4)
        nbytes = (nbytes + 63) // 64 * 64
        off = self.bump
        self.bump += nbytes
        return self.sbuf_at(name, shape, dt, off)

    def psum(self, name, shape, dt=F32):
        t = self.es.enter_context(self.nc.psum_tensor(name, list(shape), dt))
        nbytes = int(np.prod(shape[1:])) * 4
        return Slot(t, nbytes, name)

    def dram(self, name, shape, dt, kind):
        t = self.nc.dram_tensor(name, list(shape), dt, kind=kind)
        return Slot(t, 1, name, whole=True)

    def op(self, eng, fn, reads, writes, kind="c"):
        o = Op(eng, len(self.ops), fn, kind)
        deps = set()
        for v in reads:
            for lo, hi in v.ranges():
                v.slot.touch(lo, hi, o, False, deps)
        for v in writes:
            for lo, hi in v.ranges():
                v.slot.touch(lo, hi, o, True, deps)
        dl = []
        for d in deps:
            if d.kind == "c" and d.eng == eng and eng == "pe":
                continue
            d.signal = True
            dl.append(d)
        dl.sort(key=lambda d: d.idx)
        o.deps = dl
        if kind == "d":
            o.signal = True
        self.ops.append(o)
        self.streams[eng].append(o)
        return o

    def mm(self, out, lhsT, rhs, start=True, stop=True, **kw):
        return self.op("pe", lambda e: e.matmul(out.ap, lhsT.ap, rhs.ap, start=start, stop=stop, **kw),
                       [lhsT, rhs], [out])

    def transpose(self, out, in_, ident):
        return self.op("pe", lambda e: e.transpose(out.ap, in_.ap, ident.ap), [in_, ident], [out])

    def act(self, out, in_, func, bias=None, scale=None, accum_out=None):
        reads = [in_]
        kw = {}
        if bias is not None:
            if isinstance(bias, View):
                reads.append(bias)
                kw["bias"] = bias.ap
            else:
                kw["bias"] = bias
        if scale is not None:
            if isinstance(scale, View):
                reads.append(scale)
                kw["scale"] = scale.ap
            else:
                kw["scale"] = scale
        writes = [out]
        if accum_out is not None:
            writes.append(accum_out)
            kw["accum_out"] = accum_out.ap
        return self.op("act", lambda e: e.activation(out.ap, in_.ap, func, **kw), reads, writes)

    def tt(self, eng, out, in0, in1, op):
        return self.op(eng, lambda e: e.tensor_tensor(out.ap, in0.ap, in1.ap, op), [in0, in1], [out])

    def ts(self, eng, out, in0, s1, op0, s2=None, op1=None):
        reads = [in0]
        a1 = s1
        if isinstance(s1, View):
            reads.append(s1)
            a1 = s1.ap
        a2 = s2
        if isinstance(s2, View):
            reads.append(s2)
            a2 = s2.ap
        if op1 is None:
            return self.op(eng, lambda e: e.tensor_scalar(out.ap, in0.ap, a1, None, op0), reads, [out])
        return self.op(eng, lambda e: e.tensor_scalar(out.ap, in0.ap, a1, a2, op0, op1), reads, [out])

    def stt(self, eng, out, in0, scalar, in1, op0, op1):
        reads = [in0, in1]
        a = scalar
        if isinstance(scalar, View):
            reads.append(scalar)
            a = scalar.ap
        return self.op(eng, lambda e: e.scalar_tensor_tensor(out.ap, in0.ap, a, in1.ap, op0, op1), reads, [out])

    def copy(self, eng, out, in_):
        if eng == "act":
            return self.act(out, in_, AF.Copy)
        return self.op(eng, lambda e: e.tensor_copy(out.ap, in_.ap), [in_], [out])

    def memset(self, eng, out, val):
        return self.op(eng, lambda e: e.memset(out.ap, val), [], [out])

    def recip(self, out, in_):
        return self.op("dve", lambda e: e.reciprocal(out.ap, in_.ap), [in_], [out])

    def dma(self, q, out, in_, **kw):
        return self.op(q, lambda e: e.dma_start(out=out.ap, in_=in_.ap, **kw), [in_], [out], kind="d")

    def emit(self):
        nc = self.nc
        es = self.es
        ROT = 30000
        NPOOL = 24
        sem_list = {}

        def new_sem(name):
            return es.enter_context(nc.semaphore(name))

        for e in ENG_NAMES:
            cnt = 0
            dcnt = 0
            cur = None
            pool = []
            for o in self.streams[e]:
                if o.kind == "c":
                    if not o.signal:
                        continue
                    if cur is None or cnt >= ROT:
                        cur = new_sem(f"s_{e}_{len(sem_list)}")
                        sem_list[id(cur)] = cur
                        cnt = 0
                    cnt += 1
                    o.sem = cur
                    o.val = cnt
                else:
                    k = dcnt % NPOOL
                    if k >= len(pool):
                        pool.append(new_sem(f"d_{e}_{k}"))
                    o.sem = pool[k]
                    o.val = 16 * (dcnt // NPOOL + 1)
                    dcnt += 1
        all_dma = [o for o in self.ops if o.kind == "d"]
        last_dma = {}
        for o in all_dma:
            last_dma[id(o.sem)] = o
        block = es.enter_context(nc.Block())

        def run_stream(e, h, final=False):
            seen = {}
            nwait = 0
            for o in self.streams[e]:
                needs = []
                for d in o.deps:
                    needs.append((d.sem, d.val))
                if o.kind == "d" and o.val > 16:
                    needs.append((o.sem, o.val - 16))
                for s, v in needs:
                    if seen.get(id(s), 0) >= v:
                        continue
                    h.wait_ge(s, v)
                    seen[id(s)] = v
                    nwait += 1
                ins = o.fn(h)
                if o.signal:
                    ins.then_inc(o.sem, 16 if o.kind == "d" else 1)
            if final:
                for o in last_dma.values():
                    if seen.get(id(o.sem), 0) < o.val:
                        h.wait_ge(o.sem, o.val)
            return nwait

        @block.tensor
        def _(h):
            run_stream("pe", h)

        @block.scalar
        def _(h):
            run_stream("act", h)

        @block.vector
        def _(h):
            run_stream("dve", h)

        @block.gpsimd
        def _(h):
            run_stream("pool", h)

        @block.sync
        def _(h):
            run_stream("sp", h, final=True)
NLAYERS = 4

T = 2048
DM = 1024
NT = 16
NB = 4
ALPHA = 8.0 ** 0.25
LN_EPS = 1e-5
RMS_EPS = 1e-6
NKEY = 2560
BIGC = 76800


class VecPack:
    def __init__(self):
        self.cols = []
        self.off = {}
        self.n = 0

    def add(self, name, a):
        a = np.asarray(a, np.float32)
        if a.shape[0] != 128:
            b = np.zeros((128,) + a.shape[1:], np.float32)
            b[: a.shape[0]] = a
            a = b
        a = a.reshape(128, -1)
        self.off[name] = (self.n, a.shape[1])
        self.cols.append(a)
        self.n += a.shape[1]

    def array(self):
        return np.ascontiguousarray(np.concatenate(self.cols, axis=1))


def chunked(v):
    v = np.asarray(v, np.float32)
    return np.ascontiguousarray(v.reshape(-1, 128).T)


def rope_perm(r):
    h = r // 2
    idx = np.concatenate([np.arange(h, r), np.arange(0, h)])
    sgn = np.concatenate([-np.ones(h), np.ones(h)]).astype(np.float32)
    return idx, sgn


def rope_tables(r_axis, sample):
    t = np.arange(T)
    row = (t // 64).astype(np.float64)
    col = (t % 64).astype(np.float64)
    inv = 10000.0 ** (-np.arange(0, r_axis, 2, dtype=np.float64) / r_axis)
    cs, sn = [], []
    for pos in (row, col):
        ang = pos[None, :] * inv[:, None]
        ang = np.concatenate([ang, ang], axis=0)
        _, sgn = rope_perm(r_axis)
        cs.append(np.cos(ang))
        sn.append(np.sin(ang) * sgn[:, None])
    c = np.concatenate(cs, 0)
    s = np.concatenate(sn, 0)
    if not sample:
        c = np.ones_like(c)
        s = np.zeros_like(s)
    return c.astype(np.float32), s.astype(np.float32)


def perm2d(r_axis):
    i1, _ = rope_perm(r_axis)
    return np.concatenate([i1, i1 + r_axis])


def make_consts(sample):
    import ml_dtypes
    L = 2048 if sample else 256
    t = np.arange(T)
    m = t % L
    seg = t // L
    c = {}
    fidx = np.arange(T)
    f = fidx % L
    fseg = fidx // L
    ang = np.pi * np.outer(m, f).astype(np.float64) / L
    same = (seg[:, None] == fseg[None, :])
    Cf = np.where(same, np.cos(ang), 0.0)
    Sf = np.where(same, np.sin(ang), 0.0)
    bf = ml_dtypes.bfloat16
    c["cf"] = np.ascontiguousarray(Cf.astype(np.float32).astype(bf))
    c["sf"] = np.ascontiguousarray(Sf.astype(np.float32).astype(bf))
    def tile_f(M):
        return np.ascontiguousarray(M.reshape(16, 128, 16, 128).transpose(2, 1, 0, 3).reshape(16, 128, 2048).astype(np.float32).astype(bf))
    c["cft"] = tile_f(Cf)
    c["sft"] = tile_f(Sf)
    FN = np.zeros((T, 8))
    FN[t, seg] = (-1.0) ** m
    c["fnt"] = np.ascontiguousarray(FN.reshape(16, 128, 8).transpose(1, 0, 2).astype(np.float32).astype(bf))
    c["gn"] = np.ascontiguousarray(FN.T.astype(np.float32).astype(bf))
    wf = np.where(f == 0, 1.0 / (2 * L), 1.0 / L)
    tpos = m / float(L)
    bands = np.arange(1, 17)
    a2 = 2 * np.pi * tpos[:, None] * bands[None, :]
    z = np.concatenate([tpos[:, None], np.cos(a2), np.sin(a2)], axis=1)
    c["zT"] = np.ascontiguousarray(z.T.astype(np.float32))
    mA = np.zeros((8, T), np.float32)
    mB = np.zeros((8, NKEY), np.float32)
    sg8 = t // 256
    mA[sg8, t] = 1.0
    if not sample:
        BIG = 30000.0
        mB[:, :512] = -BIG
        for s in range(8):
            mB[s, 512:] = np.where(sg8 == s, 0.0, -BIG)
    c["maskA"] = mA.astype(bf)
    c["maskB"] = mB.astype(bf)
    vec = {}
    vec["wf"] = chunked(wf)
    vec["negt"] = chunked(-tpos)
    vec["m0"] = chunked((m != 0).astype(np.float32))
    vec["wN"] = np.full((128, 1), 1.0 / (2 * L), np.float32)
    vec["flag"] = np.full((128, 1), 1.0 if sample else 0.0, np.float32)
    cq, sq = rope_tables(16, sample)
    cg, sg = rope_tables(32, sample)
    rc_e = np.ones((128, T), np.float32)
    rs_e = np.zeros((128, T), np.float32)
    rc_e[0:32] = cq
    rs_e[0:32] = sq
    rc_e[64:96] = cq
    rs_e[64:96] = sq
    rc_o = np.ones((128, T), np.float32)
    rs_o = np.zeros((128, T), np.float32)
    rc_o[0:64] = cg
    rs_o[0:64] = sg
    rc_o[64:128] = cg
    rs_o[64:128] = sg
    c["rope_c"] = np.stack([rc_e, rc_o]).astype(bf)
    c["rope_s"] = np.stack([rs_e, rs_o]).astype(bf)
    return c, vec

STOP = (3, 2)
DBGF = ()
EST = 9
OST = 9


def build(voff, NV, dbg=False):
    nc = bass.Bass("TRN2", target_bir_lowering=False)
    es = ExitStack()
    P = Prog(nc, es)
    D = {}

    def din(name, shape, dt=F32):
        D[name] = P.dram(name, shape, dt, "ExternalInput")
        return D[name]

    def dview(name, fn=None):
        s = D[name]
        a = s.t.ap()
        if fn is not None:
            a = fn(a)
        return View(s, a)

    def dbg(name, v, shape, dt):
        if "dbg" not in DBGF:
            return
        dr = P.dram("dbg_" + name, list(shape), dt, "ExternalOutput")
        P.dma("sp", View(dr, dr.t.ap()), v)

    din("x_in", [T, DM])
    din("vecs", [128, NV])
    din("ident", [128, 128])
    din("cft", [16, 128, 2048], BF16)
    din("sft", [16, 128, 2048], BF16)
    din("cf", [T, T], BF16)
    din("sf", [T, T], BF16)
    din("fnt", [128, 16, 8], BF16)
    din("gn", [8, T], BF16)
    din("zT", [33, T])
    din("maskA", [8, T], BF16)
    din("maskB", [8, NKEY], BF16)
    din("rope_c", [2, 128, T], BF16)
    din("rope_s", [2, 128, T], BF16)
    din("c_ckv", [2, 512, 128])
    din("c_kr", [2, 512, 32])
    din("c_gk", [2, 512, 128])
    din("c_gv", [2, 512, 128])
    din("ada_w", [4, DM, 6144])
    din("ev_w_in", [2, DM, 1984])
    din("ev_w_out", [2, DM, DM])
    din("w_uq", [2, 256, 8 * 192])
    din("w_ukv", [2, 128, 1024])
    din("f_w1", [2, 33, 64])
    din("f_w2", [2, 64, 64])
    din("f_w3", [2, 64, 2048])
    din("decay_bc", [2, 128, 2048])
    din("od_w_in", [2, DM, 2432])
    din("od_w_out", [2, DM, DM])
    din("mlp_w1", [4, DM, 4096])
    din("mlp_w2", [4, 4096, DM])
    y_out = P.dram("y_out", [T, DM], F32, "ExternalOutput")
    o_ckv = P.dram("o_ckv", [2, T, 128], F32, "ExternalOutput")
    o_kr = P.dram("o_kr", [2, T, 32], F32, "ExternalOutput")
    o_gk = P.dram("o_gk", [2, T, 128], F32, "ExternalOutput")
    o_gv = P.dram("o_gv", [2, T, 128], F32, "ExternalOutput")
    xscr = P.dram("xscr", [128, 8 * T], F32, "Internal")

    BIG = P.sbuf("BIG", [128, BIGC], BF16)
    VEC = P.sbuf("VEC", [128, NV], F32)
    S32 = P.sbuf("S32", [128, 3584], F32)
    WR = P.sbuf("WR", [128, 3, 4096], BF16)
    ROPEC = P.sbuf("ROPEC", [128, T], BF16)
    ROPES = P.sbuf("ROPES", [128, T], BF16)
    CST = P.sbuf("CST", [128, 512], F32)
    CSTB = P.sbuf("CSTB", [128, 512], BF16)
    MODT = P.sbuf("MODT", [128, 128], F32)
    SMALL = P.sbuf("SMALL", [128, 1200], BF16)
    PS = [P.psum(f"ps{i}", [128, 512]) for i in range(8)]
    st = {"pb": 0, "wr": 0}

    def nb():
        b = PS[st["pb"] % st.get("nbm", 8)]
        st["pb"] += 1
        return b

    def bg(off, n, rows=128, r0=0):
        return View(BIG, BIG.t[r0:r0 + rows, off:off + n])

    def bgf(off, n, rows=128, r0=0):
        assert off + 2 * n <= BIGC
        rg = P.sbuf_at("bgf", [128, n], F32, BIG.base + off * 2)
        return View(rg, rg.t[r0:r0 + rows, :])

    def s32(off, n, rows=128, r0=0):
        return View(S32, S32.t[r0:r0 + rows, off:off + n])

    def vec(name, c0=0, n=1, rows=128, r0=0):
        o, w = voff[name]
        return View(VEC, VEC.t[r0:r0 + rows, o + c0:o + c0 + n])

    def R(v, pat, **kw):
        return View(v.slot, v.ap.rearrange(pat, **kw), v.base)

    def sub(v, idx):
        return View(v.slot, v.ap[idx], v.base)

    def wtile(src_view):
        k = st["wr"] % 3
        st["wr"] += 1
        return k

    XT = bgf(0, 8 * T)
    XT3 = R(XT, "p (c t) -> p c t", c=8)

    P.dma("sp", View(VEC, VEC.t[:, :]), dview("vecs"))
    IDF = View(CST, CST.t[:, 0:128])
    P.dma("sp", IDF, dview("ident"))
    IDB = View(CSTB, CSTB.t[:, 0:128])
    P.copy("act", IDB, IDF)
    ONESB = View(CSTB, CSTB.t[:, 128:256])
    P.memset("dve", ONESB, 1.0 / 1024)
    ONES64 = View(CST, CST.t[0:64, 128:192])
    P.memset("dve", ONES64, 1.0)
    ONES128 = View(CST, CST.t[:, 256:384])
    P.memset("dve", ONES128, 1.0)
    E65 = View(CST, CST.t[0:65, 192:256])
    P.memset("dve", View(CST, CST.t[:, 384:512]), 0.0)
    P.memset("dve", View(CST, CST.t[0:64, 384:448]), 1.0)
    P.memset("dve", View(CST, CST.t[64:128, 448:512]), 1.0)
    P.memset("dve", View(CST, CST.t[0:64, 192:256]), 0.0)
    P.memset("dve", View(CST, CST.t[64:65, 192:256]), 1.0)
    FNT = View(SMALL, SMALL.t[:, 0:128])
    if "nofnt" not in DBGF:
        P.dma("sp", FNT, dview("fnt", lambda a: a.rearrange("p k s -> p (k s)")))
    FNT3 = R(FNT, "p (k s) -> p k s", k=16)
    GN = View(SMALL, SMALL.t[0:8, 128:128 + 512])
    SBF = View(SMALL, SMALL.t[:, 640:648])
    if "nosilu" not in DBGF:
        P.act(SBF, vec("cond", 0, 8), AF.Silu)
    YN = View(SMALL, SMALL.t[0:8, 656:656 + 512])
    CTMP = s32(0, 8 * 258)
    CT3 = R(CTMP, "p (s j) -> p s j", s=8)
    if "noctmp" not in DBGF:
        P.memset("pool", CTMP, 0.0)

    for tt in range(0 if 'noload' in DBGF else NT):
        stg = s32(2100, 1024) if tt % 2 else s32(0, 1024)
        P.dma("sp", stg, dview("x_in", lambda a: a[tt * 128:(tt + 1) * 128, :]))
        for half in range(2):
            pb = nb()
            for j in range(4):
                c = half * 4 + j
                P.transpose(pb[:, j * 128:(j + 1) * 128], sub(stg, (slice(None), slice(c * 128, (c + 1) * 128))), IDF)
            P.copy("dve" if half else "act",
                   sub(XT3, (slice(None), slice(half * 4, half * 4 + 4), slice(tt * 128, (tt + 1) * 128))),
                   R(pb[:, :], "p (c t) -> p c t", c=4))

    def load_w(dname, li, r0, nrow_chunks, c0, ncols, rows=128):
        k = st["wr"] % 3
        st["wr"] += 1
        dst = View(WR, WR.t[0:rows, k, 0:nrow_chunks * ncols].rearrange("p (k n) -> p k n", k=nrow_chunks))
        src = dview(dname, lambda a: a[li, r0:r0 + nrow_chunks * rows, c0:c0 + ncols].rearrange("(k p) n -> p k n", p=rows))
        P.dma("pool", dst, src)
        return dst

    MOD = View(MODT, MODT.t[:, 0:48])
    SCP = View(MODT, MODT.t[:, 48:64])
    FB = View(MODT, MODT.t[0:64, 64:68])

    MODN = View(MODT, MODT.t[:, 68:116])

    def adaln_tiles(l):
        pb = PS[7]
        for g in range(12):
            wt = load_w("ada_w", l, 0, 8, g * 512, 512)
            yield
            adaln_mm(pb, wt, g)
            yield
        P.tt("dve", MODN, pb[:, 0:48], vec(f"ada_b{l}", 0, 48), ALU.add)
        yield

    def adaln_mm(pb, wt, g):
        for j in range(4):
            jc = g * 4 + j
            for kc in range(8):
                P.mm(pb[:, jc:jc + 1], sub(wt, (slice(None), kc, slice(j * 128, (j + 1) * 128))),
                     sub(SBF, (slice(None), slice(kc, kc + 1))), start=(kc == 0), stop=(kc == 7))

    def adaln_apply():
        P.copy("dve", MOD, MODN)
        P.ts("dve", sub(SCP, (slice(None), slice(0, 8))), sub(MOD, (slice(None), slice(8, 16))), 1.0, ALU.add)
        P.ts("dve", sub(SCP, (slice(None), slice(8, 16))), sub(MOD, (slice(None), slice(32, 40))), 1.0, ALU.add)

    def modcol(k, c):
        return sub(MOD, (slice(None), slice(k * 8 + c, k * 8 + c + 1)))

    def layer_norm(l, which):
        for tb in range(NB):
            par = tb % 2
            rb = R(bg(32768 + par * 8192, 4096), "p (c t) -> p c t", c=8)
            rb2 = R(bg(36864 + par * 8192, 4096), "p (c t) -> p c t", c=8)
            mean_sb = s32(par * 1536, 512)
            m2 = s32(par * 1536 + 512, 512)
            rstd = s32(par * 1536 + 1024, 512)
            ts_ = slice(tb * 512, (tb + 1) * 512)
            for c in range(8):
                P.act(sub(rb, (slice(None), c, slice(None))), sub(XT3, (slice(None), c, ts_)), AF.Copy)
                P.act(sub(rb2, (slice(None), c, slice(None))), sub(XT3, (slice(None), c, ts_)), AF.Square)
            pm = nb()
            pq = nb()
            for c in range(8):
                P.mm(pm[:, :], ONESB, sub(rb, (slice(None), c, slice(None))), start=(c == 0), stop=(c == 7))
            for c in range(8):
                P.mm(pq[:, :], ONESB, sub(rb2, (slice(None), c, slice(None))), start=(c == 0), stop=(c == 7))
            P.copy("act", mean_sb, pm[:, :])
            P.tt("pool", m2, mean_sb, mean_sb, ALU.mult)
            P.ts("pool", m2, m2, -LN_EPS, ALU.add)
            P.tt("dve", rstd, pq[:, :], m2, ALU.subtract)
            P.act(rstd, rstd, AF.Sqrt)
            P.recip(rstd, rstd)
            for c in range(8):
                xv = sub(XT3, (slice(None), c, ts_))
                P.tt("dve", xv, xv, mean_sb, ALU.subtract)
                P.tt("dve", xv, xv, rstd, ALU.mult)
                P.act(xv, xv, AF.Identity, bias=vec(f"ln_b{l}_{which}", c, 1), scale=vec(f"ln_g{l}_{which}", c, 1))

    def modulate_full(dst3, kmod, bounce=False):
        for c in range(8):
            P.act(sub(dst3, (slice(None), c, slice(None))), sub(XT3, (slice(None), c, slice(None))), AF.Identity,
                  bias=modcol(0 if kmod == 0 else 3, c), scale=sub(SCP, (slice(None), slice(kmod * 8 + c, kmod * 8 + c + 1))))
            if bounce:
                P.dma("sp", View(xscr, xscr.t.ap()[:, c * T:(c + 1) * T]), sub(XT3, (slice(None), c, slice(None))))

    def bounce_out():
        for c in range(8):
            P.dma("sp", View(xscr, xscr.t.ap()[:, c * T:(c + 1) * T]), sub(XT3, (slice(None), c, slice(None))))

    def bounce_in():
        for c in range(8):
            P.dma("sp", sub(XT3, (slice(None), c, slice(None))), View(xscr, xscr.t.ap()[:, c * T:(c + 1) * T]))

    def residual(c, tb, pb, gk):
        xv = sub(XT3, (slice(None), c, slice(tb * 512, (tb + 1) * 512)))
        P.act(xv, xv, AF.Copy, scale=ALPHA)
        P.stt("dve", xv, pb, modcol(gk, c), xv, ALU.mult, ALU.add)

    def mlp(l, bg_it=None):
        st["nbm"] = 7

        def tick():
            if bg_it is not None:
                next(bg_it, None)
        HB = R(bg(32768, 8192), "p (c t) -> p c t", c=8)
        HID = R(bg(40960, 32768), "p (c t) -> p c t", c=32)
        tmp = s32(0, 512)
        for hb in range(2):
            t0 = hb * 1024
            for c in range(8):
                P.act(sub(HB, (slice(None), c, slice(None))), sub(XT3, (slice(None), c, slice(t0, t0 + 1024))), AF.Identity,
                      bias=modcol(3, c), scale=sub(SCP, (slice(None), slice(8 + c, 9 + c))))
            for g in range(8):
                wt = load_w("mlp_w1", l, 0, 8, g * 512, 512)
                for j in range(4):
                    hc = g * 4 + j
                    for t2 in range(2):
                        pb = nb()
                        for kc in range(8):
                            P.mm(pb[:, :], sub(wt, (slice(None), kc, slice(j * 128, (j + 1) * 128))),
                                 sub(HB, (slice(None), kc, slice(t2 * 512, (t2 + 1) * 512))), start=(kc == 0), stop=(kc == 7))
                        tm = s32(((hc * 2 + t2) % 2) * 512, 512)
                        P.act(tm, pb[:, :], AF.Relu, bias=vec(f"b1_{l}", hc, 1))
                        P.tt("dve", sub(HID, (slice(None), hc, slice(t2 * 512, (t2 + 1) * 512))), tm, tm, ALU.mult)
                if j == 3:
                    tick()
            for c in range(8):
                wt = load_w("mlp_w2", l, 0, 32, c * 128, 128)
                for t2 in range(2):
                    pb = nb()
                    for kc in range(32):
                        P.mm(pb[:, :], sub(wt, (slice(None), kc, slice(None))),
                             sub(HID, (slice(None), kc, slice(t2 * 512, (t2 + 1) * 512))), start=(kc == 0), stop=(kc == 31))
                    tm = s32(1024 + t2 * 512, 512)
                    P.act(tm, pb[:, :], AF.Identity, bias=vec(f"b2_{l}", c, 1))
                    residual(c, hb * 2 + t2, tm, 5)
                tick()
        layer_norm(l, 1)
        st["nbm"] = 8

    def attention(KT, QT, krows, Vfn, scale, yout_fn, osb_off=0, bg_it=None, bg_every=4):
        PT = [bg(73728 + i * 512, 512) for i in range(4)]
        pending = []

        def finish(qb, po, par):
            o0 = osb_off + par * 512
            osb = s32(o0, 512, rows=65)
            P.copy("act", osb, po[0:65, :])
            P.recip(s32(o0, 512, rows=1, r0=64), s32(o0, 512, rows=1, r0=64))
            pbc = PS[6 + par]
            P.mm(pbc[0:64, :], E65, osb)
            P.tt("dve", yout_fn(qb), s32(o0, 512, rows=64), pbc[0:64, :], ALU.mult)

        for qb in range(NB):
            qs = slice(qb * 512, (qb + 1) * 512)
            st["att"] = st.get("att", 0) + 1
            par = st["att"] % 2
            po = PS[4 + par]
            sc_b = {}
            LOOK = 3
            nk = NKEY // 128

            def issue_s(kt):
                pb = PS[kt % 4]
                P.mm(pb[:, :], sub(KT, (slice(0, krows), slice(kt * 128, (kt + 1) * 128))), sub(QT, (slice(0, krows), qs)))
                sc_b[kt] = pb
            for kt in range(min(LOOK, nk)):
                issue_s(kt)
            for kt in range(nk):
                if kt + LOOK < nk:
                    issue_s(kt + LOOK)
                pt = PT[kt % 4]
                P.act(pt, sc_b.pop(kt)[:, :], AF.Exp, scale=scale)
                P.mm(po[0:65, :], Vfn(kt), pt, start=(kt == 0), stop=(kt == nk - 1))
                if kt == 9 and pending:
                    pending.pop(0)()
                if bg_it is not None and kt % bg_every == 0:
                    next(bg_it, None)
            while pending:
                pending.pop(0)()
            pending.append(lambda qb=qb, po=po, par=par: finish(qb, po, par))
        while pending:
            pending.pop(0)()

    def outproj(dname, i, l, Z3, YATT):
        bounce_in()
        for c in range(8):
            wa = load_w(dname, i, 0, 4, c * 128, 128)
            wb = load_w(dname, i, 512, 8, c * 128, 128, rows=64)
            for tb in range(NB):
                ts_ = slice(tb * 512, (tb + 1) * 512)
                pb = nb()
                for kc in range(4):
                    P.mm(pb[:, :], sub(wa, (slice(None), kc, slice(None))), sub(Z3, (slice(None), kc, ts_)), start=(kc == 0), stop=False)
                for h in range(8):
                    P.mm(pb[:, :], sub(wb, (slice(None), h, slice(None))), sub(YATT(h), (slice(None), ts_)), start=False, stop=(h == 7))
                residual(c, tb, pb[:, :], 2)
        layer_norm(l, 0)

    O_HT = 32768
    O_Z2 = 49152
    O_KR = 57344
    O_FILT = 59904
    O_X1 = 0
    O_VT = 8192
    O_VTOK = 16384
    O_CQ = 24576
    O_CKV = 28672
    O_HID2 = 31232
    HT3 = R(bg(O_HT, 16384), "p (c t) -> p c t", c=8)

    def conv3_chunk(i, j, pbs, dest3, c):
        for tb in range(NB):
            P.copy("act", sub(CT3, (slice(None), slice(2 * tb, 2 * tb + 2), slice(1, 257))), R(pbs[tb][:, :], "p (s j) -> p s j", s=2))
        P.ts("dve", sub(CT3, (slice(None), slice(1, 8), slice(0, 1))), sub(CT3, (slice(None), slice(0, 7), slice(256, 257))), vec("flag"), ALU.mult)
        P.ts("dve", sub(CT3, (slice(None), slice(0, 7), slice(257, 258))), sub(CT3, (slice(None), slice(1, 8), slice(1, 2))), vec("flag"), ALU.mult)
        for tb in range(NB):
            pv = R(pbs[tb][:, :], "p (s j) -> p s j", s=2)
            P.act(pbs[tb][:, :], pbs[tb][:, :], AF.Identity, bias=vec(f"hcb{i}", j, 1), scale=vec(f"hcw{i}", j * 3 + 1, 1))
            P.stt("dve", pv, sub(CT3, (slice(None), slice(2 * tb, 2 * tb + 2), slice(0, 256))), vec(f"hcw{i}", j * 3 + 0, 1), pv, ALU.mult, ALU.add)
            dv = R(sub(dest3, (slice(None), c, slice(tb * 512, (tb + 1) * 512))), "p (s j) -> p s j", s=2)
            P.stt("dve", dv, sub(CT3, (slice(None), slice(2 * tb, 2 * tb + 2), slice(2, 258))), vec(f"hcw{i}", j * 3 + 2, 1), pv, ALU.mult, ALU.add)

    def to_tokmajor(srcT3, dst3):
        for kt in range(NT):
            pb = nb()
            pv = View(pb, pb.t[:, 0:256].bitcast(BF16))
            for c in range(4):
                P.transpose(sub(pv, (slice(None), slice(c * 128, (c + 1) * 128))), sub(srcT3, (slice(None), c, slice(kt * 128, (kt + 1) * 128))), IDB)
            P.copy("act" if kt % 2 else "dve", sub(dst3, (slice(None), kt, slice(None))), pv)

    def rms_rstd(dst, psum_sumsq, n, rows=128):
        P.ts("dve", dst, psum_sumsq, 1.0 / n, ALU.mult, RMS_EPS, ALU.add)
        P.act(dst, dst, AF.Sqrt)
        P.recip(dst, dst)

    def even_mixer(i, l):
        P.dma("sp", View(ROPEC, ROPEC.t[:, :]), dview("rope_c", lambda a: a[0]))
        P.dma("sp", View(ROPES, ROPES.t[:, :]), dview("rope_s", lambda a: a[0]))
        modulate_full(HT3, 0, bounce=True)
        X1 = R(bg(O_X1, 8192), "p (c t) -> p c t", c=4)
        X2 = R(bg(O_Z2, 8192), "p (c t) -> p c t", c=4)
        VT = R(bg(O_VT, 8192), "p (c t) -> p c t", c=4)
        VTOK = R(bg(O_VTOK, 8192), "p (k c) -> p k c", k=16)
        CQ = R(bg(O_CQ, 4096), "p (c t) -> p c t", c=2)
        CKV = bg(O_CKV, NKEY)
        KR = bg(O_KR, NKEY, rows=32)
        dests = [X1, X2, VT]
        for g in range(3):
            wt = load_w("ev_w_in", i, 0, 8, g * 512, 512)
            for jj in range(4):
                j = g * 4 + jj
                pbs = [nb() for _ in range(NB)]
                for tb in range(NB):
                    for kc in range(8):
                        P.mm(pbs[tb][:, :], sub(wt, (slice(None), kc, slice(jj * 128, (jj + 1) * 128))),
                             sub(HT3, (slice(None), kc, slice(tb * 512, (tb + 1) * 512))), start=(kc == 0), stop=(kc == 7))
                conv3_chunk(i, j, pbs, dests[g], jj)
        if EST <= 1:
            bounce_in()
            return
        wt = load_w("ev_w_in", i, 0, 8, 1536, 448)
        sq = s32(2100, 512)
        for tb in range(NB):
            ts_ = slice(tb * 512, (tb + 1) * 512)
            pss = nb()
            pcs = []
            for c in range(2):
                pb = nb()
                pcs.append(pb)
                for kc in range(8):
                    P.mm(pb[:, :], sub(wt, (slice(None), kc, slice(c * 128, (c + 1) * 128))), sub(HT3, (slice(None), kc, ts_)), start=(kc == 0), stop=(kc == 7))
                P.act(sq, pb[:, :], AF.Square)
                P.mm(pss[:, :], ONES128, sq, start=(c == 0), stop=(c == 1))
            rst = s32(2612, 512)
            rms_rstd(rst, pss[:, :], 256)
            for c in range(2):
                P.stt("dve", sub(CQ, (slice(None), c, ts_)), pcs[c][:, :], vec(f"qg{i}", c, 1), rst, ALU.mult, ALU.mult)
            pb = nb()
            for kc in range(8):
                P.mm(pb[:, :], sub(wt, (slice(None), kc, slice(256, 384))), sub(HT3, (slice(None), kc, ts_)), start=(kc == 0), stop=(kc == 7))
            P.act(sq, pb[:, :], AF.Square)
            pss2 = nb()
            P.mm(pss2[:, :], ONES128, sq)
            rms_rstd(rst, pss2[:, :], 128)
            ckf = s32(0, 512)
            P.stt("dve", ckf, pb[:, :], vec(f"kvg{i}", 0, 1), rst, ALU.mult, ALU.mult)
            P.copy("act", sub(CKV, (slice(None), slice(512 + tb * 512, 1024 + tb * 512))), ckf)
            if "nockvout" in DBGF:
                continue
            pt = nb()
            for q in range(4):
                P.transpose(pt[:, q * 128:(q + 1) * 128], sub(ckf, (slice(None), slice(q * 128, (q + 1) * 128))), IDF)
            ost = s32(512, 512)
            P.copy("act", ost, pt[:, :])
            P.dma("sp", View(o_ckv, o_ckv.t.ap()[i, tb * 512:(tb + 1) * 512, :].rearrange("(q p) r -> p q r", p=128)),
                  R(ost, "p (q r) -> p q r", q=4))
            if "nokr" in DBGF:
                continue
            pk = nb()
            pkp = nb()
            for kc in range(8):
                P.mm(pk[0:32, :], sub(wt, (slice(None), kc, slice(384, 416))), sub(HT3, (slice(None), kc, ts_)), start=(kc == 0), stop=(kc == 7))
            for kc in range(8):
                P.mm(pkp[0:32, :], sub(wt, (slice(None), kc, slice(416, 448))), sub(HT3, (slice(None), kc, ts_)), start=(kc == 0), stop=(kc == 7))
            k1 = s32(1024, 512, rows=32)
            k2 = s32(1536, 512, rows=32)
            P.tt("dve", k1, pk[0:32, :], View(ROPEC, ROPEC.t[0:32, ts_]), ALU.mult)
            P.tt("dve", k2, pkp[0:32, :], View(ROPES, ROPES.t[0:32, ts_]), ALU.mult)
            P.tt("pool", k1, k1, k2, ALU.add)
            P.copy("act", sub(KR, (slice(None), slice(512 + tb * 512, 1024 + tb * 512))), k1)
            if "nokrout" in DBGF:
                continue
            pt2 = nb()
            for q in range(4):
                P.mm(pt2[:, q * 32:(q + 1) * 32], sub(k1, (slice(None), slice(q * 128, (q + 1) * 128))), View(CST, CST.t[0:32, 0:32]))
            ost2 = s32(2100, 128)
            P.copy("act", ost2, pt2[:, 0:128])
            P.dma("sp", View(o_kr, o_kr.t.ap()[i, tb * 512:(tb + 1) * 512, :].rearrange("(q p) r -> p q r", p=128)),
                  R(ost2, "p (q r) -> p q r", q=4))
        if EST <= 2:
            bounce_in()
            return
        for q in range(4):
            stg = s32(0, 128)
            P.dma("sp", stg, dview("c_ckv", lambda a: a[i, q * 128:(q + 1) * 128, :]))
            stg2 = s32(128, 32)
            P.dma("sp", stg2, dview("c_kr", lambda a: a[i, q * 128:(q + 1) * 128, :]))
            pb = nb()
            P.transpose(pb[:, 0:128], stg, IDF)
            P.copy("act", sub(CKV, (slice(None), slice(q * 128, (q + 1) * 128))), pb[:, 0:128])
            if "noctxkr" not in DBGF:
                pb2 = nb()
                P.mm(pb2[0:32, 0:128], stg2, IDF)
                P.copy("act", sub(KR, (slice(None), slice(q * 128, (q + 1) * 128))), pb2[0:32, 0:128])
        if EST <= 3:
            bounce_in()
            return
        if i == 0:
            dbg("x1", bg(O_X1, 8192), [128, 8192], BF16)
            dbg("x2", bg(O_Z2, 8192), [128, 8192], BF16)
            dbg("v", bg(O_VT, 8192), [128, 8192], BF16)
            dbg("cq", bg(O_CQ, 4096), [128, 4096], BF16)
        hyena(i, X1, X2, VT, VTOK)
        if i == 0:
            dbg("y1", bg(O_VT, 8192), [128, 8192], BF16)
            dbg("z2", bg(O_Z2, 8192), [128, 8192], BF16)
        if EST <= 4:
            bounce_in()
            return
        mla(i, CQ, CKV, KR)
        if i == 0:
            dbg("yatt", bg(O_HT, 16384, rows=64), [64, 16384], BF16)
            dbg("vh", bg(12288 + 1300, 1300), [128, 1300], BF16)
            dbg("kt", bg(NKEY, NKEY, rows=104), [104, NKEY], BF16)
            dbg("qt", bg(8192 + T, T, rows=104), [104, T], BF16)
        if EST <= 5:
            bounce_in()
            return
        YATT = lambda h: bg(O_HT + h * T, T, rows=64)
        outproj("ev_w_out", i, l, X2, YATT)

    def hyena(i, X1, X2, VT, VTOK):
        FILT = R(bg(O_FILT, 16384), "p (k d c) -> p k d c", k=16, d=2)
        YR = R(bg(O_HT, 8192), "p (f c) -> p f c", f=16)
        YI = R(bg(O_HT + 8192, 8192), "p (f c) -> p f c", f=16)

        h1 = bgf(O_HT, 2048, rows=64)
        h2 = bgf(O_HT + 4096, 2048, rows=64)
        tmpf = bgf(O_HT + 8192, 2048, rows=64)
        zT = bgf(O_HT + 12288, 2048, rows=33)
        P.dma("sp", zT, dview("zT"))
        w1 = s32(2100, 64, rows=33)
        P.dma("sp", w1, dview("f_w1", lambda a: a[i]))
        w2 = s32(2164, 64, rows=64)
        P.dma("sp", w2, dview("f_w2", lambda a: a[i]))
        P.tt("dve", sub(FB, (slice(None), slice(0, 1))), vec(f"fb1_{i}", 0, 1, rows=64), vec(f"ffreq{i}", 0, 1, rows=64), ALU.mult)
        P.tt("dve", sub(FB, (slice(None), slice(1, 2))), vec(f"fb2_{i}", 0, 1, rows=64), vec(f"ffreq{i}", 0, 1, rows=64), ALU.mult)

        def sin_layer(dst, wv, src, krows, bcol):
            for tb in range(NB):
                ts_ = slice(tb * 512, (tb + 1) * 512)
                pb = nb()
                P.mm(pb[0:64, :], wv, sub(src, (slice(0, krows), ts_)))
                a = sub(dst, (slice(None), ts_))
                tq = sub(tmpf, (slice(None), ts_))
                P.act(a, pb[0:64, :], AF.Identity, bias=sub(FB, (slice(None), slice(bcol, bcol + 1))), scale=vec(f"ffreq{i}", 0, 1, rows=64))
                MG = 12582912.0
                P.ts("dve", tq, a, 1.0 / (2 * math.pi), ALU.mult, MG, ALU.add)
                P.ts("dve", tq, tq, MG, ALU.subtract, -2 * math.pi, ALU.mult)
                P.tt("dve", a, a, tq, ALU.add)
                P.ts("dve", a, a, 3.141592, ALU.min, -3.141592, ALU.max)
                P.act(a, a, AF.Sin)
        sin_layer(h1, w1, zT, 33, 0)
        sin_layer(h2, w2, h1, 64, 1)
        HB2 = View(SMALL, SMALL.t[0:64, 0:1])
        H2B = View(BIG, BIG.t[64:128, O_KR:O_KR + T])
        P.copy("dve", H2B, h2)
        to_tokmajor(VT, VTOK)
        for n in range(2):
            k = st["wr"] % 3
            st["wr"] += 1
            W3 = View(WR, WR.t[64:128, k, 0:1024])
            P.dma("pool", sub(W3, (slice(None), slice(0, 512))), dview("f_w3", lambda a: a[i, :, n * 512:(n + 1) * 512]))
            P.dma("pool", sub(W3, (slice(None), slice(512, 1024))), dview("f_w3", lambda a: a[i, :, 1024 + n * 512:1024 + (n + 1) * 512]))
            AD = s32(2100, 1024)
            P.dma("sp", sub(AD, (slice(None), slice(0, 512))), dview("decay_bc", lambda a: a[i, :, n * 512:(n + 1) * 512]))
            P.dma("sp", sub(AD, (slice(None), slice(512, 1024))), dview("decay_bc", lambda a: a[i, :, 1024 + n * 512:1024 + (n + 1) * 512]))
            P.act(AD, AD, AF.Abs)
            for kt in range(NT):
                pf = nb()
                pbk = nb()
                P.mm(pf[:, :], sub(H2B, (slice(None), slice(kt * 128, (kt + 1) * 128))), sub(W3, (slice(None), slice(0, 512))))
                P.mm(pbk[:, :], sub(H2B, (slice(None), slice(kt * 128, (kt + 1) * 128))), sub(W3, (slice(None), slice(512, 1024))))
                o_ = (kt % 2) * 1024
                w1 = s32(o_, 512)
                w2 = s32(o_ + 512, 512)
                P.act(w1, sub(AD, (slice(None), slice(0, 512))), AF.Exp, scale=vec("negt", kt, 1))
                P.act(w2, sub(AD, (slice(None), slice(512, 1024))), AF.Exp, scale=vec("negt", kt, 1))
                P.tt("dve", w1, pf[:, :], w1, ALU.mult)
                P.stt("dve", w2, pbk[:, :], vec("m0", kt, 1), w2, ALU.mult, ALU.mult)
                P.tt("pool", sub(FILT, (slice(None), kt, 0, slice(None))), w1, w2, ALU.add)
                P.tt("pool", sub(FILT, (slice(None), kt, 1, slice(None))), w2, w1, ALU.subtract)
            SRC = VTOK
            if n == 0 and i == 0:
                dbg("filt", bg(O_FILT, 16384), [128, 16384], BF16)
                dbg("h2b", View(BIG, BIG.t[64:128, O_KR:O_KR + T]), [64, T], BF16)
                dbg("vtok", bg(O_VTOK, 8192), [128, 8192], BF16)
            pn1 = nb()
            pn2 = nb()
            for kt in range(NT):
                P.mm(pn1[0:8, :], sub(FNT3, (slice(None), kt, slice(None))), sub(SRC, (slice(None), kt, slice(None))), start=(kt == 0), stop=(kt == NT - 1))
            for kt in range(NT):
                P.mm(pn2[0:8, :], sub(FNT3, (slice(None), kt, slice(None))), sub(FILT, (slice(None), kt, 0, slice(None))), start=(kt == 0), stop=(kt == NT - 1))
            tn = s32(3072, 512, rows=8)
            P.act(tn, pn2[0:8, :], AF.Identity, scale=vec("wN", 0, 1, rows=8))
            P.tt("dve", YN, pn1[0:8, :], tn, ALU.mult)
            for fc in range(16):
                k = st["wr"] % 3
                st["wr"] += 1
                CT = View(WR, WR.t[:, k, 0:2048].rearrange("p (k j) -> p k j", k=16))
                ST = View(WR, WR.t[:, k, 2048:4096].rearrange("p (k j) -> p k j", k=16))
                P.dma("sp", CT, dview("cft", lambda a: a[fc].rearrange("p (k j) -> p k j", k=16)))
                P.dma("sp", ST, dview("sft", lambda a: a[fc].rearrange("p (k j) -> p k j", k=16)))
                pa, pp, pbb, pq = nb(), nb(), nb(), nb()
                for kt in range(NT):
                    fl = (kt == 0)
                    ll = (kt == NT - 1)
                    P.mm(pa[:, :], sub(CT, (slice(None), kt, slice(None))), sub(SRC, (slice(None), kt, slice(None))), start=fl, stop=ll)
                    P.mm(pp[:, :], sub(CT, (slice(None), kt, slice(None))), sub(FILT, (slice(None), kt, 0, slice(None))), start=fl, stop=ll)
                    P.mm(pbb[:, :], sub(ST, (slice(None), kt, slice(None))), sub(SRC, (slice(None), kt, slice(None))), start=fl, stop=ll)
                    P.mm(pq[:, :], sub(ST, (slice(None), kt, slice(None))), sub(FILT, (slice(None), kt, 1, slice(None))), start=fl, stop=ll)
                p1 = s32(0, 512)
                q1 = s32(512, 512)
                P.act(p1, pp[:, :], AF.Identity, scale=vec("wf", fc, 1))
                P.act(q1, pq[:, :], AF.Identity, scale=vec("wf", fc, 1))
                t1, t2, t3, t4 = s32(1024, 512), s32(1536, 512), s32(2100, 512), s32(2612, 512)
                P.tt("dve", t1, pa[:, :], p1, ALU.mult)
                P.tt("dve", t2, pbb[:, :], q1, ALU.mult)
                P.tt("pool", sub(YR, (slice(None), fc, slice(None))), t1, t2, ALU.add)
                P.tt("dve", t3, pbb[:, :], p1, ALU.mult)
                P.tt("dve", t4, pa[:, :], q1, ALU.mult)
                P.tt("pool", sub(YI, (slice(None), fc, slice(None))), t3, t4, ALU.subtract)
            if n == 0 and i == 0:
                dbg("yr", bg(O_HT, 8192), [128, 8192], BF16)
                dbg("yi", bg(O_HT + 8192, 8192), [128, 8192], BF16)
            for tb in range(NB):
                ts_ = slice(tb * 512, (tb + 1) * 512)
                P.dma("sp", GN, dview("gn", lambda a: a[:, ts_]))
                acc = [nb() for _ in range(4)]
                for fq in range(4):
                    k = st["wr"] % 3
                    st["wr"] += 1
                    GC4 = View(WR, WR.t[:, k, 0:2048].rearrange("p (k t) -> p k t", k=4))
                    GS4 = View(WR, WR.t[:, k, 2048:4096].rearrange("p (k t) -> p k t", k=4))
                    P.dma("sp", GC4, dview("cf", lambda a: a[fq * 512:(fq + 1) * 512, ts_].rearrange("(k p) t -> p k t", p=128)))
                    P.dma("sp", GS4, dview("sf", lambda a: a[fq * 512:(fq + 1) * 512, ts_].rearrange("(k p) t -> p k t", p=128)))
                    for f4 in range(4):
                        fc = fq * 4 + f4
                        for c in range(4):
                            cs = slice(c * 128, (c + 1) * 128)
                            P.mm(acc[c][:, :], sub(YR, (slice(None), fc, cs)), sub(GC4, (slice(None), f4, slice(None))), start=(fc == 0), stop=False)
                            P.mm(acc[c][:, :], sub(YI, (slice(None), fc, cs)), sub(GS4, (slice(None), f4, slice(None))), start=False, stop=False)
                for c in range(4):
                    cs = slice(c * 128, (c + 1) * 128)
                    P.mm(acc[c][:, :], sub(YN, (slice(None), cs)), GN, start=False, stop=True)
                    tm = s32(3072, 512)
                    if n == 0:
                        vv = sub(VT, (slice(None), c, ts_))
                        P.stt("dve", tm, vv, vec(f"skip{i}_0", c, 1), acc[c][:, :], ALU.mult, ALU.add)
                        P.tt("pool", vv, tm, sub(X1, (slice(None), c, ts_)), ALU.mult)
                    else:
                        vv = sub(VT, (slice(None), c, ts_))
                        P.stt("dve", tm, vv, vec(f"skip{i}_1", c, 1), acc[c][:, :], ALU.mult, ALU.add)
                        xv = sub(X2, (slice(None), c, ts_))
                        P.tt("pool", xv, tm, xv, ALU.mult)
            if n == 0:
                to_tokmajor(VT, VTOK)

    def mla(i, CQ, CKV, KR):
        KTb = [bg(0 + b * NKEY, NKEY) for b in range(2)]
        QTb = [bg(8192 + b * T, T) for b in range(2)]
        VHb = [R(bg(12288 + b * 1300, 1300), "p (k e) -> p k e", k=20) for b in range(2)]
        for b in range(2):
            P.dma("sp", View(BIG, BIG.t[96:104, b * NKEY:(b + 1) * NKEY]), dview("maskB"))
            P.dma("sp", View(BIG, BIG.t[96:104, 8192 + b * T:8192 + (b + 1) * T]), dview("maskA"))
            P.memset("dve", bg(12288 + b * 1300, 1300), 1.0)
        k = st["wr"] % 3
        st["wr"] += 1
        WKV = View(WR, WR.t[:, k, 0:1024])
        P.dma("pool", WKV, dview("w_ukv", lambda a: a[i]))
        WUQ = View(WR, WR.t[:, k, 1024:4096].rearrange("p (k n) -> p k n", k=2))
        P.dma("pool", WUQ, dview("w_uq", lambda a: a[i].rearrange("(k p) n -> p k n", p=128)))
        sc = 96.0 ** -0.5
        for h in range(8):
            b = h % 2
            KT, QT, VH = KTb[b], QTb[b], VHb[b]
            for kb in range(5):
                pb = nb()
                P.mm(pb[0:64, :], sub(WKV, (slice(None), slice(h * 128, h * 128 + 64))), sub(CKV, (slice(None), slice(kb * 512, (kb + 1) * 512))))
                P.copy("dve", sub(KT, (slice(0, 64), slice(kb * 512, (kb + 1) * 512))), pb[0:64, :])
            P.copy("dve", sub(KT, (slice(64, 96), slice(None))), KR)
            for k0, kn in ((0, 8), (8, 8), (16, 4)):
                pb = nb()
                for jj in range(kn):
                    kt = k0 + jj
                    P.mm(pb[:, jj * 64:(jj + 1) * 64], sub(CKV, (slice(None), slice(kt * 128, (kt + 1) * 128))), sub(WKV, (slice(None), slice(h * 128 + 64, h * 128 + 128))))
                P.copy("act", sub(VH, (slice(None), slice(k0, k0 + kn), slice(0, 64))), R(pb[:, 0:kn * 64], "p (k e) -> p k e", k=kn))
            for tb in range(NB):
                ts_ = slice(tb * 512, (tb + 1) * 512)
                pq = nb()
                pp = nb()
                for kc in range(2):
                    P.mm(pq[0:96, :], sub(WUQ, (slice(None), kc, slice(h * 192, h * 192 + 96))), sub(CQ, (slice(None), kc, ts_)), start=(kc == 0), stop=(kc == 1))
                for kc in range(2):
                    P.mm(pp[0:96, :], sub(WUQ, (slice(None), kc, slice(h * 192 + 96, h * 192 + 192))), sub(CQ, (slice(None), kc, ts_)), start=(kc == 0), stop=(kc == 1))
                t1 = s32(1024 + (tb % 2) * 1024, 512, rows=32, r0=64)
                t2 = s32(1536 + (tb % 2) * 1024, 512, rows=32, r0=64)
                P.copy("dve", sub(QT, (slice(0, 64), ts_)), pq[0:64, :])
                P.tt("dve", t1, pq[64:96, :], View(ROPEC, ROPEC.t[64:96, ts_]), ALU.mult)
                P.tt("dve", t2, pp[64:96, :], View(ROPES, ROPES.t[64:96, ts_]), ALU.mult)
                P.tt("pool", sub(QT, (slice(64, 96), ts_)), t1, t2, ALU.add)
            attention(KT, QT, 104, lambda kt, VH=VH: sub(VH, (slice(None), kt, slice(None))), sc,
                      lambda qb, h=h: bg(O_HT + h * T + qb * 512, 512, rows=64))

    def odd_mixer(i, l):
        P.dma("sp", View(ROPEC, ROPEC.t[:, :]), dview("rope_c", lambda a: a[1]))
        P.dma("sp", View(ROPES, ROPES.t[:, :]), dview("rope_s", lambda a: a[1]))
        modulate_full(HT3, 0, bounce=True)
        YCVF = R(bgf(0, 4 * T), "p (c t) -> p c t", c=4)
        YCV = R(bg(O_Z2, 8192), "p (c t) -> p c t", c=4)
        KTs = [bg(16384 + b * NKEY, NKEY) for b in range(2)]
        VV = R(bg(21504, 2600), "p (k g e) -> p k g e", k=20, g=2)
        QTs = [bg(57344 + h * T, T) for h in range(8)]
        for b in range(2):
            P.dma("sp", View(BIG, BIG.t[64:72, 16384 + b * NKEY:16384 + (b + 1) * NKEY]), dview("maskB"))
        for h in range(8):
            P.dma("sp", View(BIG, BIG.t[64:72, 57344 + h * T:57344 + (h + 1) * T]), dview("maskA"))
        P.memset("dve", bg(21504, 2600), 1.0)
        G3 = R(s32(0, 8 * 286), "p (s j) -> p s j", s=8)
        P.memset("pool", s32(0, 8 * 286), 0.0)
        GBF = R(bg(24576, 8192), "p (c t) -> p c t", c=4)
        for c in range(4):
            wa = load_w("od_w_in", i, 0, 8, c * 128, 128)
            wg = load_w("od_w_in", i, 0, 8, 512 + c * 128, 128)
            for tb in range(NB):
                ts_ = slice(tb * 512, (tb + 1) * 512)
                pa = nb()
                pg = nb()
                for kc in range(8):
                    P.mm(pa[:, :], sub(wa, (slice(None), kc, slice(None))), sub(HT3, (slice(None), kc, ts_)), start=(kc == 0), stop=(kc == 7))
                for kc in range(8):
                    P.mm(pg[:, :], sub(wg, (slice(None), kc, slice(None))), sub(HT3, (slice(None), kc, ts_)), start=(kc == 0), stop=(kc == 7))
                sg = s32(2400 + (tb % 2) * 512, 512)
                P.act(sg, pg[:, :], AF.Sigmoid)
                P.tt("dve", sub(GBF, (slice(None), c, ts_)), pa[:, :], sg, ALU.mult)

        def conv_chunk(c):
            P.memset("pool", sub(G3, (slice(None), slice(0, 1), slice(0, 15))), 0.0)
            P.memset("pool", sub(G3, (slice(None), slice(7, 8), slice(271, 286))), 0.0)
            P.copy("pool", sub(G3, (slice(None), slice(None), slice(15, 271))), R(sub(GBF, (slice(None), c, slice(None))), "p (s j) -> p s j", s=8))
            P.ts("dve", sub(G3, (slice(None), slice(1, 8), slice(0, 15))), sub(G3, (slice(None), slice(0, 7), slice(256, 271))), vec("flag"), ALU.mult)
            P.ts("dve", sub(G3, (slice(None), slice(0, 7), slice(271, 286))), sub(G3, (slice(None), slice(1, 8), slice(15, 30))), vec("flag"), ALU.mult)
            av = R(sub(YCVF, (slice(None), c, slice(None))), "p (s j) -> p s j", s=8)
            P.act(av, sub(G3, (slice(None), slice(None), slice(0, 256))), AF.Identity,
                  bias=vec(f"dwb{i}", c, 1), scale=vec(f"dww{i}", c * 31, 1))
            yield
            for k in range(1, 31):
                P.stt("dve", av, sub(G3, (slice(None), slice(None), slice(k, k + 256))), vec(f"dww{i}", c * 31 + k, 1), av, ALU.mult, ALU.add)
                yield
        if OST <= 1:
            bounce_in()
            return
        BONES = View(CST, CST.t[:, 384:512])

        def normed_pair(dst0, dst1, wcol, gname, gpname, tb, wt_main, wt_part):
            ts_ = slice(tb * 512, (tb + 1) * 512)
            pq = nb()
            pp = nb()
            for kc in range(8):
                P.mm(pq[:, :], sub(wt_main, (slice(None), kc, wcol)), sub(HT3, (slice(None), kc, ts_)), start=(kc == 0), stop=(kc == 7))
            for kc in range(8):
                P.mm(pp[:, :], sub(wt_part, (slice(None), kc, wcol)), sub(HT3, (slice(None), kc, ts_)), start=(kc == 0), stop=(kc == 7))
            st["nh"] = st.get("nh", 0) + 1
            o_ = (st["nh"] % 2) * 1536
            sq = s32(o_ + 512, 512)
            P.act(sq, pq[:, :], AF.Square)
            pss = nb()
            P.mm(pss[:, :], BONES, sq)
            rst = s32(o_ + 1024, 512)
            rms_rstd(rst, pss[:, :], 64)
            t1 = s32(o_, 512)
            t2 = s32(o_ + 512, 512)
            P.stt("dve", t1, pq[:, :], vec(gname, 0, 1), rst, ALU.mult, ALU.mult)
            P.stt("dve", t2, pp[:, :], vec(gpname, 0, 1), rst, ALU.mult, ALU.mult)
            P.tt("pool", t1, t1, View(ROPEC, ROPEC.t[:, ts_]), ALU.mult)
            P.tt("pool", t2, t2, View(ROPES, ROPES.t[:, ts_]), ALU.mult)
            P.tt("dve", t1, t1, t2, ALU.add)
            P.copy("act", dst0, sub(t1, (slice(0, 64), slice(None))))
            P.copy("dve", dst1, sub(t1, (slice(64, 128), slice(None))))
            return t1
        for g in range(0 if "noq" in DBGF else 2):
            wq = load_w("od_w_in", i, 0, 8, 1024 + g * 256, 256)
            wqp = load_w("od_w_in", i, 0, 8, 1280 + 512 + g * 256, 256)
            for hp in range(2):
                h = g * 4 + hp * 2
                for tb in range(NB):
                    tsl = slice(tb * 512, (tb + 1) * 512)
                    normed_pair(sub(QTs[h], (slice(0, 64), tsl)), sub(QTs[h + 1], (slice(0, 64), tsl)), slice(hp * 128, hp * 128 + 128),
                                f"gq2{i}", f"gqp2{i}", tb, wq, wqp)
        wk = load_w("od_w_in", i, 0, 8, 1536, 256)
        wkp = load_w("od_w_in", i, 0, 8, 2304, 128)
        for tb in range(0 if "nok" in DBGF else NB):
            ksl = slice(512 + tb * 512, 1024 + tb * 512)
            t1 = normed_pair(sub(KTs[0], (slice(0, 64), ksl)), sub(KTs[1], (slice(0, 64), ksl)), slice(0, 128),
                             f"gk2{i}", f"gkp2{i}", tb, wk, wkp)
            pt = nb()
            for q in range(4):
                P.transpose(pt[:, q * 128:(q + 1) * 128], sub(t1, (slice(None), slice(q * 128, (q + 1) * 128))), IDF)
            ost = s32(3072, 512)
            P.copy("act", ost, pt[:, :])
            P.dma("sp", View(o_gk, o_gk.t.ap()[i, tb * 512:(tb + 1) * 512, :].rearrange("(q p) r -> p q r", p=128)),
                  R(ost, "p (q r) -> p q r", q=4))
        for k0 in range(0, 0 if "nov" in DBGF else NT, 4):
            pb = nb()
            for jj in range(4):
                kt = k0 + jj
                for kc in range(8):
                    P.mm(pb[:, jj * 128:(jj + 1) * 128], sub(HT3, (slice(None), kc, slice(kt * 128, (kt + 1) * 128))), sub(wk, (slice(None), kc, slice(128, 256))), start=(kc == 0), stop=(kc == 7))
            ost = s32(3072, 512)
            P.copy("act", ost, pb[:, :])
            P.copy("act", sub(VV, (slice(None), slice(4 + k0, 8 + k0), slice(None), slice(0, 64))), R(ost, "p (k g e) -> p k g e", k=4, g=2))
            P.dma("sp", View(o_gv, o_gv.t.ap()[i, k0 * 128:(k0 + 4) * 128, :].rearrange("(k p) r -> p k r", p=128)), R(ost, "p (k r) -> p k r", k=4))
        for q in range(0 if "noctx" in DBGF else 4):
            stg = s32(0, 128)
            P.dma("sp", stg, dview("c_gk", lambda a: a[i, q * 128:(q + 1) * 128, :]))
            stg2 = s32(128, 128)
            P.dma("sp", stg2, dview("c_gv", lambda a: a[i, q * 128:(q + 1) * 128, :]))
            for kv in range(0 if "noctxk" in DBGF else 2):
                pb = nb()
                P.mm(pb[0:64, 0:128], sub(stg, (slice(None), slice(kv * 64, kv * 64 + 64))), IDF)
                P.copy("act", sub(KTs[kv], (slice(0, 64), slice(q * 128, (q + 1) * 128))), pb[0:64, 0:128])
            if "noctxv" not in DBGF:
                P.copy("act", sub(VV, (slice(None), q, slice(None), slice(0, 64))), R(stg2, "p (g e) -> p g e", g=2))
        if OST <= 2:
            bounce_in()
            return
        cv_it = iter(())
        for h in range(8):
            kv = h // 4
            if h % 2 == 0:
                for _ in cv_it:
                    pass
                cv_it = conv_chunk(h // 2)
            attention(KTs[kv], QTs[h], 72, lambda kt, kv=kv: sub(VV, (slice(None), kt, kv, slice(None))), 0.125,
                      lambda qb, h=h: bg(O_HT + h * T + qb * 512, 512, rows=64), osb_off=2400, bg_it=cv_it, bg_every=4)
        for _ in cv_it:
            pass
        rb = R(bg(24576, 2048), "p (c t) -> p c t", c=4)
        rb2 = R(bg(24576 + 2048, 2048), "p (c t) -> p c t", c=4)
        ONES512 = View(CSTB, CSTB.t[:, 256:384])
        mean_sb = s32(0, 512)
        m2 = s32(512, 512)
        rstd = s32(1024, 512)
        for tb in range(NB):
            ts_ = slice(tb * 512, (tb + 1) * 512)
            for c in range(4):
                P.act(sub(rb, (slice(None), c, slice(None))), sub(YCVF, (slice(None), c, ts_)), AF.Copy)
                P.act(sub(rb2, (slice(None), c, slice(None))), sub(YCVF, (slice(None), c, ts_)), AF.Square)
            pm = nb()
            pq = nb()
            for c in range(4):
                P.mm(pm[:, :], ONES512, sub(rb, (slice(None), c, slice(None))), start=(c == 0), stop=(c == 3))
            for c in range(4):
                P.mm(pq[:, :], ONES512, sub(rb2, (slice(None), c, slice(None))), start=(c == 0), stop=(c == 3))
            P.copy("act", mean_sb, pm[:, :])
            P.tt("pool", m2, mean_sb, mean_sb, ALU.mult)
            P.ts("pool", m2, m2, -LN_EPS, ALU.add)
            P.tt("dve", rstd, pq[:, :], m2, ALU.subtract)
            P.act(rstd, rstd, AF.Sqrt)
            P.recip(rstd, rstd)
            for c in range(4):
                xv = sub(YCVF, (slice(None), c, ts_))
                P.tt("dve", xv, xv, mean_sb, ALU.subtract)
                P.tt("dve", xv, xv, rstd, ALU.mult)
                P.act(sub(YCV, (slice(None), c, ts_)), xv, AF.Silu, bias=vec(f"cvb{i}", c, 1), scale=vec(f"cvg{i}", c, 1))
        if OST <= 3:
            bounce_in()
            return
        if OST <= 4:
            bounce_in()
            return
        YATT = lambda h: bg(O_HT + h * T, T, rows=64)
        outproj("od_w_out", i, l, YCV, YATT)

    P.memset("dve", View(CSTB, CSTB.t[:, 256:384]), 1.0 / 512)
    ada_it = adaln_tiles(0)
    for _ in ada_it:
        pass
    for l in range(4):
        if (l, 1) <= tuple(STOP):
            adaln_apply()
            if "nomixer" in DBGF:
                pass
            elif l % 2 == 0:
                P.memset("pool", CTMP, 0.0)
                even_mixer(l // 2, l)
            else:
                odd_mixer(l // 2, l)
        if (l, 2) <= tuple(STOP):
            ada_it = adaln_tiles(l + 1) if (l < 3 and "noada" not in DBGF) else iter(())
            mlp(l, ada_it)
            for _ in ada_it:
                pass
    for tt in range(0 if 'noout' in DBGF else NT):
        stg = s32(((tt % 2) * 1024), 1024)
        for half in range(2):
            pb = nb()
            for j in range(4):
                c = half * 4 + j
                P.transpose(pb[:, j * 128:(j + 1) * 128], sub(XT3, (slice(None), c, slice(tt * 128, (tt + 1) * 128))), IDF)
            P.copy("dve" if half else "act", sub(stg, (slice(None), slice(half * 512, half * 512 + 512))), pb[:, :])
        P.dma("sp", View(y_out, y_out.t.ap()[tt * 128:(tt + 1) * 128, :]), stg)
    P.emit()
    es.close()
    return nc

_CACHE = {}


def _prep_shared(inp):
    g = lambda k: np.asarray(inp[k], np.float32)
    W = {}
    ev = g("ev_w_in")
    p32 = perm2d(16)
    krp = ev[:, :, 1920:1952][:, :, p32]
    W["ev_w_in"] = np.ascontiguousarray(np.concatenate([ev, krp], axis=2))
    uq = g("mla_w_uq").reshape(2, 256, 8, 96)
    part = np.zeros((2, 256, 8, 96), np.float32)
    part[:, :, :, 64:96] = uq[:, :, :, 64:96][:, :, :, p32]
    W["w_uq"] = np.ascontiguousarray(np.concatenate([uq, part], axis=3).reshape(2, 256, 8 * 192))
    W["w_ukv"] = g("mla_w_ukv")
    W["ev_w_out"] = g("ev_w_out")
    od = g("od_w_in")
    p64 = perm2d(32)
    q = od[:, :, 1024:1536].reshape(2, 1024, 8, 64)
    k = od[:, :, 1536:1664].reshape(2, 1024, 2, 64)
    qp = q[:, :, :, p64].reshape(2, 1024, 512)
    kp = k[:, :, :, p64].reshape(2, 1024, 128)
    W["od_w_in"] = np.ascontiguousarray(np.concatenate([od, qp, kp], axis=2))
    W["od_w_out"] = g("od_w_out")
    W["ada_w"] = g("ada_w")
    W["mlp_w1"] = g("mlp_w1")
    W["mlp_w2"] = g("mlp_w2")
    W["f_w1"] = g("hy_filt_w1")
    W["f_w2"] = g("hy_filt_w2")
    W["f_w3"] = g("hy_filt_w3")
    W["decay_bc"] = np.ascontiguousarray(np.broadcast_to(g("hy_decay")[:, None, :], (2, 128, 2048)))
    W["ident"] = np.eye(128, dtype=np.float32)
    return W


def _vecs(inp, cond, cvec):
    g = lambda k: np.asarray(inp[k], np.float32)
    vp = VecPack()
    vp.add("cond", chunked(cond))
    for k_, v_ in cvec.items():
        vp.add(k_, v_)
    for l in range(4):
        vp.add(f"ada_b{l}", chunked(g("ada_b")[l]))
        for w in range(2):
            vp.add(f"ln_g{l}_{w}", chunked(g("ln_g")[l, w]))
            vp.add(f"ln_b{l}_{w}", chunked(g("ln_b")[l, w]))
        vp.add(f"b1_{l}", chunked(g("mlp_b1")[l]))
        vp.add(f"b2_{l}", chunked(g("mlp_b2")[l]))
    p64 = perm2d(32)
    for i in range(2):
        cw = g("hy_conv_w")[i]
        hcw = cw.T.reshape(12, 128, 3).transpose(1, 0, 2).reshape(128, 36)
        vp.add(f"hcw{i}", hcw)
        vp.add(f"hcb{i}", chunked(g("hy_conv_b")[i]))
        vp.add(f"skip{i}_0", chunked(g("hy_skip")[i, 0]))
        vp.add(f"skip{i}_1", chunked(g("hy_skip")[i, 1]))
        vp.add(f"qg{i}", chunked(g("mla_q_norm_g")[i]))
        vp.add(f"kvg{i}", chunked(g("mla_kv_norm_g")[i]))
        vp.add(f"fb1_{i}", g("hy_filt_b1")[i][:, None])
        vp.add(f"fb2_{i}", g("hy_filt_b2")[i][:, None])
        vp.add(f"ffreq{i}", g("hy_sin_freq")[i][:, None])
        dw = g("cv_dw_w")[i]
        vp.add(f"dww{i}", dw.T.reshape(4, 128, 31).transpose(1, 0, 2).reshape(128, 124))
        vp.add(f"dwb{i}", chunked(g("cv_dw_b")[i]))
        vp.add(f"cvg{i}", chunked(g("cv_ln_g")[i]))
        vp.add(f"cvb{i}", chunked(g("cv_ln_b")[i]))
        gq = g("gqa_q_norm_g")[i]
        gk = g("gqa_k_norm_g")[i]
        vp.add(f"gq2{i}", np.concatenate([gq, gq])[:, None])
        vp.add(f"gqp2{i}", np.concatenate([gq[p64], gq[p64]])[:, None])
        vp.add(f"gk2{i}", np.concatenate([gk, gk])[:, None])
        vp.add(f"gkp2{i}", np.concatenate([gk[p64], gk[p64]])[:, None])
        vp.add(f"gq{i}", gq[:, None])
        vp.add(f"gqp{i}", gq[p64][:, None])
        vp.add(f"gk{i}", gk[:, None])
        vp.add(f"gkp{i}", gk[p64][:, None])
    return vp


def kernel(**inp):
    W = _prep_shared(inp)
    consts = {True: make_consts(True), False: make_consts(False)}
    xp = np.asarray(inp["x_prompt"], np.float32)
    xs = np.asarray(inp["x_sample"], np.float32)
    in_maps = []
    voff = None
    NV = None
    for core in range(8):
        sample = core in (4, 5)
        c, cvec = consts[sample]
        m = dict(W)
        for k_ in ("cft", "sft", "cf", "sf", "fnt", "gn", "zT", "maskA", "maskB", "rope_c", "rope_s"):
            m[k_] = c[k_]
        if sample:
            b = core - 4
            m["x_in"] = np.ascontiguousarray(xs[b])
            cond = np.asarray(inp["c"], np.float32)[b]
            m["c_ckv"] = np.ascontiguousarray(np.asarray(inp["cache_mla_ckv"], np.float32)[b])
            m["c_kr"] = np.ascontiguousarray(np.asarray(inp["cache_mla_krope"], np.float32)[b])
            m["c_gk"] = np.ascontiguousarray(np.asarray(inp["cache_gqa_k"], np.float32)[b].reshape(2, 512, 128))
            m["c_gv"] = np.ascontiguousarray(np.asarray(inp["cache_gqa_v"], np.float32)[b].reshape(2, 512, 128))
        else:
            pc = core if core < 4 else core - 6
            m["x_in"] = np.ascontiguousarray(xp[pc * 8:(pc + 1) * 8].reshape(T, DM))
            cond = np.asarray(inp["c_ctx"], np.float32)
            m["c_ckv"] = np.zeros((2, 512, 128), np.float32)
            m["c_kr"] = np.zeros((2, 512, 32), np.float32)
            m["c_gk"] = np.zeros((2, 512, 128), np.float32)
            m["c_gv"] = np.zeros((2, 512, 128), np.float32)
        vp = _vecs(inp, cond, cvec)
        m["vecs"] = vp.array()
        voff, NV = vp.off, vp.n
        in_maps.append(m)
    key = NV
    if key not in _CACHE:
        _CACHE[key] = build(voff, NV)
    nc = _CACHE[key]
    res = run_bass_kernel_spmd(nc, in_maps, core_ids=list(range(8)))
    r = res.results
    y_prompt = np.concatenate([r[c]["y_out"].reshape(8, 256, DM) for c in range(4)], axis=0)
    y_sample = np.stack([r[4]["y_out"], r[5]["y_out"]], axis=0)

    def st(name, last):
        outs = []
        for c in range(4):
            a = r[c][name]
            a = a.reshape(2, 8, 256, -1).transpose(1, 0, 2, 3)
            outs.append(a)
        a = np.concatenate(outs, axis=0)
        return np.ascontiguousarray(a.reshape((32, 2, 256) + last)).astype(np.float32)
    return (y_prompt.astype(np.float32), y_sample.astype(np.float32), st("o_ckv", (128,)), st("o_kr", (32,)),
            st("o_gk", (2, 64)), st("o_gv", (2, 64)))
```
